# Optimizing a Trainium2 kernel written in Bass

```python
import math
import jax, jax.numpy as jnp
from jax import lax
import numpy as np

D_MODEL = 1024
BATCH = 4
SEQ = 4096
DEPTH = 2

S5_GROUP = 16
S5_GROUPS = D_MODEL // S5_GROUP
S5_STATE = 64
N_HEADS = 16
KV_HEADS = 4
HEAD_DIM = 64
CMP_BLOCK = 32
CMP_STRIDE = 16
CMP_HIDDEN = 2 * HEAD_DIM
SEL_BLOCK = 64
SEL_TOPN = 16
WINDOW = 512
Q_BLOCK = 128
ROPE_THETA = 500000.0
ROPE_DIM = HEAD_DIM // 4
D_FF = 2816
CONV_WIDTH = 3
EPS = 1e-6
NEG = -1e30
SEL_FORCE = 1e4
SEL_NEG = -1e4

kernel_name = "yoco_s5_nsa_convffn_trunk"


def rmsnorm(x, g):
    xf = x.astype(jnp.float32)
    y = xf * lax.rsqrt(jnp.mean(xf * xf, axis=-1, keepdims=True) + EPS)
    return (y * g.astype(jnp.float32)).astype(x.dtype)


def rope_partial(x, pos):
    half = ROPE_DIM // 2
    inv = ROPE_THETA ** (-jnp.arange(half, dtype=jnp.float32) / half)
    ang = pos.astype(jnp.float32)[:, None] * inv[None, :]
    cos = jnp.cos(ang)[:, None, :]
    sin = jnp.sin(ang)[:, None, :]
    xr = x[..., :ROPE_DIM].astype(jnp.float32)
    x1, x2 = xr[..., :half], xr[..., half:]
    rot = jnp.concatenate([x1 * cos - x2 * sin, x2 * cos + x1 * sin], axis=-1).astype(x.dtype)
    return jnp.concatenate([rot, x[..., ROPE_DIM:]], axis=-1)


def s5_mixer(u, lam_re, lam_im, log_dt, b_re, b_im, c_re, c_im, d_skip, w_glu):
    f32 = jnp.float32
    Bsz, T, _ = u.shape
    G, P, I = S5_GROUPS, S5_STATE, S5_GROUP
    uf = u.astype(f32).reshape(Bsz, T, G, I)
    dt = jnp.exp(log_dt.astype(f32))[:, None]
    lr, li = lam_re.astype(f32), lam_im.astype(f32)
    mag = jnp.exp(lr * dt)
    ab_re, ab_im = mag * jnp.cos(li * dt), mag * jnp.sin(li * dt)
    nr, ni = ab_re - 1.0, ab_im
    den = lr * lr + li * li
    coef_re = (nr * lr + ni * li) / den
    coef_im = (ni * lr - nr * li) / den
    br, bi = b_re.astype(f32), b_im.astype(f32)
    bb_re = coef_re[..., None] * br - coef_im[..., None] * bi
    bb_im = coef_re[..., None] * bi + coef_im[..., None] * br
    bu_re = jnp.einsum('btgi,gpi->btgp', uf, bb_re)
    bu_im = jnp.einsum('btgi,gpi->btgp', uf, bb_im)
    a_re = jnp.broadcast_to(ab_re[None, None], (1, T, G, P))
    a_im = jnp.broadcast_to(ab_im[None, None], (1, T, G, P))

    def combine(e1, e2):
        a1r, a1i, b1r, b1i = e1
        a2r, a2i, b2r, b2i = e2
        return (a2r * a1r - a2i * a1i, a2r * a1i + a2i * a1r,
                a2r * b1r - a2i * b1i + b2r, a2r * b1i + a2i * b1r + b2i)

    _, _, xr, xi = lax.associative_scan(combine, (a_re, a_im, bu_re, bu_im), axis=1)
    y = (jnp.einsum('btgp,gip->btgi', xr, c_re.astype(f32))
         - jnp.einsum('btgp,gip->btgi', xi, c_im.astype(f32)))
    y = y + uf * d_skip.astype(f32).reshape(G, I)
    y = jax.nn.gelu(y).reshape(Bsz, T, D_MODEL).astype(u.dtype)
    a, g = jnp.split(y @ w_glu, 2, axis=-1)
    return a * jax.nn.sigmoid(g)


def conv_ffn(h, w_in, conv_w, conv_b, w_out):
    gate, val = jnp.split(h @ w_in, 2, axis=-1)
    gate = lax.conv_general_dilated(
        gate, conv_w[:, None, :], window_strides=(1,), padding=[(CONV_WIDTH - 1, 0)],
        dimension_numbers=('NWC', 'WIO', 'NWC'), feature_group_count=D_FF) + conv_b
    return (jax.nn.gelu(gate) * val) @ w_out


def _compress(z, pe, w1, w2):
    Bsz, T, G, dh = z.shape
    r = CMP_BLOCK // CMP_STRIDE
    ch = z.reshape(Bsz, T // CMP_STRIDE, CMP_STRIDE, G, dh)
    n_cmp = T // CMP_STRIDE - r + 1
    blk = jnp.concatenate([ch[:, j:j + n_cmp] for j in range(r)], axis=2)
    blk = blk + pe[None, None, :, None, :]
    blk = blk.transpose(0, 1, 3, 2, 4).reshape(Bsz, n_cmp, G, CMP_BLOCK * dh)
    return jax.nn.gelu(blk @ w1) @ w2


def shared_kv(s, w_kv, pe_k, pe_v, k_w1, k_w2, v_w1, v_w2):
    Bsz, T, _ = s.shape
    kv = (s @ w_kv).reshape(Bsz, T, 6, KV_HEADS, HEAD_DIM)
    k_cmp_t, v_cmp_t = kv[:, :, 0], kv[:, :, 1]
    k_slc, v_slc = kv[:, :, 2], kv[:, :, 3]
    k_win, v_win = kv[:, :, 4], kv[:, :, 5]
    pos = jnp.arange(T)
    k_slc = rope_partial(k_slc, pos)
    k_win = rope_partial(k_win, pos)
    k_cmp = _compress(k_cmp_t, pe_k, k_w1, k_w2)
    v_cmp = _compress(v_cmp_t, pe_v, v_w1, v_w2)
    n_cmp = k_cmp.shape[1]
    k_cmp = rope_partial(k_cmp, jnp.arange(n_cmp) * CMP_STRIDE + CMP_BLOCK - 1)
    n_sb = T // SEL_BLOCK
    k_sel = k_slc.reshape(Bsz, n_sb, SEL_BLOCK, KV_HEADS, HEAD_DIM).transpose(0, 3, 1, 2, 4)
    v_sel = v_slc.reshape(Bsz, n_sb, SEL_BLOCK, KV_HEADS, HEAD_DIM).transpose(0, 3, 1, 2, 4)
    padw = ((0, 0), (WINDOW, 0), (0, 0), (0, 0))
    return (k_cmp, v_cmp, k_sel, v_sel, jnp.pad(k_win, padw), jnp.pad(v_win, padw))


def nsa_mixer(h, w_q, w_o, k_cmp, v_cmp, k_sel, v_sel, k_win_pad, v_win_pad):
    f32 = jnp.float32
    Bsz, T, _ = h.shape
    G, R, dh = KV_HEADS, N_HEADS // KV_HEADS, HEAD_DIM
    proj = h @ w_q
    q = proj[..., :N_HEADS * dh].reshape(Bsz, T, N_HEADS, dh)
    gates = jax.nn.sigmoid(proj[..., N_HEADS * dh:].astype(f32)).reshape(Bsz, T, N_HEADS, 3)
    q = rope_partial(q, jnp.arange(T)) * (dh ** -0.5)
    n_blk = T // Q_BLOCK
    q_blocks = q.reshape(Bsz, n_blk, Q_BLOCK, G, R, dh).transpose(1, 0, 2, 3, 4, 5)
    n_cmp = k_cmp.shape[1]
    n_sb = k_sel.shape[2]
    top_n = min(SEL_TOPN, n_sb)
    cmp_end = jnp.arange(n_cmp) * CMP_STRIDE + CMP_BLOCK - 1
    ci = np.arange(n_cmp)[:, None]
    sj = np.arange(n_sb)[None, :]
    overlap = jnp.asarray(((ci * CMP_STRIDE < (sj + 1) * SEL_BLOCK)
                           & (ci * CMP_STRIDE + CMP_BLOCK > sj * SEL_BLOCK)).astype(np.float32))
    b_ix = jnp.arange(Bsz)[:, None, None, None]
    g_ix = jnp.arange(G)[None, None, :, None]
    blk_id = jnp.arange(n_sb)

    def attend_block(args):
        qb, bi = args
        t = bi * Q_BLOCK + jnp.arange(Q_BLOCK)
        s1 = jnp.einsum('bqgrd,bngd->bqgrn', qb, k_cmp).astype(f32)
        m1 = (cmp_end[None, :] <= t[:, None])[None, :, None, None, :]
        p1 = jax.nn.softmax(jnp.where(m1, s1, NEG), axis=-1) * m1
        o_c = jnp.einsum('bqgrn,bngd->bqgrd', p1.astype(v_cmp.dtype), v_cmp)
        imp = jnp.einsum('bqgrn,nj->bqgj', p1, overlap)
        cur = t // SEL_BLOCK
        valid = (blk_id[None, :] <= cur[:, None])[None, :, None, :]
        forced = ((blk_id[None, :] == 0) | (blk_id[None, :] == cur[:, None])
                  | (blk_id[None, :] == cur[:, None] - 1))[None, :, None, :]
        score = jnp.where(forced, SEL_FORCE, jnp.where(valid, imp, SEL_NEG))
        vals, idx = lax.top_k(score, top_n)
        ks = k_sel[b_ix, g_ix, idx]
        vs = v_sel[b_ix, g_ix, idx]
        tok = idx[..., None] * SEL_BLOCK + jnp.arange(SEL_BLOCK)
        m2 = (tok <= t[None, :, None, None, None]) & (vals > 0.5 * SEL_NEG)[..., None]
        m2 = m2.reshape(Bsz, Q_BLOCK, G, 1, top_n * SEL_BLOCK)
        s2 = jnp.einsum('bqgrd,bqgnld->bqgrnl', qb, ks).astype(f32).reshape(Bsz, Q_BLOCK, G, R, top_n * SEL_BLOCK)
        p2 = jax.nn.softmax(jnp.where(m2, s2, NEG), axis=-1)
        o_s = jnp.einsum('bqgrk,bqgkd->bqgrd', p2.astype(vs.dtype), vs.reshape(Bsz, Q_BLOCK, G, top_n * SEL_BLOCK, dh))
        start = bi * Q_BLOCK
        kw = lax.dynamic_slice_in_dim(k_win_pad, start, Q_BLOCK + WINDOW, axis=1)
        vw = lax.dynamic_slice_in_dim(v_win_pad, start, Q_BLOCK + WINDOW, axis=1)
        kpos = start - WINDOW + jnp.arange(Q_BLOCK + WINDOW)
        m3 = ((kpos[None, :] >= 0) & (kpos[None, :] <= t[:, None])
              & (t[:, None] - kpos[None, :] < WINDOW))[None, :, None, None, :]
        s3 = jnp.einsum('bqgrd,bkgd->bqgrk', qb, kw).astype(f32)
        p3 = jax.nn.softmax(jnp.where(m3, s3, NEG), axis=-1)
        o_w = jnp.einsum('bqgrk,bkgd->bqgrd', p3.astype(vw.dtype), vw)
        return (o_c, o_s, o_w)

    o_c, o_s, o_w = lax.map(attend_block, (q_blocks, jnp.arange(n_blk)))

    def unblock(o):
        return o.transpose(1, 0, 2, 3, 4, 5).reshape(Bsz, T, N_HEADS, dh).astype(f32)

    o = (gates[..., 0:1] * unblock(o_c) + gates[..., 1:2] * unblock(o_s)
         + gates[..., 2:3] * unblock(o_w)).astype(h.dtype)
    return o.reshape(Bsz, T, N_HEADS * dh) @ w_o


def setup_inputs(seed: int = 0) -> dict:
    key = jax.random.key(seed)
    ks = jax.random.split(key, 32)
    n_a = DEPTH // 2
    n_b = DEPTH - n_a
    D, G, P, I = D_MODEL, S5_GROUPS, S5_STATE, S5_GROUP
    nrm = jax.random.normal
    f32 = jnp.float32
    x = nrm(ks[0], (BATCH, SEQ, D), f32)
    a_lam_re = -0.5 * jnp.exp(0.05 * nrm(ks[1], (n_a, G, P), f32))
    a_lam_im = math.pi * jnp.arange(P, dtype=f32)[None, None, :] + 0.05 * nrm(ks[2], (n_a, G, P), f32)
    a_log_dt = jax.random.uniform(ks[3], (n_a, G), f32, math.log(1e-3), math.log(1e-1))
    a_b_re = nrm(ks[4], (n_a, G, P, I), f32) * (2 * I) ** -0.5
    a_b_im = nrm(ks[5], (n_a, G, P, I), f32) * (2 * I) ** -0.5
    a_c_re = nrm(ks[6], (n_a, G, I, P), f32) * P ** -0.5
    a_c_im = nrm(ks[7], (n_a, G, I, P), f32) * P ** -0.5
    a_d = nrm(ks[8], (n_a, D), f32)
    a_w_glu = nrm(ks[9], (n_a, D, 2 * D), f32) * D ** -0.5
    b_w_q = nrm(ks[10], (n_b, D, N_HEADS * HEAD_DIM + 3 * N_HEADS), f32) * D ** -0.5
    b_w_o = nrm(ks[11], (n_b, N_HEADS * HEAD_DIM, D), f32) * (N_HEADS * HEAD_DIM) ** -0.5
    kv_norm_g = 1.0 + 0.05 * nrm(ks[12], (D,), f32)
    w_kv = nrm(ks[13], (D, 6 * KV_HEADS * HEAD_DIM), f32) * D ** -0.5
    cmp_pe_k = 0.1 * nrm(ks[14], (CMP_BLOCK, HEAD_DIM), f32)
    cmp_pe_v = 0.1 * nrm(ks[15], (CMP_BLOCK, HEAD_DIM), f32)
    cmp_k_w1 = nrm(ks[16], (CMP_BLOCK * HEAD_DIM, CMP_HIDDEN), f32) * (CMP_BLOCK * HEAD_DIM) ** -0.5
    cmp_k_w2 = nrm(ks[17], (CMP_HIDDEN, HEAD_DIM), f32) * CMP_HIDDEN ** -0.5
    cmp_v_w1 = nrm(ks[18], (CMP_BLOCK * HEAD_DIM, CMP_HIDDEN), f32) * (CMP_BLOCK * HEAD_DIM) ** -0.5
    cmp_v_w2 = nrm(ks[19], (CMP_HIDDEN, HEAD_DIM), f32) * CMP_HIDDEN ** -0.5
    mix_pre_g = 1.0 + 0.05 * nrm(ks[20], (DEPTH, D), f32)
    mix_post_g = 1.0 + 0.05 * nrm(ks[21], (DEPTH, D), f32)
    ffn_pre_g = 1.0 + 0.05 * nrm(ks[22], (DEPTH, D), f32)
    ffn_post_g = 1.0 + 0.05 * nrm(ks[23], (DEPTH, D), f32)
    ffn_w_in = nrm(ks[24], (DEPTH, D, 2 * D_FF), f32) * D ** -0.5
    ffn_conv_w = nrm(ks[25], (DEPTH, CONV_WIDTH, D_FF), f32) * CONV_WIDTH ** -0.5
    ffn_conv_b = 0.01 * nrm(ks[26], (DEPTH, D_FF), f32)
    ffn_w_out = nrm(ks[27], (DEPTH, D_FF, D), f32) * D_FF ** -0.5
    return {"x": x, "a_lam_re": a_lam_re, "a_lam_im": a_lam_im, "a_log_dt": a_log_dt,
            "a_b_re": a_b_re, "a_b_im": a_b_im, "a_c_re": a_c_re, "a_c_im": a_c_im,
            "a_d": a_d, "a_w_glu": a_w_glu, "b_w_q": b_w_q, "b_w_o": b_w_o,
            "kv_norm_g": kv_norm_g, "w_kv": w_kv, "cmp_pe_k": cmp_pe_k, "cmp_pe_v": cmp_pe_v,
            "cmp_k_w1": cmp_k_w1, "cmp_k_w2": cmp_k_w2, "cmp_v_w1": cmp_v_w1, "cmp_v_w2": cmp_v_w2,
            "mix_pre_g": mix_pre_g, "mix_post_g": mix_post_g, "ffn_pre_g": ffn_pre_g,
            "ffn_post_g": ffn_post_g, "ffn_w_in": ffn_w_in, "ffn_conv_w": ffn_conv_w,
            "ffn_conv_b": ffn_conv_b, "ffn_w_out": ffn_w_out}


def reference(x, a_lam_re, a_lam_im, a_log_dt, a_b_re, a_b_im, a_c_re, a_c_im, a_d, a_w_glu,
              b_w_q, b_w_o, kv_norm_g, w_kv, cmp_pe_k, cmp_pe_v, cmp_k_w1, cmp_k_w2,
              cmp_v_w1, cmp_v_w2, mix_pre_g, mix_post_g, ffn_pre_g, ffn_post_g,
              ffn_w_in, ffn_conv_w, ffn_conv_b, ffn_w_out):
    n_a = DEPTH // 2
    kv = None
    for layer in range(DEPTH):
        h = rmsnorm(x, mix_pre_g[layer])
        if layer < n_a:
            i = layer
            m = s5_mixer(h, a_lam_re[i], a_lam_im[i], a_log_dt[i], a_b_re[i], a_b_im[i],
                         a_c_re[i], a_c_im[i], a_d[i], a_w_glu[i])
        else:
            j = layer - n_a
            m = nsa_mixer(h, b_w_q[j], b_w_o[j], *kv)
        x = x + rmsnorm(m, mix_post_g[layer])
        h = rmsnorm(x, ffn_pre_g[layer])
        f = conv_ffn(h, ffn_w_in[layer], ffn_conv_w[layer], ffn_conv_b[layer], ffn_w_out[layer])
        x = x + rmsnorm(f, ffn_post_g[layer])
        if layer == n_a - 1:
            kv = shared_kv(rmsnorm(x, kv_norm_g), w_kv, cmp_pe_k, cmp_pe_v,
                           cmp_k_w1, cmp_k_w2, cmp_v_w1, cmp_v_w2)
    return x
```

```python
import contextlib
import math
import numpy as np
import concourse.bass as bass
import concourse.mybir as mybir
from concourse.bass_utils import run_bass_kernel_spmd

F32 = mybir.dt.float32
BF16 = mybir.dt.bfloat16
I32 = mybir.dt.int32
AF = mybir.ActivationFunctionType
ALU = mybir.AluOpType
AX = mybir.AxisListType

T = 4096
D = 1024
NT = T // 128
DC = 8
FF = 2816
FC = FF // 128
EPS = 1e-6
NEGM = -30000.0

ENGS = ("tensor", "vector", "scalar", "gpsimd", "sync")


class Op:
    __slots__ = ("eng", "fn", "reads", "writes", "dma", "waits", "sig", "idx")

    def __init__(self, eng, fn, reads, writes, dma):
        self.eng, self.fn, self.reads, self.writes, self.dma = eng, fn, reads, writes, dma
        self.waits = {}
        self.sig = None
        self.idx = None


def _key(t):
    if isinstance(t, (str, tuple)):
        return t
    return t.name


class Prog:
    def __init__(self, nc):
        self.nc = nc
        self.ops = []
        self.bg = set()
        self.bgq = []

    def pump(self, n=None):
        k = len(self.bgq) if n is None else min(n, len(self.bgq))
        for _ in range(k):
            self.bgq.pop(0)()

    def add(self, eng, fn, reads=(), writes=(), dma=False):
        op = Op(eng, fn, [_key(r) for r in reads if r is not None],
                [_key(w) for w in writes if w is not None], dma)
        op.idx = len(self.ops)
        self.ops.append(op)
        return op

    def barrier(self):
        op = Op("barrier", None, [], [], False)
        op.idx = len(self.ops)
        self.ops.append(op)
        return op

    def dma(self, eng, out, in_, reads=None, writes=None):
        r = [in_] if reads is None else reads
        w = [out] if writes is None else writes
        return self.add(eng, lambda e: e.dma_start(out=out, in_=in_), r, w, dma=True)

    def act(self, out, in_, func, bias=None, scale=None, accum_out=None, eng="scalar", reads=None, writes=None):
        kw = {}
        rd = [in_]
        if bias is not None:
            kw["bias"] = bias
            if not isinstance(bias, (int, float)):
                rd.append(bias)
        if scale is not None:
            kw["scale"] = scale
            if not isinstance(scale, (int, float)):
                rd.append(scale)
        wr = [out]
        if accum_out is not None:
            kw["accum_out"] = accum_out
            wr.append(accum_out)
        return self.add(eng, lambda e: e.activation(out=out, in_=in_, func=func, **kw),
                        rd if reads is None else reads, wr if writes is None else writes)

    def tt(self, out, in0, in1, op, eng="vector", reads=None, writes=None):
        return self.add(eng, lambda e: e.tensor_tensor(out=out, in0=in0, in1=in1, op=op),
                        [in0, in1] if reads is None else reads, [out] if writes is None else writes)

    def ts(self, out, in0, s1, op0, s2=None, op1=None, eng="vector", reads=None, writes=None):
        rd = [in0]
        for s in (s1, s2):
            if s is not None and not isinstance(s, (int, float)):
                rd.append(s)
        if op1 is None:
            f = lambda e: e.tensor_scalar(out=out, in0=in0, scalar1=s1, scalar2=None, op0=op0)
        else:
            f = lambda e: e.tensor_scalar(out=out, in0=in0, scalar1=s1, scalar2=s2, op0=op0, op1=op1)
        return self.add(eng, f, rd if reads is None else reads, [out] if writes is None else writes)

    def stt(self, out, in0, scalar, in1, op0, op1, reads=None, writes=None):
        rd = [in0, in1]
        if not isinstance(scalar, (int, float)):
            rd.append(scalar)
        return self.add("vector", lambda e: e.scalar_tensor_tensor(out=out, in0=in0, scalar=scalar, in1=in1, op0=op0, op1=op1),
                        rd if reads is None else reads, [out] if writes is None else writes)

    def copy(self, out, in_, eng="vector", reads=None, writes=None):
        if eng == "scalar":
            return self.add(eng, lambda e: e.activation(out=out, in_=in_, func=AF.Copy), [in_] if reads is None else reads,
                            [out] if writes is None else writes)
        return self.add(eng, lambda e: e.tensor_copy(out=out, in_=in_), [in_] if reads is None else reads,
                        [out] if writes is None else writes)

    def memset(self, ap, val, eng="vector"):
        return self.add(eng, lambda e: e.memset(ap, val), [], [ap])

    def mm(self, out, lhsT, rhs, start, stop, reads=None, writes=None):
        return self.add("tensor", lambda e: e.matmul(out, lhsT=lhsT, rhs=rhs, start=start, stop=stop, skip_group_check=True),
                        [lhsT, rhs] if reads is None else reads, [out] if writes is None else writes)

    def tr(self, out, in_, ident, reads=None, writes=None):
        return self.add("tensor", lambda e: e.transpose(out, in_, ident),
                        [in_, ident] if reads is None else reads, [out] if writes is None else writes)

    def emit(self):
        nc, ops = self.nc, self.ops
        last_w, readers = {}, {}
        deps = [set() for _ in ops]
        for op in ops:
            ds = deps[op.idx]
            if op.eng == "barrier":
                last_w = {k_: v_ for k_, v_ in last_w.items() if k_ in self.bg}
                readers = {}
                continue
            for b in op.reads:
                if b in last_w:
                    ds.add(last_w[b])
            for b in op.writes:
                if b in last_w:
                    ds.add(last_w[b])
                for r in readers.get(b, ()):
                    ds.add(r)
            for b in op.reads:
                readers.setdefault(b, []).append(op.idx)
            for b in op.writes:
                last_w[b] = op.idx
                readers[b] = []
            ds.discard(op.idx)

        def needs(p, c):
            if p.dma:
                return True
            if p.eng != c.eng:
                return True
            return p.eng != "tensor"

        need_sig = set()
        for op in ops:
            for d in deps[op.idx]:
                if needs(ops[d], op):
                    need_sig.add(d)
        last_on = {}
        for op in ops:
            if op.eng == "barrier":
                need_sig.update(last_on.values())
                continue
            if op.dma:
                need_sig.add(op.idx)
            else:
                last_on[op.eng] = op.idx
        eng_cnt = {e: 0 for e in ENGS}
        dma_cnt, dma_keys, sigval = {}, [], {}
        bar_snap = {}
        for op in ops:
            if op.eng == "barrier":
                snap = {("eng", e): v for e, v in eng_cnt.items() if v > 0}
                snap.update({k_: v_ for k_, v_ in dma_cnt.items() if k_[1] not in self.bg})
                bar_snap[op.idx] = snap
                continue
            if op.idx not in need_sig:
                continue
            if op.dma:
                k = ("dma", op.writes[0] if op.writes else op.reads[0])
                if k not in dma_cnt:
                    dma_cnt[k] = 0
                    dma_keys.append(k)
                dma_cnt[k] += 16
                op.sig = (k, 16)
                sigval[op.idx] = (k, dma_cnt[k])
            else:
                k = ("eng", op.eng)
                eng_cnt[op.eng] += 1
                op.sig = (k, 1)
                sigval[op.idx] = (k, eng_cnt[op.eng])
        seen = {e: {} for e in ENGS}
        pend = {e: {} for e in ENGS}
        for op in ops:
            if op.eng == "barrier":
                for e in ENGS:
                    for k, v in bar_snap[op.idx].items():
                        if pend[e].get(k, 0) < v:
                            pend[e][k] = v
                continue
            if pend[op.eng]:
                for k, v in pend[op.eng].items():
                    if seen[op.eng].get(k, 0) < v and op.waits.get(k, 0) < v:
                        op.waits[k] = v
                pend[op.eng] = {}
            for d in deps[op.idx]:
                if d not in sigval or not needs(ops[d], op):
                    continue
                k, v = sigval[d]
                if seen[op.eng].get(k, 0) >= v:
                    continue
                if op.waits.get(k, 0) < v:
                    op.waits[k] = v
            for k, v in op.waits.items():
                seen[op.eng][k] = v
        self.stats = dict(n_ops=len(ops), n_dma_sems=len(dma_keys), eng_cnt=eng_cnt)
        with contextlib.ExitStack() as st:
            sems = {}
            for e in ENGS:
                sems[("eng", e)] = st.enter_context(nc.semaphore("s_" + e))
            for i, k in enumerate(dma_keys):
                sems[k] = st.enter_context(nc.semaphore("d%d" % i))
            block = st.enter_context(nc.Block())
            by_eng = {e: [o for o in ops if o.eng == e] for e in ENGS}
            self.stats["per_eng"] = {e: len(v) for e, v in by_eng.items()}

            def make(e):
                def body(eng):
                    for op in by_eng[e]:
                        for k, v in op.waits.items():
                            eng.wait_ge(sems[k], v)
                        ins = op.fn(eng)
                        if op.sig is not None:
                            ins.then_inc(sems[op.sig[0]], op.sig[1])
                    if e == "sync":
                        for k, v in dma_cnt.items():
                            eng.wait_ge(sems[k], v)
                        for e2 in ENGS:
                            if e2 != "sync" and eng_cnt[e2] > 0:
                                eng.wait_ge(sems[("eng", e2)], eng_cnt[e2])
                return body

            for e in ENGS:
                getattr(block, e)(make(e))
        return self


def _st_layout(a):
    return np.ascontiguousarray(a.reshape(32, 2, 64).transpose(1, 2, 0).reshape(128, 32))


def _blk_layout(a):
    out = np.zeros((128, 32, 32), np.float32)
    a = a.reshape(32, 2, 64, 16)
    for gl in range(2):
        out[gl * 64:(gl + 1) * 64, :, gl * 16:(gl + 1) * 16] = a[:, gl].transpose(1, 0, 2)
    return out


def host_layout(inp, b):
    f = lambda a: np.ascontiguousarray(a, dtype=np.float32)
    m = {}
    m["x"] = f(inp["x"][b])
    lam = np.stack([_st_layout(f(inp["a_lam_re"][0])), _st_layout(f(inp["a_lam_im"][0])),
                    _st_layout(np.repeat(f(inp["a_log_dt"][0])[:, None], 64, axis=1))], axis=1)
    m["s5par"] = f(lam)
    m["s5b"] = f(np.stack([_blk_layout(f(inp["a_b_re"][0])), _blk_layout(f(inp["a_b_im"][0]))], axis=1))
    m["s5c"] = f(np.stack([_blk_layout(f(inp["a_c_re"][0]).transpose(0, 2, 1)),
                           _blk_layout(f(inp["a_c_im"][0]).transpose(0, 2, 1))], axis=1))
    m["s5d"] = f(inp["a_d"][0].reshape(8, 128).T)
    m["wglu"] = f(inp["a_w_glu"][0])
    gains = np.stack([inp["mix_pre_g"][0], inp["mix_post_g"][0], inp["ffn_pre_g"][0], inp["ffn_post_g"][0],
                      inp["kv_norm_g"], inp["mix_pre_g"][1], inp["mix_post_g"][1], inp["ffn_pre_g"][1],
                      inp["ffn_post_g"][1]], axis=0)
    m["gains"] = f(gains)
    for l in range(2):
        m["win%d" % l] = f(inp["ffn_w_in"][l])
        m["wout%d" % l] = f(inp["ffn_w_out"][l])
        cw = inp["ffn_conv_w"][l].reshape(3, FC, 128).transpose(2, 1, 0)
        cb = inp["ffn_conv_b"][l].reshape(FC, 128).T[:, :, None]
        m["conv%d" % l] = f(np.concatenate([cw, cb], axis=2))
    m["ident"] = np.eye(128, dtype=np.float32)
    return m


class Ctx:
    pass


def strided(ap2d, k, step=8):
    return ap2d.rearrange("p (c k) -> p k c", k=step)[:, k, :]


def emit_norm_A(P, C, src_tile, gain_ap, gain_key, h, idx):
    s = C.stat[idx % 2]
    P.act(C.sq[:], src_tile[:], AF.Square, accum_out=s[:, 0:1])
    P.act(s[:, 1:2], s[:, 0:1], AF.Sqrt, bias=C.epsb[:, 0:1], scale=1.0 / D)
    P.add("vector", lambda e: e.reciprocal(out=s[:, 2:3], in_=s[:, 1:2]), [s], [s])
    P.stt(h[:], src_tile[:], s[:, 2:3], gain_ap, ALU.mult, ALU.mult, reads=[src_tile, s, gain_key])


def emit_norm_B(P, C, h, dstT, col0, idx, ceng="scalar"):
    pT = C.ptr[idx % 2]
    for dc in range(DC):
        P.tr(pT[:, dc, :], h[:, dc * 128:(dc + 1) * 128], C.identb[:])
    P.copy(dstT[:, :, col0:col0 + 128], pT[:], eng=ceng)


def emit_norm_to_T(P, C, src_tile, gain_ap, gain_key, dstT, col0, idx, ceng="scalar"):
    s = C.stat[idx % 2]
    P.act(C.sq[:], src_tile[:], AF.Square, accum_out=s[:, 0:1])
    P.act(s[:, 1:2], s[:, 0:1], AF.Sqrt, bias=C.epsb[:, 0:1], scale=1.0 / D)
    P.add("vector", lambda e: e.reciprocal(out=s[:, 2:3], in_=s[:, 1:2]), [s], [s])
    h = C.hb[idx % 2]
    P.stt(h[:], src_tile[:], s[:, 2:3], gain_ap, ALU.mult, ALU.mult, reads=[src_tile, s, gain_key])
    pT = C.ptr[idx % 2]
    for dc in range(DC):
        P.tr(pT[:, dc, :], h[:, dc * 128:(dc + 1) * 128], C.identb[:])
    P.copy(dstT[:, :, col0:col0 + 128], pT[:], eng=ceng)


def emit_postnorm_residual(P, C, f_tile, res_tile, gain_ap, gain_key, out_tile, idx):
    s = C.stat[idx % 2]
    P.act(C.sq[:], f_tile[:], AF.Square, accum_out=s[:, 0:1])
    P.act(s[:, 1:2], s[:, 0:1], AF.Sqrt, bias=C.epsb[:, 0:1], scale=1.0 / D)
    P.add("vector", lambda e: e.reciprocal(out=s[:, 2:3], in_=s[:, 1:2]), [s], [s])
    P.stt(f_tile[:], f_tile[:], s[:, 2:3], gain_ap, ALU.mult, ALU.mult, reads=[f_tile, s, gain_key])
    P.tt(out_tile[:], f_tile[:], res_tile[:], ALU.add, eng="gpsimd")


def emit_s5(P, C, nc, st_alloc, x_d, s5par_d, s5b_d, s5c_d, s5d_d, gains_d, yT):
    sb, ps = st_alloc
    V, G = "vector", "gpsimd"
    par = sb("s5par", [128, 3, 32])
    P.dma("sync", par[:], s5par_d)
    pp = sb("s5pp", [128, 24, 32])
    ki = sb("s5ki", [128, 32], I32)
    pw = sb("s5pw", [128, 2, 9, 32])
    cF = sb("s5cF", [128, 2, 8, 32])
    zt = sb("s5zt", [128, 2, 9, 32])
    RR = sb("s5RR", [128, 32])
    dsk = sb("s5dsk", [128, 8])
    P.dma("sync", dsk[:], s5d_d)
    LR, LI, LDT = par[:, 0, :], par[:, 1, :], par[:, 2, :]
    sl = lambda i: pp[:, i, :]
    (DT, LRDT, TH, MAG, Q, KF, THR, SH, CH, SIN, COS, ABR, ABI, NR, DEN, RDEN, T1, T2, COEFR, COEFI, INVR) = [sl(i) for i in range(21)]
    P.act(DT, LDT, AF.Exp)
    P.tt(LRDT, LR, DT, ALU.mult)
    P.tt(TH, LI, DT, ALU.mult)
    P.act(MAG, LRDT, AF.Exp)
    P.ts(Q, TH, 1.0 / (2 * math.pi), ALU.mult)
    P.copy(ki[:], Q)
    P.copy(KF, ki[:])
    P.stt(THR, KF, -2 * math.pi, TH, ALU.mult, ALU.add)
    P.act(SH, THR, AF.Sin, scale=0.5)
    P.act(CH, THR, AF.Sin, scale=-0.5, bias=C.halfpi[:, 0:1])
    P.stt(SIN, SH, 2.0, CH, ALU.mult, ALU.mult)
    P.tt(COS, SH, SH, ALU.mult)
    P.ts(COS, COS, -2.0, ALU.mult, 1.0, ALU.add)
    P.tt(ABR, MAG, COS, ALU.mult)
    P.tt(ABI, MAG, SIN, ALU.mult)
    P.ts(NR, ABR, -1.0, ALU.add)
    P.tt(DEN, LR, LR, ALU.mult)
    P.tt(T1, LI, LI, ALU.mult)
    P.tt(DEN, DEN, T1, ALU.add)
    P.add(V, lambda e: e.reciprocal(out=RDEN, in_=DEN), [pp], [pp])
    P.tt(T1, NR, LR, ALU.mult)
    P.tt(T2, ABI, LI, ALU.mult)
    P.tt(T1, T1, T2, ALU.add)
    P.tt(COEFR, T1, RDEN, ALU.mult)
    P.tt(T1, ABI, LR, ALU.mult)
    P.tt(T2, NR, LI, ALU.mult)
    P.tt(T1, T1, T2, ALU.subtract)
    P.tt(COEFI, T1, RDEN, ALU.mult)

    def cmul(outr, outi, ar, ai, br, bi):
        P.tt(T1, ai, bi, ALU.mult)
        P.tt(T2, ar, br, ALU.mult)
        P.tt(outr, T2, T1, ALU.subtract)
        P.tt(T1, ar, bi, ALU.mult)
        P.tt(T2, ai, br, ALU.mult)
        P.tt(outi, T1, T2, ALU.add)

    P.memset(pw[:, 0, 0, :], 1.0)
    P.memset(pw[:, 1, 0, :], 0.0)
    P.copy(pw[:, 0, 1, :], ABR)
    P.copy(pw[:, 1, 1, :], ABI)
    for k in range(1, 8):
        cmul(pw[:, 0, k + 1, :], pw[:, 1, k + 1, :], pw[:, 0, k, :], pw[:, 1, k, :], ABR, ABI)
    for k in range(8):
        cmul(cF[:, 0, k, :], cF[:, 1, k, :], pw[:, 0, 7 - k, :], pw[:, 1, 7 - k, :], COEFR, COEFI)
    P.act(RR[:], LRDT, AF.Exp, scale=8.0)
    P.act(INVR, LRDT, AF.Exp, scale=-8.0)
    P.tt(zt[:, 0, 0, :], pw[:, 0, 8, :], INVR, ALU.mult)
    P.tt(zt[:, 1, 0, :], pw[:, 1, 8, :], INVR, ALU.mult)
    for j in range(8):
        cmul(zt[:, 0, j + 1, :], zt[:, 1, j + 1, :], zt[:, 0, j, :], zt[:, 1, j, :], zt[:, 0, j, :], zt[:, 1, j, :])

    tabA = sb("s5tabA", [128, 2, 32, 16])
    tabB = sb("s5tabB", [128, 2, 32, 32])
    tT1 = sb("s5A1", [128, 512])
    tT2 = sb("s5A2", [128, 512])
    A1, A2 = tT1, tT2
    for tab, nlev, j0 in ((tabA, 4, 0), (tabB, 5, 4)):
        P.memset(tab[:, 0, :, 0:1], 1.0)
        P.memset(tab[:, 1, :, 0:1], 0.0)
        for lv in range(nlev):
            n = 1 << lv
            zr = zt[:, 0, j0 + lv, :].unsqueeze(2).to_broadcast([128, 32, n])
            zi = zt[:, 1, j0 + lv, :].unsqueeze(2).to_broadcast([128, 32, n])
            ar, ai = tab[:, 0, :, 0:n], tab[:, 1, :, 0:n]
            t1 = tT1[:].rearrange("p (a b) -> p a b", b=16)[:, :, 0:n]
            t2 = tT2[:].rearrange("p (a b) -> p a b", b=16)[:, :, 0:n]
            P.tt(t1, ar, zr, ALU.mult, reads=[tab, zt], writes=[tT1])
            P.tt(t2, ai, zi, ALU.mult, reads=[tab, zt], writes=[tT2])
            P.tt(tab[:, 0, :, n:2 * n], t1, t2, ALU.subtract, reads=[tT1, tT2], writes=[tab])
            P.tt(t1, ar, zi, ALU.mult, reads=[tab, zt], writes=[tT1])
            P.tt(t2, ai, zr, ALU.mult, reads=[tab, zt], writes=[tT2])
            P.tt(tab[:, 1, :, n:2 * n], t1, t2, ALU.add, reads=[tT1, tT2], writes=[tab])

    rstd = sb("s5rstd", [128, NT])
    ssq = sb("s5ssq", [128, NT])
    for tt in range(NT):
        xi = C.xin[tt % 2]
        P.dma("sync", xi[:], x_d[tt * 128:(tt + 1) * 128, :])
        P.act(C.sq[:], xi[:], AF.Square, accum_out=ssq[:, tt:tt + 1], writes=[C.sq, ("ssq", tt)])
    P.act(ssq[:], ssq[:], AF.Sqrt, bias=C.epsb[:, 0:1], scale=1.0 / D,
          reads=[("ssq", t) for t in range(NT)] + [C.epsb], writes=["ssq_all"])
    P.add(V, lambda e: e.reciprocal(out=rstd[:], in_=ssq[:]), ["ssq_all"], [rstd])

    g0 = sb("s5g0", [128, D])
    P.dma("sync", g0[:], gains_d[0:1, :].partition_broadcast(128))
    HB = NT // 4
    xblk = sb("s5xblk", [128, HB, 128])
    hblk = sb("s5hblk", [128, HB, 128], BF16)
    hTd = [sb("s5hT%d" % i, [128, 8, T // 8], BF16) for i in range(1)]
    bc = sb("s5bc", [128, 2, 4, 32])
    cc = sb("s5cc", [128, 2, 4, 32])
    tA = sb("s5tA", [128, 9, 32])
    tB = sb("s5tB", [128, 9, 32])
    Fre = [sb("s5Fre%d" % q, [128, 8, 128], BF16) for q in range(4)]
    Fim = [sb("s5Fim%d" % q, [128, 8, 128], BF16) for q in range(4)]
    Ere = [sb("s5Ere%d" % q, [128, 9, 128], BF16) for q in range(4)]
    Eni = [sb("s5Eni%d" % q, [128, 9, 128], BF16) for q in range(4)]
    for q in range(4):
        for t_ in (Fre[q], Fim[q], Ere[q], Eni[q]):
            P.memset(t_[:], 0.0, eng=G)
    FTre = [sb("s5FTre%d" % i, [128, 8, 128], BF16) for i in range(2)]
    FTim = [sb("s5FTim%d" % i, [128, 8, 128], BF16) for i in range(2)]
    crT = [sb("s5cr%d" % i, [128, 512]) for i in range(2)]
    srT = [sb("s5sr%d" % i, [128, 512]) for i in range(2)]
    pt1 = sb("s5pt1", [128, 512])
    pt2 = sb("s5pt2", [128, 512])
    Vpr = sb("s5Vpr", [128, 512])
    Vpi = sb("s5Vpi", [128, 512])
    Wrs = [sb("s5Wr%d" % i, [128, 512]) for i in range(2)]
    Wis = [sb("s5Wi%d" % i, [128, 512]) for i in range(2)]
    Xs = sb("s5Xs", [128, 4, 2, 520], BF16)
    P.add(G, lambda e: e.memset(Xs[:], 0.0), [], [("Xs", q) for q in range(4)])
    KT = sb("s5KT", [128, 8, 128], BF16)
    ytmp = [sb("s5yt%d" % i, [128, 512]) for i in range(1)] * 2
    pvr, pvi, pk0, pk1, py0, py1 = C.pb[0], C.pb[1], C.pb[2], C.pb[3], C.pb[4], C.pb[5]
    xview = x_d.rearrange("(t p) (dc c) -> p t dc c", p=128, c=128)

    for dc in range(DC):
        hT = hTd[0]
        for hf in range(4):
            P.dma("sync", xblk[:], xview[:, hf * HB:(hf + 1) * HB, dc, :])
            P.tt(xblk[:], xblk[:], rstd[:, hf * HB:(hf + 1) * HB].unsqueeze(2).to_broadcast([128, HB, 128]), ALU.mult)
            P.tt(hblk[:], xblk[:], g0[:, dc * 128:(dc + 1) * 128].unsqueeze(1).to_broadcast([128, HB, 128]), ALU.mult)
            for t8 in range(HB // 8):
                pT = C.ptr[t8 % 2]
                for j in range(8):
                    P.tr(pT[:, j, :], hblk[:, t8 * 8 + j, :], C.identb[:])
                c0_ = (hf * HB + t8 * 8) * 16
                P.copy(hT[:, :, c0_:c0_ + 128].rearrange("p k c -> p c k"),
                       pT[:].rearrange("p a b -> p (a b)").rearrange("p (c k) -> p c k", k=8), eng="scalar")
        P.dma("sync", bc[:], s5b_d[:, :, dc * 4:(dc + 1) * 4, :])
        P.dma("sync", cc[:], s5c_d[:, :, dc * 4:(dc + 1) * 4, :])
        def stage1(q, dc=dc, hT=hT):
            st = dc * 4 + q
            co = 32 * q
            pvr, pvi = C.pb[2 * (q % 2)], C.pb[2 * (q % 2) + 1]
            cFr = cF[:, 0, :, st:st + 1].to_broadcast([128, 8, 32])
            cFi = cF[:, 1, :, st:st + 1].to_broadcast([128, 8, 32])
            Br = bc[:, 0, q:q + 1, :].to_broadcast([128, 8, 32])
            Bi = bc[:, 1, q:q + 1, :].to_broadcast([128, 8, 32])
            a8, b8 = tA[:, 0:8, :], tB[:, 0:8, :]
            P.tt(a8, Br, cFr, ALU.mult, reads=[bc, cF], writes=[tA])
            P.tt(b8, Bi, cFi, ALU.mult, reads=[bc, cF], writes=[tB])
            P.tt(Fre[q][:, :, co:co + 32], a8, b8, ALU.subtract, reads=[tA, tB], writes=[Fre[q]])
            P.tt(a8, Br, cFi, ALU.mult, reads=[bc, cF], writes=[tA])
            P.tt(b8, Bi, cFr, ALU.mult, reads=[bc, cF], writes=[tB])
            P.tt(Fim[q][:, :, co:co + 32], a8, b8, ALU.add, reads=[tA, tB], writes=[Fim[q]])
            pr = pw[:, 0, :, st:st + 1].to_broadcast([128, 9, 32])
            pi = pw[:, 1, :, st:st + 1].to_broadcast([128, 9, 32])
            Cr = cc[:, 0, q:q + 1, :].to_broadcast([128, 9, 32])
            Ci = cc[:, 1, q:q + 1, :].to_broadcast([128, 9, 32])
            P.tt(tA[:], Cr, pr, ALU.mult, reads=[cc, pw], writes=[tA])
            P.tt(tB[:], Ci, pi, ALU.mult, reads=[cc, pw], writes=[tB])
            P.tt(Ere[q][:, :, co:co + 32], tA[:], tB[:], ALU.subtract, reads=[tA, tB], writes=[Ere[q]])
            P.tt(tA[:], Cr, pi, ALU.mult, reads=[cc, pw], writes=[tA])
            P.tt(tB[:], Ci, pr, ALU.mult, reads=[cc, pw], writes=[tB])
            P.stt(Eni[q][:, :, co:co + 32], tA[:], -1.0, tB[:], ALU.mult, ALU.subtract, reads=[tA, tB], writes=[Eni[q]])
            ftr, fti = FTre[q % 2], FTim[q % 2]
            for k in range(8):
                P.tr(C.ptr[0][:, k, :], Fre[q][:, k, :], C.identb[:])
            P.copy(ftr[:], C.ptr[0][:], eng="scalar")
            for k in range(8):
                P.tr(C.ptr[1][:, k, :], Fim[q][:, k, :], C.identb[:])
            P.copy(fti[:], C.ptr[1][:], eng="scalar")
            for k in range(8):
                P.mm(pvr[:], ftr[:, k, :], hT[:, k, :], k == 0, k == 7)
            for k in range(8):
                P.mm(pvi[:], fti[:, k, :], hT[:, k, :], k == 0, k == 7)
            cr, sr = crT[q % 2], srT[q % 2]
            Br = tabB[:, 0, st, :].unsqueeze(2).to_broadcast([128, 32, 16])
            Bi = tabB[:, 1, st, :].unsqueeze(2).to_broadcast([128, 32, 16])
            Ar = tabA[:, 0, st, :].unsqueeze(1).to_broadcast([128, 32, 16])
            Ai = tabA[:, 1, st, :].unsqueeze(1).to_broadcast([128, 32, 16])
            v3 = lambda t_: t_[:].rearrange("p (a b) -> p a b", b=16)
            P.tt(v3(pt1), Br, Ar, ALU.mult, eng=G, reads=[tabA, tabB], writes=[pt1])
            P.tt(v3(pt2), Bi, Ai, ALU.mult, eng=G, reads=[tabA, tabB], writes=[pt2])
            P.tt(cr[:], pt1[:], pt2[:], ALU.subtract, eng=G)
            P.tt(v3(pt1), Br, Ai, ALU.mult, eng=G, reads=[tabA, tabB], writes=[pt1])
            P.tt(v3(pt2), Bi, Ar, ALU.mult, eng=G, reads=[tabA, tabB], writes=[pt2])
            P.tt(sr[:], pt1[:], pt2[:], ALU.add, eng=G)
        def stage2(q, dc=dc):
            st = dc * 4 + q
            pvr, pvi = C.pb[2 * (q % 2)], C.pb[2 * (q % 2) + 1]
            cr, sr = crT[q % 2], srT[q % 2]
            Wr, Wi = Wrs[q % 2], Wis[q % 2]
            P.tt(A1[:], pvr[:], cr[:], ALU.mult)
            P.tt(A2[:], pvi[:], sr[:], ALU.mult)
            P.tt(Vpr[:], A1[:], A2[:], ALU.add)
            P.tt(A1[:], pvi[:], cr[:], ALU.mult)
            P.tt(A2[:], pvr[:], sr[:], ALU.mult)
            P.tt(Vpi[:], A1[:], A2[:], ALU.subtract)
            Rb = RR[:, st:st + 1].to_broadcast([128, 512])
            P.add(V, lambda e, Rb=Rb: e.tensor_tensor_scan(out=Wr[:], data0=Rb, data1=Vpr[:], initial=0.0, op0=ALU.mult, op1=ALU.add),
                  [RR, Vpr], [Wr])
            P.add(V, lambda e, Rb=Rb: e.tensor_tensor_scan(out=Wi[:], data0=Rb, data1=Vpi[:], initial=0.0, op0=ALU.mult, op1=ALU.add),
                  [RR, Vpi], [Wi])
            P.tt(pt1[:], Wr[:], cr[:], ALU.mult, eng=G)
            P.tt(pt2[:], Wi[:], sr[:], ALU.mult, eng=G)
            P.tt(Xs[:, q, 0, 1:513], pt1[:], pt2[:], ALU.subtract, eng=G, writes=[("Xs", q)])
            P.tt(pt1[:], Wi[:], cr[:], ALU.mult, eng=G)
            P.tt(pt2[:], Wr[:], sr[:], ALU.mult, eng=G)
            P.tt(Xs[:, q, 1, 1:513], pt1[:], pt2[:], ALU.add, eng=G, writes=[("Xs", q)])
        stage1(0)
        for q in range(4):
            if q < 3:
                stage1(q + 1)
            stage2(q)
            P.pump(1)
        for tau in range(8):
            pk = (pk0, pk1)[tau // 4]
            o = pk[:, (tau % 4) * 128:(tau % 4 + 1) * 128]
            n = 0
            for q in range(4):
                P.mm(o, Fre[q][:, 7 - tau, :], Ere[q][:, 0, :], n == 0, False)
                n += 1
                P.mm(o, Fim[q][:, 7 - tau, :], Eni[q][:, 0, :], False, n == 7)
                n += 1
        P.copy(KT[:, 0:4, :].rearrange("p a b -> p (a b)"), pk0[:], eng="scalar")
        P.copy(KT[:, 4:8, :].rearrange("p a b -> p (a b)"), pk1[:], eng="scalar")
        for k in range(8):
            py = (py0, py1)[k % 2]
            n_mm = 8 + k + 1
            n = 0
            for q in range(4):
                P.mm(py[:], Ere[q][:, k + 1, :], Xs[:, q, 0, 0:512], n == 0, False, reads=[Ere[q], ("Xs", q)])
                n += 1
                P.mm(py[:], Eni[q][:, k + 1, :], Xs[:, q, 1, 0:512], False, False, reads=[Eni[q], ("Xs", q)])
                n += 1
            for k2 in range(k + 1):
                n += 1
                P.mm(py[:], KT[:, k - k2, :], hT[:, k2, :], False, n == n_mm)
            yt = ytmp[k % 2]
            P.stt(yt[:], hT[:, k, :], dsk[:, dc:dc + 1], py[:], ALU.mult, ALU.add)
            P.act(strided(yT[:, dc, :], k), yt[:], AF.Gelu_apprx_tanh, writes=[("yT", dc)])


def emit_glu(P, C, sb, x_d, wglu_d, gains_d, yT, xmid_d, dbg_m=None):
    wglu = sb("wglu", [128, DC, 2 * D], BF16)
    wv = wglu_d.rearrange("(dc p) n -> p dc n", p=128)
    for dc in range(DC):
        P.dma("gpsimd", wglu[:, dc, :], wv[:, dc, :], writes=[("wglu", dc)])
    g1 = sb("glu_g", [128, D])
    P.dma("sync", g1[:], gains_d[1:2, :].partition_broadcast(128))
    sig = sb("glu_sig", [128, D])
    mt = [sb("glu_m%d" % i, [128, D]) for i in range(2)]
    xo = [sb("glu_xo%d" % i, [128, D]) for i in range(2)]
    banks = [C.pb[0], C.pb[1], C.pb[2], C.pb[3]]
    for tt in range(NT):
        xi = C.xin[tt % 2]
        P.dma("sync", xi[:], x_d[tt * 128:(tt + 1) * 128, :])
        for nch in range(4):
            for dc in range(DC):
                P.mm(banks[nch][:], yT[:, dc, tt * 128:(tt + 1) * 128], wglu[:, dc, nch * 512:(nch + 1) * 512],
                     dc == 0, dc == DC - 1, reads=[("yT", dc), ("wglu", dc)])
        m = mt[tt % 2]
        for hh in range(2):
            P.act(sig[:, hh * 512:(hh + 1) * 512], banks[2 + hh][:], AF.Sigmoid, writes=[("sig", hh)])
            P.tt(m[:, hh * 512:(hh + 1) * 512], banks[hh][:], sig[:, hh * 512:(hh + 1) * 512], ALU.mult,
                 reads=[banks[hh], ("sig", hh)], writes=[m])
        if dbg_m is not None:
            P.dma("sync", dbg_m[tt * 128:(tt + 1) * 128, :], m[:])
        o = xo[tt % 2]
        emit_postnorm_residual(P, C, m, xi, g1[:], g1, o, tt)
        P.dma("sync", xmid_d[tt * 128:(tt + 1) * 128, :], o[:])


def emit_ffn(P, C, sb, layer, src_d, dst_d, win_d, wout_d, conv_d, gains_d, gi_pre, gi_post, wq_eng="sync"):
    L = "f%d_" % layer
    win = sb(L + "win", [128, DC, 2 * FF], BF16)
    wout = sb(L + "wout", [128, FC, D], BF16)
    wiv = win_d.rearrange("(dc p) n -> p dc n", p=128)
    wov = wout_d.rearrange("(fc p) n -> p fc n", p=128)
    HF = FF // 2

    def load_weights():
        n_ = 0
        for cg in range(2):
            for hh in range(2):
                c0 = hh * FF + cg * HF
                P.dma(("sync", "scalar")[n_ % 2], win[:, :, c0:c0 + HF], wiv[:, :, c0:c0 + HF], writes=[(L + "win", hh, cg)])
                n_ += 1
        for hf_ in range(2):
            P.dma(("sync", "scalar")[hf_], wout[:, hf_ * 11:(hf_ + 1) * 11, :], wov[:, hf_ * 11:(hf_ + 1) * 11, :], writes=[(L + "wout", hf_)])
    cv = sb(L + "conv", [128, FC, 4])
    P.dma("sync", cv[:], conv_d)
    gpre = sb(L + "gpre", [128, D])
    gpost = sb(L + "gpost", [128, D])
    P.dma("sync", gpre[:], gains_d[gi_pre:gi_pre + 1, :].partition_broadcast(128))
    P.dma("sync", gpost[:], gains_d[gi_post:gi_post + 1, :].partition_broadcast(128))
    h2Ts = [sb(L + "h2T%d" % i, [128, DC, 512], BF16) for i in range(1)] * 2
    hbx = [C.hb[0], C.hb[1], sb(L + "hb2", [128, D], BF16), sb(L + "hb3", [128, D], BF16)]
    uT = sb(L + "uT", [128, FC, 512], BF16)
    gprev = sb(L + "gprev", [128, FC, 2])
    P.add("vector", lambda e: e.memset(gprev[:], 0.0), [], [(L + "gprev", fc) for fc in range(FC)])
    gbuf = [sb(L + "gbuf%d" % i, [128, 516]) for i in range(1)]
    cva = [sb(L + "cva%d" % i, [128, 512]) for i in range(1)]
    cvb = [sb(L + "cvb%d" % i, [128, 512]) for i in range(1)]
    xr = sb(L + "xr", [128, D])
    ft = [sb(L + "ft%d" % i, [128, D]) for i in range(1)]
    ot = [xr]
    pg = [C.pb[0], C.pb[1]]
    pv = [C.pb[2], C.pb[3]]
    po = [C.pb[4], C.pb[5]]
    NCH = T // 512

    def normA(ch):
        for j in range(4):
            tt = ch * 4 + j
            P.dma("sync", C.xin[j % 2][:], src_d[tt * 128:(tt + 1) * 128, :])
            emit_norm_A(P, C, C.xin[j % 2], gpre[:], gpre, hbx[j], tt)

    def normB(ch):
        for j in range(4):
            emit_norm_B(P, C, hbx[j], h2Ts[ch % 2], j * 128, ch * 4 + j)

    normA(0)
    load_weights()
    normB(0)
    for ch in range(NCH):
        h2T = h2Ts[ch % 2]
        for fc in range(FC):
            g_, v_ = pg[fc % 2], pv[fc % 2]
            for dc in range(DC):
                P.mm(g_[:], win[:, dc, fc * 128:(fc + 1) * 128], h2T[:, dc, :], dc == 0, dc == DC - 1,
                     reads=[(L + "win", 0, fc // 11), h2T])
            for dc in range(DC):
                P.mm(v_[:], win[:, dc, FF + fc * 128:FF + (fc + 1) * 128], h2T[:, dc, :], dc == 0, dc == DC - 1,
                     reads=[(L + "win", 1, fc // 11), h2T])
            gb = gbuf[0]
            P.copy(gb[:, 0:2], gprev[:, fc, :], eng="gpsimd", reads=[(L + "gprev", fc)], writes=[gb])
            P.copy(gb[:, 2:514], g_[:], eng="scalar")
            P.copy(gprev[:, fc, :], gb[:, 512:514], eng="gpsimd", reads=[gb], writes=[(L + "gprev", fc)])
            ca, cb_ = cva[0], cvb[0]
            P.ts(ca[:], gb[:, 0:512], cv[:, fc, 0:1], ALU.mult, cv[:, fc, 3:4], ALU.add)
            P.stt(cb_[:], gb[:, 1:513], cv[:, fc, 1:2], ca[:], ALU.mult, ALU.add)
            P.stt(ca[:], gb[:, 2:514], cv[:, fc, 2:3], cb_[:], ALU.mult, ALU.add)
            P.act(cb_[:], ca[:], AF.Gelu_apprx_tanh)
            P.tt(uT[:, fc, :], cb_[:], v_[:], ALU.mult, writes=[(L + "uT", fc)])
        P.pump(4)
        if ch + 1 < NCH:
            normA(ch + 1)
        for j in range(4):
            tt = ch * 4 + j
            for nch in range(2):
                for fc in range(FC):
                    P.mm(po[nch][:], uT[:, fc, j * 128:(j + 1) * 128], wout[:, fc, nch * 512:(nch + 1) * 512],
                         fc == 0, fc == FC - 1, reads=[(L + "uT", fc), (L + "wout", fc // 11)])
            f = ft[0]
            P.dma("sync", xr[:], src_d[tt * 128:(tt + 1) * 128, :])
            P.copy(f[:, 0:512], po[0][:], eng="scalar", writes=[f])
            P.copy(f[:, 512:1024], po[1][:], eng="scalar", writes=[f])
            o = ot[0]
            emit_postnorm_residual(P, C, f, xr, gpost[:], gpost, o, tt)
            P.dma("sync", dst_d[tt * 128:(tt + 1) * 128, :], o[:])
        if ch + 1 < NCH:
            normB(ch + 1)
NCMP = 255
THETA = 500000.0


def host_consts():
    c = {}
    k = np.arange(T)
    c["ind"] = (np.arange(64)[:, None] == (k[None, :] // 64)).astype(np.float32)
    inv = THETA ** (-np.arange(8, dtype=np.float32) / 8.0)
    ang = k.astype(np.float32)[:, None] * inv[None, :]
    cs = np.stack([np.cos(ang), np.sin(ang)], 0).astype(np.float32)
    c["ropek"] = np.ascontiguousarray(cs.reshape(2, NT, 128, 8).transpose(2, 0, 1, 3))
    pc = (np.arange(256) * 16 + 31).astype(np.float32)
    angc = pc[:, None] * inv[None, :]
    csc = np.stack([np.cos(angc), np.sin(angc)], 0).astype(np.float32)
    c["ropec"] = np.ascontiguousarray(csc.reshape(2, 2, 128, 8).transpose(2, 0, 1, 3))
    n = np.arange(256)
    j = np.arange(64)
    ov = ((n[:, None] * 16 < (j[None, :] + 1) * 64) & (n[:, None] * 16 + 32 > j[None, :] * 64) & (n[:, None] < NCMP))
    c["ovl"] = np.ascontiguousarray(ov.astype(np.float32).reshape(2, 128, 64).transpose(1, 0, 2))
    t = np.arange(T)
    cm = np.where((n[:, None] * 16 + 31 <= t[None, :]) & (n[:, None] < NCMP), 0.0, NEGM).astype(np.float32)
    c["cmask"] = np.ascontiguousarray(cm.reshape(2, 128, NT, 128).transpose(2, 1, 0, 3))
    kp = np.arange(128)
    causal = np.where(kp[:, None] <= kp[None, :], 0.0, NEGM)
    strict = np.where(kp[:, None] > kp[None, :], 0.0, NEGM)
    c["dmask"] = np.ascontiguousarray(np.stack([causal, strict], 1).astype(np.float32))
    cur = t // 64
    valid = j[None, :] <= cur[:, None]
    forced = (j[None, :] == 0) | (j[None, :] == cur[:, None]) | (j[None, :] == cur[:, None] - 1)
    vm = (valid & ~forced).astype(np.float32)
    fb = np.where(forced, 1e4, np.where(valid, 0.0, -1e4)).astype(np.float32)
    sel = np.stack([vm, fb], 1)
    c["selc"] = np.ascontiguousarray(sel.reshape(NT, 128, 2, 64))
    return c


def host_layout_l1(inp, m):
    f = lambda a: np.ascontiguousarray(a, dtype=np.float32)
    m["wkv"] = f(inp["w_kv"])
    m["wq"] = f(inp["b_w_q"][0])
    m["wo"] = f(inp["b_w_o"][0])
    w1 = np.stack([inp["cmp_k_w1"], inp["cmp_v_w1"]], 0).reshape(2, 32, 64, 128).transpose(2, 0, 1, 3)
    m["w1d"] = f(np.concatenate([w1, w1], 0))
    m["w2"] = f(np.stack([inp["cmp_k_w2"], inp["cmp_v_w2"]], 1))
    pe = np.stack([inp["cmp_pe_k"].T, inp["cmp_pe_v"].T], 1)
    m["peT"] = f(pe)
    m.update(host_consts())
    return m


def rope_ops(P, src, dst, cos, sin, ta, tb, nh, skey, dkey):
    cb = cos.unsqueeze(1).to_broadcast([128, nh, 8])
    sb_ = sin.unsqueeze(1).to_broadcast([128, nh, 8])
    x1, x2 = src[:, :, 0:8], src[:, :, 8:16]
    a, b = ta[:, 0:nh, :], tb[:, 0:nh, :]
    P.tt(a, x1, cb, ALU.mult, reads=[skey, "ropetab"], writes=[ta])
    P.tt(b, x2, sb_, ALU.mult, reads=[skey, "ropetab"], writes=[tb])
    P.tt(dst[:, :, 0:8], a, b, ALU.subtract, reads=[ta, tb], writes=[dkey])
    P.tt(a, x2, cb, ALU.mult, reads=[skey, "ropetab"], writes=[ta])
    P.tt(b, x1, sb_, ALU.mult, reads=[skey, "ropetab"], writes=[tb])
    P.tt(dst[:, :, 8:16], a, b, ALU.add, reads=[ta, tb], writes=[dkey])


def emit_kv(P, C, sb, KV, x1_d, D_):
    G = "gpsimd"
    wkv = sb("wkv", [128, DC, 1536], BF16)
    wv = D_["wkv"].rearrange("(dc p) n -> p dc n", p=128)
    for dc in range(DC):
        P.dma(G, wkv[:, dc, :], wv[:, dc, :], writes=[("wkv", dc)])
    gk = sb("kv_g", [128, D])
    P.dma("sync", gk[:], D_["gains"][4:5, :].partition_broadcast(128))
    ropec = sb("ropec", [128, 2, 2, 8])
    P.dma("sync", ropec[:], D_["ropec"], writes=["ropetab"])
    for g in range(4):
        P.dma(G, KV.Kaug[64:128, g, :], D_["ind"], writes=[("Kaug_ind", g)])
    P.dma(G, KV.Ov[:], D_["ovl"])
    P.add("vector", lambda e: e.memset(KV.Vs[:], 1.0), [], ["Vs"])
    P.add("vector", lambda e: e.memset(KV.Vw[:], 1.0), [], ["Vw"])
    P.add("vector", lambda e: e.memset(KV.Vc[:], 1.0), [], ["Vc"])
    P.add("vector", lambda e: e.memset(KV.KcT[:], 0.0), [], ["KcT"])
    zT = sb("kv_zT", [128, 2, 2, 16, T // 16], BF16)
    kfs = [sb("kv_kf%d" % i, [128, 4, 64]) for i in range(2)]
    kbs = [sb("kv_kb%d" % i, [128, 4, 64], BF16) for i in range(2)]
    tas = [sb("kv_ta%d" % i, [128, 4, 8]) for i in range(2)]
    tbs = [sb("kv_tb%d" % i, [128, 4, 8]) for i in range(2)]
    kf, kb, ta, tb = kfs[0], kbs[0], tas[0], tbs[0]
    chunk_scope = contextlib.ExitStack()
    sb_outer = sb
    sb = lambda name, shape, dt=F32: chunk_scope.enter_context(KV.nc.sbuf_tensor("S_" + name, list(shape), dt))
    sT = sb("kv_sT", [128, DC, 512], BF16)
    hbx = [C.hb[0], C.hb[1], sb("kv_hb2", [128, D], BF16), sb("kv_hb3", [128, D], BF16)]
    NCH = T // 512
    pend = []

    def normA(ch):
        for j in range(4):
            tt = ch * 4 + j
            P.dma("sync", C.xin[j % 2][:], x1_d[tt * 128:(tt + 1) * 128, :])
            emit_norm_A(P, C, C.xin[j % 2], gk[:], gk, hbx[j], tt)

    def normB(ch):
        for j in range(4):
            emit_norm_B(P, C, hbx[j], sT, j * 128, ch * 4 + j)

    normA(0)
    normB(0)
    for ch in range(NCH):
        for cc_ in range(4):
            kv, gp = cc_ // 2, cc_ % 2
            pz = C.pb[cc_ % 2]
            for dc in range(DC):
                P.mm(pz[:], wkv[:, dc, cc_ * 128:(cc_ + 1) * 128], sT[:, dc, :], dc == 0, dc == DC - 1, reads=[("wkv", dc), sT])
            P.copy(zT[:, gp, kv, :, ch * 32:(ch + 1) * 32].rearrange("p s n -> p n s"),
                   pz[:].rearrange("p (n s) -> p n s", s=16), eng="scalar", writes=[zT])
        if ch + 1 < NCH:
            normA(ch + 1)
        for j in range(4):
            tt = ch * 4 + j
            for br in range(2):
                pk = C.pb[2 + (2 * j + br) % 2]
                kf, kb, ta, tb = kfs[br], kbs[br], tas[br], tbs[br]
                for dc in range(DC):
                    P.mm(pk[:], sT[:, dc, j * 128:(j + 1) * 128], wkv[:, dc, 512 * (br + 1):512 * (br + 2)], dc == 0, dc == DC - 1,
                         reads=[sT, ("wkv", dc)])
                while pend:
                    pend.pop(0)()
                Vdst = KV.Vs if br == 0 else KV.Vw
                P.copy(Vdst[:, tt, :, 0:64], pk[:, 256:512].rearrange("p (g d) -> p g d", g=4), eng="scalar",
                       writes=["Vs" if br == 0 else "Vw"])
                P.copy(kf[:], pk[:, 0:256].rearrange("p (g d) -> p g d", g=4), eng="scalar")
                P.copy(kb[:], kf[:], eng="gpsimd")
                rope_ops(P, kf, kb, KV.ropek[:, 0, tt, :], KV.ropek[:, 1, tt, :], ta, tb, 4, kf, kb)

                def trans(kb=kb, br=br, tt=tt):
                    pT = C.ptr[0]
                    for g in range(4):
                        P.tr(pT[0:64, g, :], kb[:, g, :], C.identb[:])
                    Kdst = KV.Kaug if br == 0 else KV.Kw
                    P.copy(Kdst[0:64, :, tt * 128:(tt + 1) * 128], pT[0:64, 0:4, :], eng="scalar",
                           writes=["Kaug" if br == 0 else "Kw"])
                pend.append(trans)
        while pend:
            pend.pop(0)()
        if ch + 1 < NCH:
            normB(ch + 1)
    chunk_scope.close()
    P.barrier()
    sb = sb_outer
    w1d = sb("w1d", [128, 2, 32, 128], BF16)
    P.dma(G, w1d[:], D_["w1d"])
    w2 = sb("w2", [128, 2, 64], BF16)
    P.dma(G, w2[:], D_["w2"])
    peT = sb("peT", [64, 2, 32], BF16)
    P.dma(G, peT[:], D_["peT"])
    cbias = sb("kv_cbias", [128, 2])
    hidT = sb("kv_hidT", [128, 256], BF16)
    P.memset(hidT[:], 0.0)
    for kv in range(2):
        for l in range(32):
            P.mm(C.pb[0][:, kv:kv + 1], w1d[0:64, kv, l, :], peT[:, kv, l:l + 1], l == 0, l == 31)
    P.copy(cbias[:], C.pb[0][:, 0:2])
    for kv in range(2):
        for g in range(4):
            kf, kb, ta, tb = kfs[g % 2], kbs[g % 2], tas[g % 2], tbs[g % 2]
            gp, base = g // 2, (g % 2) * 64
            ph = C.pb[(kv * 4 + g) % 2]
            zrow = zT[base:base + 64, gp, kv, :, :]
            for l in range(32):
                rhs = zrow[:, l, 0:NCMP] if l < 16 else zrow[:, l - 16, 1:NCMP + 1]
                P.mm(ph[:, 0:NCMP], w1d[base:base + 64, kv, l, :], rhs, l == 0, l == 31)
            P.act(hidT[:, 0:NCMP], ph[:, 0:NCMP], AF.Gelu_apprx_tanh, bias=cbias[:, kv:kv + 1])
            for nt_ in range(2):
                po = C.pb[2 + nt_]
                P.mm(po[:, 0:64], hidT[:, nt_ * 128:(nt_ + 1) * 128], w2[:, kv, :], True, True)
                if kv == 1:
                    P.copy(KV.Vc[:, nt_, g, 0:64], po[:, 0:64], eng="scalar", writes=["Vc"])
                else:
                    P.copy(kf[:, 0:1, :], po[:, 0:64].unsqueeze(1), eng="scalar", writes=[kf])
                    P.copy(kb[:, 0:1, :], kf[:, 0:1, :], eng="gpsimd", reads=[kf], writes=[kb])
                    rope_ops(P, kf[:, 0:1, :], kb[:, 0:1, :], ropec[:, 0, nt_, :], ropec[:, 1, nt_, :], ta, tb, 1, kf, kb)
                    pT = C.ptr[nt_]
                    P.tr(pT[0:64, 0, :], kb[:, 0, :], C.identb[:])
                    P.copy(KV.KcT[0:64, g, nt_ * 128:(nt_ + 1) * 128], pT[0:64, 0, :], eng="scalar", writes=["KcT"])


def emit_attn(P, C, sb, KV, x1_d, xmid_d, D_, dbg_m2=None, nc=None):
    G = "gpsimd"
    wq = sb("wq", [128, DC, 1072], BF16)
    wo = sb("wo", [128, DC, D], BF16)
    wqv = D_["wq"].rearrange("(dc p) n -> p dc n", p=128)
    wov = D_["wo"].rearrange("(dc p) n -> p dc n", p=128)
    for dc in range(DC):
        P.dma(G, wq[:, dc, :], wqv[:, dc, :])
        P.dma(G, wo[:, dc, :], wov[:, dc, :])
    gpre = sb("at_gpre", [128, D])
    gpost = sb("at_gpost", [128, D])
    P.dma("sync", gpre[:], D_["gains"][5:6, :].partition_broadcast(128))
    P.dma("sync", gpost[:], D_["gains"][6:7, :].partition_broadcast(128))
    dmask = sb("at_dmask", [128, 2, 128], BF16)
    P.dma(G, dmask[:], D_["dmask"])
    cmask = [sb("at_cmask%d" % i, [128, 2, 128], BF16) for i in range(2)]
    cmx = [sb("at_cmx%d" % i, [128, 2, 512], BF16) for i in range(2)]
    dmx = sb("at_dmx", [128, 2, 512], BF16)
    for i_ in range(2):
        P.copy(dmx[:, i_, :].rearrange("p (r t) -> p r t", r=4), dmask[:, i_:i_ + 1, :].to_broadcast([128, 4, 128]),
               reads=[dmask], writes=[dmx])
    selc = [sb("at_selc%d" % i, [128, 2, 64]) for i in range(2)]
    hqT = sb("at_hqT", [128, DC, 128], BF16)
    qb = sb("at_qb", [128, 16, 64], BF16)
    ta = sb("at_ta", [128, 16, 8])
    tb = sb("at_tb", [128, 16, 8])
    gt = [sb("at_gt%d" % i, [128, 16, 3]) for i in range(3)]
    Qaug = [sb("at_Qaug%d" % i, [128, 4, 512], BF16) for i in range(2)]
    Pexp = [sb("at_Pexp%d" % i, [128, 512], BF16) for i in range(3)]
    rdc = sb("at_rdc", [128, 4])
    rdw = sb("at_rdw", [128, 4])
    rds = sb("at_rds", [128, 4])
    imp = sb("at_imp", [128, 64])
    imt = sb("at_imt", [128, 4, 64])
    score = sb("at_score", [128, 64])
    score2 = sb("at_score2", [128, 64])
    m8a = sb("at_m8a", [128, 8])
    m8b = sb("at_m8b", [128, 8])
    thr = sb("at_thr", [128, 1])
    nmt = [sb("at_nmt%d" % i, [128, 128], BF16) for i in range(4)]
    for t_ in nmt:
        P.memset(t_[:], 0.0)
    ocw = [sb("at_ocw%d" % i, [128, 16, 64], BF16) for i in range(2)]
    tmpw = sb("at_tmpw", [128, 4, 64])
    tmps = sb("at_tmps", [128, 4, 64])
    ob = [sb("at_ob%d" % i, [128, 16, 64], BF16) for i in range(2)]
    obT = sb("at_obT", [128, DC, 128], BF16)
    m2 = sb("at_m2", [128, D])
    qf = m2[:].rearrange("p (h d) -> p h d", h=16)
    xres = sb("at_xres", [128, D])
    xo = xres
    S_b = [C.pb[0], C.pb[1], C.pb[6]]
    pc1, pc2, pw1, ps1 = C.pb[2], C.pb[3], C.pb[4], C.pb[5]
    v65 = lambda bank: bank[:, 0:260].rearrange("p (r d) -> p r d", r=4)
    v64 = lambda bank: bank[:, 0:256].rearrange("p (r d) -> p r d", r=4)
    KVK = "kvcache"
    cnt = [0]
    sbc = [0]

    def sbank():
        S = S_b[sbc[0] % 3]
        sbc[0] += 1
        return S
    defer = []
    late = []

    def run_deferred(upto=None):
        fs = [x for x in defer if upto is None or x[0] <= upto]
        rest = [x for x in defer if not (upto is None or x[0] <= upto)]
        del defer[:]
        defer.extend(rest)
        for _, f in fs:
            f()

    def run_late():
        fs = list(late)
        del late[:]
        for f in fs:
            f()

    def unit(qkey, lhsT, rhs, mask_ap, accv, rhs_v, first, last, extra=None):
        i = cnt[0]
        cnt[0] += 1
        S = sbank()
        P.mm(S[:], lhsT, rhs, True, mask_ap is None, reads=[KVK] + (list(qkey) if isinstance(qkey, list) else [qkey]))
        if mask_ap is not None:
            P.mm(S[:], C.identb[:], mask_ap, False, True)
        pe = Pexp[i % 3]
        P.act(pe[:], S[:], AF.Exp)
        run_deferred(i - 2)

        def pv():
            for r in range(4):
                P.mm(accv[0][:, r, :], pe[:, r * 128:(r + 1) * 128], rhs_v, first and r == 0, last, reads=[pe, KVK], writes=[accv[1]])
                if extra is not None:
                    P.mm(extra[0][:, r, :], pe[:, r * 128:(r + 1) * 128], extra[2], first and r == 0, last, reads=[pe, KVK], writes=[extra[1]])
        defer.append((i, pv))

    def recip_den(dst, bank):
        P.ts(dst[:], v65(bank)[:, :, 64], 1e-30, ALU.max, reads=[bank], writes=[dst])
        P.add("vector", lambda e: e.reciprocal(out=dst[:], in_=dst[:]), [dst], [dst])

    def pA(qt):
        b = qt % 2
        xi = C.xin[b]
        P.dma("sync", xi[:], x1_d[qt * 128:(qt + 1) * 128, :])
        cm, sc, cx = cmask[b], selc[b], cmx[b]
        P.dma(G, cm[:], D_["cmask"][qt])
        P.dma("sync", sc[:], D_["selc"][qt])
        for i_ in range(2):
            P.copy(cx[:, i_, :].rearrange("p (r t) -> p r t", r=4), cm[:, i_:i_ + 1, :].to_broadcast([128, 4, 128]),
                   eng="gpsimd", reads=[cm], writes=[cx])
        emit_norm_A(P, C, xi, gpre[:], gpre, C.hb[b], qt)

    def pB(qt):
        b = qt % 2
        emit_norm_B(P, C, C.hb[b], hqT, 0, qt)
        for nch in range(2):
            S = sbank()
            for dc in range(DC):
                P.mm(S[:], hqT[:, dc, :], wq[:, dc, nch * 512:(nch + 1) * 512], dc == 0, dc == DC - 1)
            P.act(m2[:, nch * 512:(nch + 1) * 512], S[:], AF.Copy, scale=0.125, writes=[m2])

    def pB2(qt):
        S = sbank()
        for dc in range(DC):
            P.mm(S[:, 0:48], hqT[:, dc, :], wq[:, dc, 1024:1072], dc == 0, dc == DC - 1)
        P.act(gt[qt % 3][:].rearrange("p h c -> p (h c)"), S[:, 0:48], AF.Sigmoid)

    def pD(qt):
        P.copy(qb[:], qf, eng="gpsimd", reads=[m2], writes=[qb])
        rope_ops(P, qf, qb, KV.ropek[:, 0, qt, :], KV.ropek[:, 1, qt, :], ta, tb, 16, m2, qb)

    def pE(qt):
        b = qt % 2
        for hh in range(2):
            pT = C.ptr[hh]
            for h8 in range(8):
                P.tr(pT[0:64, h8, :], qb[:, hh * 8 + h8, :], C.identb[:])
            P.copy(Qaug[b][0:64, hh * 2:(hh + 1) * 2, :].rearrange("p g (r t) -> p (g r) t", r=4), pT[0:64, :, :], eng="scalar",
                   writes=[("Qq", b, hh * 2), ("Qq", b, hh * 2 + 1)])

    def prologue(qt):
        pA(qt)
        pB(qt)
        pB2(qt)
        pD(qt)
        pE(qt)

    def a_stage(qt, g):
        b = qt % 2
        gs = slice(g * 4, (g + 1) * 4)
        Q = Qaug[b]
        cx, sc = cmx[b], selc[b]
        for nt_ in range(2):
            unit(("Qq", b, g), KV.KcT[0:64, g, nt_ * 128:(nt_ + 1) * 128], Q[0:64, g, :], cx[:, nt_, :], (v65(pc1), pc1),
                 KV.Vc[:, nt_, g, 0:65], nt_ == 0, nt_ == 1, extra=(v64(pc2), pc2, KV.Ov[:, nt_, :]))

        def evac_cmp():
            recip_den(rdc, pc1)
            rb = rdc[:, :].unsqueeze(2).to_broadcast([128, 4, 64])
            P.tt(imt[:], v64(pc2), rb, ALU.mult, reads=[pc2, rdc], writes=[imt])
            P.add("vector", lambda e: e.tensor_reduce(out=imp[:], in_=imt[:].rearrange("p r j -> p j r"), axis=AX.X, op=ALU.add), [imt], [imp])
            P.tt(rdc[:], rdc[:], gt[qt % 3][:, gs, 0], ALU.mult)
            P.tt(ocw[b][:, gs, :], v65(pc1)[:, :, 0:64], rb, ALU.mult, reads=[pc1, rdc], writes=[("ocw", b, g)])
            P.tt(score[:], imp[:], sc[:, 0, :], ALU.mult)
            P.tt(score[:], score[:], sc[:, 1, :], ALU.add)
            P.add("vector", lambda e: e.max(out=m8a[:], in_=score[:]), [score], [m8a])
            P.add("vector", lambda e: e.match_replace(out=score2[:], in_to_replace=m8a[:], in_values=score[:], imm_value=-1e9), [m8a, score], [score2])
            P.add("vector", lambda e: e.max(out=m8b[:], in_=score2[:]), [score2], [m8b])
            P.ts(thr[:], m8b[:, 7:8], -5000.0, ALU.max)
            P.ts(score2[:], score[:], thr[:, 0:1], ALU.is_ge)
            nm = nmt[g]
            P.ts(nm[:, 64:128], score2[:], -1.0, ALU.add, -NEGM, ALU.mult)

            def neg_rows():
                pT = C.ptr[g % 2]
                P.tr(pT[:, 0, :], nm[:], C.identb[:])
                P.copy(Q[64:128, g, :].rearrange("p (r t) -> p r t", r=4), pT[64:128, 0:1, :].to_broadcast([64, 4, 128]), eng="scalar",
                       reads=[pT], writes=[("Qn", b, g)])
            late.append(neg_rows)
        defer.append((cnt[0] - 1, evac_cmp))
        kts = [k_ for k_ in range(qt - 4, qt + 1) if k_ >= 0]
        for kt in kts:
            mk = None
            if kt == qt:
                mk = dmx[:, 0, :]
            elif kt == qt - 4:
                mk = dmx[:, 1, :]
            unit(("Qq", b, g), KV.Kw[0:64, g, kt * 128:(kt + 1) * 128], Q[0:64, g, :], mk,
                 (v65(pw1), pw1), KV.Vw[:, kt, g, :], kt == kts[0], kt == kts[-1])

        def evac_win():
            recip_den(rdw, pw1)
            P.tt(rdw[:], rdw[:], gt[qt % 3][:, gs, 2], ALU.mult)
            P.tt(tmpw[:], v65(pw1)[:, :, 0:64], rdw[:, :].unsqueeze(2).to_broadcast([128, 4, 64]), ALU.mult, reads=[pw1, rdw], writes=[tmpw])
            P.tt(ocw[b][:, gs, :], ocw[b][:, gs, :], tmpw[:], ALU.add, eng="gpsimd", reads=[("ocw", b, g), tmpw], writes=[("ocw", b, g)])
        defer.append((cnt[0] - 1, evac_win))

    def b_stage(qt, g):
        b = qt % 2
        gs = slice(g * 4, (g + 1) * 4)
        Q = Qaug[b]
        for kt in range(qt + 1):
            unit([("Qn", b, g), ("Qq", b, g)], KV.Kaug[:, g, kt * 128:(kt + 1) * 128], Q[:, g, :], dmx[:, 0, :] if kt == qt else None,
                 (v65(ps1), ps1), KV.Vs[:, kt, g, :], kt == 0, kt == qt)

        def evac_sel():
            recip_den(rds, ps1)
            P.tt(rds[:], rds[:], gt[qt % 3][:, gs, 1], ALU.mult)
            P.tt(tmps[:], v65(ps1)[:, :, 0:64], rds[:, :].unsqueeze(2).to_broadcast([128, 4, 64]), ALU.mult, reads=[ps1, rds], writes=[tmps])
            P.tt(ob[b][:, gs, :], tmps[:], ocw[b][:, gs, :], ALU.add, eng="gpsimd", reads=[tmps, ("ocw", b, g)], writes=[("ob", b, g)])
        defer.append((cnt[0] - 1, evac_sel))

    def epilogue(qt):
        b = qt % 2
        P.dma("sync", xres[:], x1_d[qt * 128:(qt + 1) * 128, :])
        pT = C.ptr[0]
        for dc in range(DC):
            P.tr(pT[:, dc, :], ob[b][:].rearrange("p h d -> p (h d)")[:, dc * 128:(dc + 1) * 128], C.identb[:],
                 reads=[("ob", b, dc // 2), C.identb])
        P.copy(obT[:], pT[:], eng="scalar")
        for nch in range(2):
            for dc in range(DC):
                P.mm(C.pb[2 + nch][:], obT[:, dc, :], wo[:, dc, nch * 512:(nch + 1) * 512], dc == 0, dc == DC - 1)
            P.copy(m2[:, nch * 512:(nch + 1) * 512], C.pb[2 + nch][:], eng="scalar", writes=[m2])
        if dbg_m2 is not None:
            P.dma("sync", dbg_m2[qt * 128:(qt + 1) * 128, :], m2[:])
        emit_postnorm_residual(P, C, m2, xres, gpost[:], gpost, xo, qt)
        P.dma("sync", xmid_d[qt * 128:(qt + 1) * 128, :], xo[:])

    prologue(0)
    for g in range(4):
        a_stage(0, g)
    run_deferred()
    run_late()
    if NT > 1:
        prologue(1)
    for qt in range(NT):
        nxt = qt + 2
        for g in range(4):
            if qt + 1 < NT:
                a_stage(qt + 1, g)
            b_stage(qt, g)
            run_late()
            if g == 0 and qt > 0:
                epilogue(qt - 1)
            if nxt < NT:
                if g == 0:
                    pA(nxt)
                elif g == 1:
                    pB(nxt)
                elif g == 2:
                    pB2(nxt)
                    pD(nxt)
        if nxt < NT:
            pE(nxt)
    run_deferred()
    run_late()
    epilogue(NT - 1)
def build(stage="full", debug=False):
    nc = bass.Bass("TRN2", target_bir_lowering=False)
    P = Prog(nc)

    def din(name, shape, dt=F32):
        return nc.dram_tensor(name, list(shape), dt, kind="ExternalInput").ap()

    def dout(name, shape, dt=F32):
        return nc.dram_tensor(name, list(shape), dt, kind="ExternalOutput").ap()

    def dscr(name, shape, dt=F32):
        return nc.dram_tensor(name, list(shape), dt, kind="Internal").ap()

    x_d = din("x", [T, D])
    s5par_d = din("s5par", [128, 3, 32])
    s5b_d = din("s5b", [128, 2, 32, 32])
    s5c_d = din("s5c", [128, 2, 32, 32])
    s5d_d = din("s5d", [128, 8])
    wglu_d = din("wglu", [D, 2 * D])
    gains_d = din("gains", [9, D])
    win_d = [din("win%d" % l, [D, 2 * FF]) for l in range(2)]
    wout_d = [din("wout%d" % l, [FF, D]) for l in range(2)]
    conv_d = [din("conv%d" % l, [128, FC, 4]) for l in range(2)]
    ident_d = din("ident", [128, 128])
    D_ = {"gains": gains_d}
    for nm, shp in (("wkv", [D, 1536]), ("wq", [D, 1072]), ("wo", [D, D]), ("w1d", [128, 2, 32, 128]), ("w2", [128, 2, 64]),
                    ("peT", [64, 2, 32]), ("ind", [64, T]), ("ropek", [128, 2, NT, 8]), ("ropec", [128, 2, 2, 8]),
                    ("ovl", [128, 2, 64]), ("cmask", [NT, 128, 2, 128]), ("dmask", [128, 2, 128]), ("selc", [NT, 128, 2, 64])):
        D_[nm] = din(nm, shp)

    out_d = dout("out", [T, D])
    xmid_d = dscr("xmid", [T, D])
    x1_d = out_d if stage == "l0" else dscr("x1", [T, D])
    dbg_m = dout("dbg_m", [T, D]) if debug else None
    dbg_xmid = dout("dbg_xmid", [T, D]) if debug else None
    dbg_m2 = dout("dbg_m2", [T, D]) if (debug and stage != "l0") else None
    dbg_x1 = dout("dbg_x1", [T, D]) if (debug and stage != "l0") else None

    winb_d = [dscr("winb%d" % l, [D, 2 * FF], BF16) for l in range(2)]
    woutb_d = [dscr("woutb%d" % l, [FF, D], BF16) for l in range(2)]
    for l in range(2):
        P.bg.add("winb%d" % l)
        P.bg.add("woutb%d" % l)
    def precast(l):
        for r0 in range(0, D, 128):
            P.bgq.append(lambda r0=r0: P.dma("gpsimd", winb_d[l][r0:r0 + 128, :], win_d[l][r0:r0 + 128, :]))
        for r0 in range(0, FF, 128):
            P.bgq.append(lambda r0=r0: P.dma("gpsimd", woutb_d[l][r0:r0 + 128, :], wout_d[l][r0:r0 + 128, :]))
    precast(0)
    with contextlib.ExitStack() as glob:
        def gsb(name, shape, dt=F32):
            return glob.enter_context(nc.sbuf_tensor("S_" + name, list(shape), dt))

        def gps(name, shape, dt=F32):
            return glob.enter_context(nc.psum_tensor("P_" + name, list(shape), dt))

        C = Ctx()
        identf = gsb("identf", [128, 128])
        C.identb = gsb("identb", [128, 128], BF16)
        P.dma("sync", identf[:], ident_d)
        P.copy(C.identb[:], identf[:])
        C.epsb = gsb("epsb", [128, 1])
        P.memset(C.epsb[:], EPS)
        C.halfpi = gsb("halfpi", [128, 1])
        P.memset(C.halfpi[:], math.pi / 2)
        C.pb = [gps("pb%d" % i, [128, 512]) for i in range(7)]
        _ptr = gps("ptr0", [128, 8, 128], BF16)
        C.ptr = [_ptr, _ptr]
        C.xin = [gsb("xin%d" % i, [128, D]) for i in range(2)]
        C.sq = gsb("sqjunk", [128, D])
        C.stat = [gsb("stat%d" % i, [128, 4]) for i in range(2)]
        C.hb = [gsb("hb%d" % i, [128, D], BF16) for i in range(2)]

        with contextlib.ExitStack() as ph1:
            sb1 = lambda name, shape, dt=F32: ph1.enter_context(nc.sbuf_tensor("S_" + name, list(shape), dt))
            yT = sb1("yT", [128, DC, T], BF16)
            with contextlib.ExitStack() as ph1a:
                sba = lambda name, shape, dt=F32: ph1a.enter_context(nc.sbuf_tensor("S_" + name, list(shape), dt))
                emit_s5(P, C, nc, (sba, None), x_d, s5par_d, s5b_d, s5c_d, s5d_d, gains_d, yT)
            P.barrier()
            with contextlib.ExitStack() as ph1b:
                sbb = lambda name, shape, dt=F32: ph1b.enter_context(nc.sbuf_tensor("S_" + name, list(shape), dt))
                emit_glu(P, C, sbb, x_d, wglu_d, gains_d, yT, xmid_d, dbg_m)
            P.barrier()
        if debug:
            with contextlib.ExitStack() as phd:
                t_ = phd.enter_context(nc.sbuf_tensor("dbgt", [128, D], F32))
                for tt in range(NT):
                    P.dma("sync", t_[:], xmid_d[tt * 128:(tt + 1) * 128, :], reads=["xmid_all"])
                    P.dma("sync", dbg_xmid[tt * 128:(tt + 1) * 128, :], t_[:])
            P.barrier()
        with contextlib.ExitStack() as ph2:
            sb2 = lambda name, shape, dt=F32: ph2.enter_context(nc.sbuf_tensor("S_" + name, list(shape), dt))
            P.pump()
            precast(1)
            emit_ffn(P, C, sb2, 0, xmid_d, x1_d, winb_d[0], woutb_d[0], conv_d[0], gains_d, 2, 3, wq_eng="scalar")
        P.barrier()
        if stage != "l0":
            with contextlib.ExitStack() as phk:
                sbk = lambda name, shape, dt=F32: phk.enter_context(nc.sbuf_tensor("S_" + name, list(shape), dt))
                KV = Ctx()
                KV.nc = nc
                KV.Kaug = sbk("Kaug", [128, 4, T], BF16)
                KV.Kw = sbk("Kw", [64, 4, T], BF16)
                KV.Vs = sbk("Vs", [128, NT, 4, 65], BF16)
                KV.Vw = sbk("Vw", [128, NT, 4, 65], BF16)
                KV.KcT = sbk("KcT", [64, 4, 256], BF16)
                KV.Vc = sbk("Vc", [128, 2, 4, 65], BF16)
                KV.Ov = sbk("Ov", [128, 2, 64], BF16)
                KV.ropek = sbk("ropek", [128, 2, NT, 8])
                P.dma("sync", KV.ropek[:], D_["ropek"], writes=["ropetab"])
                with contextlib.ExitStack() as phk1:
                    sbk1 = lambda name, shape, dt=F32: phk1.enter_context(nc.sbuf_tensor("S_" + name, list(shape), dt))
                    emit_kv(P, C, sbk1, KV, x1_d, D_)
                P.barrier()
                if debug:
                    for nm, t_ in (("Kaug", KV.Kaug), ("Kw", KV.Kw), ("Vs", KV.Vs), ("Vw", KV.Vw), ("KcT", KV.KcT), ("Vc", KV.Vc)):
                        dd = nc.dram_tensor("dbg_" + nm, list(t_.shape), BF16, kind="ExternalOutput").ap()
                        P.dma("sync", dd, t_[:])
                    P.barrier()
                with contextlib.ExitStack() as phk2:
                    sbk2 = lambda name, shape, dt=F32: phk2.enter_context(nc.sbuf_tensor("S_" + name, list(shape), dt))
                    emit_attn(P, C, sbk2, KV, x1_d, xmid_d, D_, dbg_m2, nc=nc)
                P.barrier()
            if debug:
                with contextlib.ExitStack() as phd:
                    t_ = phd.enter_context(nc.sbuf_tensor("S_dbgt2", [128, D], F32))
                    for tt in range(NT):
                        P.dma("sync", t_[:], x1_d[tt * 128:(tt + 1) * 128, :])
                        P.dma("sync", dbg_x1[tt * 128:(tt + 1) * 128, :], t_[:])
                P.barrier()
            P.pump()
            with contextlib.ExitStack() as ph3:
                sb3 = lambda name, shape, dt=F32: ph3.enter_context(nc.sbuf_tensor("S_" + name, list(shape), dt))
                emit_ffn(P, C, sb3, 1, xmid_d, out_d, winb_d[1], woutb_d[1], conv_d[1], gains_d, 7, 8, wq_eng="scalar")
            P.barrier()
        P.emit()
    return nc, P


_NC_CACHE = {}


def kernel(**inputs):
    if "nc" not in _NC_CACHE:
        _NC_CACHE["nc"] = build("full")[0]
    nc = _NC_CACHE["nc"]
    in_maps = [host_layout_l1(inputs, host_layout(inputs, c // 2)) for c in range(8)]
    res = run_bass_kernel_spmd(nc, in_maps, core_ids=list(range(8)))
    out = np.stack([res.results[2 * b]["out"] for b in range(4)], axis=0)
    return out.astype(np.float32)
```

```python
import contextlib
import math
import numpy as np
import concourse.bass as bass
import concourse.mybir as mybir
from concourse.bass_utils import run_bass_kernel_spmd

F32 = mybir.dt.float32
BF16 = mybir.dt.bfloat16
I32 = mybir.dt.int32
AF = mybir.ActivationFunctionType
ALU = mybir.AluOpType
AX = mybir.AxisListType

T = 4096
D = 1024
NT = T // 128
DC = 8
FF = 2816
FC = FF // 128
EPS = 1e-6
NEGM = -30000.0

ENGS = ("tensor", "vector", "scalar", "gpsimd", "sync")


class Op:
    __slots__ = ("eng", "fn", "reads", "writes", "dma", "waits", "sig", "idx")

    def __init__(self, eng, fn, reads, writes, dma):
        self.eng, self.fn, self.reads, self.writes, self.dma = eng, fn, reads, writes, dma
        self.waits = {}
        self.sig = None
        self.idx = None


def _key(t):
    if isinstance(t, (str, tuple)):
        return t
    return t.name


class Prog:
    def __init__(self, nc):
        self.nc = nc
        self.ops = []
        self.bg = set()
        self.bgq = []

    def pump(self, n=None):
        k = len(self.bgq) if n is None else min(n, len(self.bgq))
        for _ in range(k):
            self.bgq.pop(0)()

    def add(self, eng, fn, reads=(), writes=(), dma=False):
        op = Op(eng, fn, [_key(r) for r in reads if r is not None],
                [_key(w) for w in writes if w is not None], dma)
        op.idx = len(self.ops)
        self.ops.append(op)
        return op

    def barrier(self):
        op = Op("barrier", None, [], [], False)
        op.idx = len(self.ops)
        self.ops.append(op)
        return op

    def dma(self, eng, out, in_, reads=None, writes=None):
        r = [in_] if reads is None else reads
        w = [out] if writes is None else writes
        return self.add(eng, lambda e: e.dma_start(out=out, in_=in_), r, w, dma=True)

    def act(self, out, in_, func, bias=None, scale=None, accum_out=None, eng="scalar", reads=None, writes=None):
        kw = {}
        rd = [in_]
        if bias is not None:
            kw["bias"] = bias
            if not isinstance(bias, (int, float)):
                rd.append(bias)
        if scale is not None:
            kw["scale"] = scale
            if not isinstance(scale, (int, float)):
                rd.append(scale)
        wr = [out]
        if accum_out is not None:
            kw["accum_out"] = accum_out
            wr.append(accum_out)
        return self.add(eng, lambda e: e.activation(out=out, in_=in_, func=func, **kw),
                        rd if reads is None else reads, wr if writes is None else writes)

    def tt(self, out, in0, in1, op, eng="vector", reads=None, writes=None):
        return self.add(eng, lambda e: e.tensor_tensor(out=out, in0=in0, in1=in1, op=op),
                        [in0, in1] if reads is None else reads, [out] if writes is None else writes)

    def ts(self, out, in0, s1, op0, s2=None, op1=None, eng="vector", reads=None, writes=None):
        rd = [in0]
        for s in (s1, s2):
            if s is not None and not isinstance(s, (int, float)):
                rd.append(s)
        if op1 is None:
            f = lambda e: e.tensor_scalar(out=out, in0=in0, scalar1=s1, scalar2=None, op0=op0)
        else:
            f = lambda e: e.tensor_scalar(out=out, in0=in0, scalar1=s1, scalar2=s2, op0=op0, op1=op1)
        return self.add(eng, f, rd if reads is None else reads, [out] if writes is None else writes)

    def stt(self, out, in0, scalar, in1, op0, op1, reads=None, writes=None):
        rd = [in0, in1]
        if not isinstance(scalar, (int, float)):
            rd.append(scalar)
        return self.add("vector", lambda e: e.scalar_tensor_tensor(out=out, in0=in0, scalar=scalar, in1=in1, op0=op0, op1=op1),
                        rd if reads is None else reads, [out] if writes is None else writes)

    def copy(self, out, in_, eng="vector", reads=None, writes=None):
        if eng == "scalar":
            return self.add(eng, lambda e: e.activation(out=out, in_=in_, func=AF.Copy), [in_] if reads is None else reads,
                            [out] if writes is None else writes)
        return self.add(eng, lambda e: e.tensor_copy(out=out, in_=in_), [in_] if reads is None else reads,
                        [out] if writes is None else writes)

    def memset(self, ap, val, eng="vector"):
        return self.add(eng, lambda e: e.memset(ap, val), [], [ap])

    def mm(self, out, lhsT, rhs, start, stop, reads=None, writes=None):
        return self.add("tensor", lambda e: e.matmul(out, lhsT=lhsT, rhs=rhs, start=start, stop=stop, skip_group_check=True),
                        [lhsT, rhs] if reads is None else reads, [out] if writes is None else writes)

    def tr(self, out, in_, ident, reads=None, writes=None):
        return self.add("tensor", lambda e: e.transpose(out, in_, ident),
                        [in_, ident] if reads is None else reads, [out] if writes is None else writes)

    def emit(self):
        nc, ops = self.nc, self.ops
        last_w, readers = {}, {}
        deps = [set() for _ in ops]
        for op in ops:
            ds = deps[op.idx]
            if op.eng == "barrier":
                last_w = {k_: v_ for k_, v_ in last_w.items() if k_ in self.bg}
                readers = {}
                continue
            for b in op.reads:
                if b in last_w:
                    ds.add(last_w[b])
            for b in op.writes:
                if b in last_w:
                    ds.add(last_w[b])
                for r in readers.get(b, ()):
                    ds.add(r)
            for b in op.reads:
                readers.setdefault(b, []).append(op.idx)
            for b in op.writes:
                last_w[b] = op.idx
                readers[b] = []
            ds.discard(op.idx)

        def needs(p, c):
            if p.dma:
                return True
            if p.eng != c.eng:
                return True
            return p.eng != "tensor"

        need_sig = set()
        for op in ops:
            for d in deps[op.idx]:
                if needs(ops[d], op):
                    need_sig.add(d)
        last_on = {}
        for op in ops:
            if op.eng == "barrier":
                need_sig.update(last_on.values())
                continue
            if op.dma:
                need_sig.add(op.idx)
            else:
                last_on[op.eng] = op.idx
        eng_cnt = {e: 0 for e in ENGS}
        dma_cnt, dma_keys, sigval = {}, [], {}
        bar_snap = {}
        for op in ops:
            if op.eng == "barrier":
                snap = {("eng", e): v for e, v in eng_cnt.items() if v > 0}
                snap.update({k_: v_ for k_, v_ in dma_cnt.items() if k_[1] not in self.bg})
                bar_snap[op.idx] = snap
                continue
            if op.idx not in need_sig:
                continue
            if op.dma:
                k = ("dma", op.writes[0] if op.writes else op.reads[0])
                if k not in dma_cnt:
                    dma_cnt[k] = 0
                    dma_keys.append(k)
                dma_cnt[k] += 16
                op.sig = (k, 16)
                sigval[op.idx] = (k, dma_cnt[k])
            else:
                k = ("eng", op.eng)
                eng_cnt[op.eng] += 1
                op.sig = (k, 1)
                sigval[op.idx] = (k, eng_cnt[op.eng])
        seen = {e: {} for e in ENGS}
        pend = {e: {} for e in ENGS}
        for op in ops:
            if op.eng == "barrier":
                for e in ENGS:
                    for k, v in bar_snap[op.idx].items():
                        if pend[e].get(k, 0) < v:
                            pend[e][k] = v
                continue
            if pend[op.eng]:
                for k, v in pend[op.eng].items():
                    if seen[op.eng].get(k, 0) < v and op.waits.get(k, 0) < v:
                        op.waits[k] = v
                pend[op.eng] = {}
            for d in deps[op.idx]:
                if d not in sigval or not needs(ops[d], op):
                    continue
                k, v = sigval[d]
                if seen[op.eng].get(k, 0) >= v:
                    continue
                if op.waits.get(k, 0) < v:
                    op.waits[k] = v
            for k, v in op.waits.items():
                seen[op.eng][k] = v
        self.stats = dict(n_ops=len(ops), n_dma_sems=len(dma_keys), eng_cnt=eng_cnt)
        with contextlib.ExitStack() as st:
            sems = {}
            for e in ENGS:
                sems[("eng", e)] = st.enter_context(nc.semaphore("s_" + e))
            for i, k in enumerate(dma_keys):
                sems[k] = st.enter_context(nc.semaphore("d%d" % i))
            block = st.enter_context(nc.Block())
            by_eng = {e: [o for o in ops if o.eng == e] for e in ENGS}
            self.stats["per_eng"] = {e: len(v) for e, v in by_eng.items()}

            def make(e):
                def body(eng):
                    for op in by_eng[e]:
                        for k, v in op.waits.items():
                            eng.wait_ge(sems[k], v)
                        ins = op.fn(eng)
                        if op.sig is not None:
                            ins.then_inc(sems[op.sig[0]], op.sig[1])
                    if e == "sync":
                        for k, v in dma_cnt.items():
                            eng.wait_ge(sems[k], v)
                        for e2 in ENGS:
                            if e2 != "sync" and eng_cnt[e2] > 0:
                                eng.wait_ge(sems[("eng", e2)], eng_cnt[e2])
                return body

            for e in ENGS:
                getattr(block, e)(make(e))
        return self


def _st_layout(a):
    return np.ascontiguousarray(a.reshape(32, 2, 64).transpose(1, 2, 0).reshape(128, 32))


def _blk_layout(a):
    out = np.zeros((128, 32, 32), np.float32)
    a = a.reshape(32, 2, 64, 16)
    for gl in range(2):
        out[gl * 64:(gl + 1) * 64, :, gl * 16:(gl + 1) * 16] = a[:, gl].transpose(1, 0, 2)
    return out


def host_layout(inp, b):
    f = lambda a: np.ascontiguousarray(a, dtype=np.float32)
    m = {}
    m["x"] = f(inp["x"][b])
    lam = np.stack([_st_layout(f(inp["a_lam_re"][0])), _st_layout(f(inp["a_lam_im"][0])),
                    _st_layout(np.repeat(f(inp["a_log_dt"][0])[:, None], 64, axis=1))], axis=1)
    m["s5par"] = f(lam)
    m["s5b"] = f(np.stack([_blk_layout(f(inp["a_b_re"][0])), _blk_layout(f(inp["a_b_im"][0]))], axis=1))
    m["s5c"] = f(np.stack([_blk_layout(f(inp["a_c_re"][0]).transpose(0, 2, 1)),
                           _blk_layout(f(inp["a_c_im"][0]).transpose(0, 2, 1))], axis=1))
    m["s5d"] = f(inp["a_d"][0].reshape(8, 128).T)
    m["wglu"] = f(inp["a_w_glu"][0])
    gains = np.stack([inp["mix_pre_g"][0], inp["mix_post_g"][0], inp["ffn_pre_g"][0], inp["ffn_post_g"][0],
                      inp["kv_norm_g"], inp["mix_pre_g"][1], inp["mix_post_g"][1], inp["ffn_pre_g"][1],
                      inp["ffn_post_g"][1]], axis=0)
    m["gains"] = f(gains)
    for l in range(2):
        m["win%d" % l] = f(inp["ffn_w_in"][l])
        m["wout%d" % l] = f(inp["ffn_w_out"][l])
        cw = inp["ffn_conv_w"][l].reshape(3, FC, 128).transpose(2, 1, 0)
        cb = inp["ffn_conv_b"][l].reshape(FC, 128).T[:, :, None]
        m["conv%d" % l] = f(np.concatenate([cw, cb], axis=2))
    m["ident"] = np.eye(128, dtype=np.float32)
    return m


class Ctx:
    pass


def strided(ap2d, k, step=8):
    return ap2d.rearrange("p (c k) -> p k c", k=step)[:, k, :]


def emit_norm_A(P, C, src_tile, gain_ap, gain_key, h, idx):
    s = C.stat[idx % 2]
    P.act(C.sq[:], src_tile[:], AF.Square, accum_out=s[:, 0:1])
    P.act(s[:, 1:2], s[:, 0:1], AF.Sqrt, bias=C.epsb[:, 0:1], scale=1.0 / D)
    P.add("vector", lambda e: e.reciprocal(out=s[:, 2:3], in_=s[:, 1:2]), [s], [s])
    P.stt(h[:], src_tile[:], s[:, 2:3], gain_ap, ALU.mult, ALU.mult, reads=[src_tile, s, gain_key])


def emit_norm_B(P, C, h, dstT, col0, idx, ceng="scalar"):
    pT = C.ptr[idx % 2]
    for dc in range(DC):
        P.tr(pT[:, dc, :], h[:, dc * 128:(dc + 1) * 128], C.identb[:])
    P.copy(dstT[:, :, col0:col0 + 128], pT[:], eng=ceng)


def emit_norm_to_T(P, C, src_tile, gain_ap, gain_key, dstT, col0, idx, ceng="scalar"):
    s = C.stat[idx % 2]
    P.act(C.sq[:], src_tile[:], AF.Square, accum_out=s[:, 0:1])
    P.act(s[:, 1:2], s[:, 0:1], AF.Sqrt, bias=C.epsb[:, 0:1], scale=1.0 / D)
    P.add("vector", lambda e: e.reciprocal(out=s[:, 2:3], in_=s[:, 1:2]), [s], [s])
    h = C.hb[idx % 2]
    P.stt(h[:], src_tile[:], s[:, 2:3], gain_ap, ALU.mult, ALU.mult, reads=[src_tile, s, gain_key])
    pT = C.ptr[idx % 2]
    for dc in range(DC):
        P.tr(pT[:, dc, :], h[:, dc * 128:(dc + 1) * 128], C.identb[:])
    P.copy(dstT[:, :, col0:col0 + 128], pT[:], eng=ceng)


def emit_postnorm_residual(P, C, f_tile, res_tile, gain_ap, gain_key, out_tile, idx):
    s = C.stat[idx % 2]
    P.act(C.sq[:], f_tile[:], AF.Square, accum_out=s[:, 0:1])
    P.act(s[:, 1:2], s[:, 0:1], AF.Sqrt, bias=C.epsb[:, 0:1], scale=1.0 / D)
    P.add("vector", lambda e: e.reciprocal(out=s[:, 2:3], in_=s[:, 1:2]), [s], [s])
    P.stt(f_tile[:], f_tile[:], s[:, 2:3], gain_ap, ALU.mult, ALU.mult, reads=[f_tile, s, gain_key])
    P.tt(out_tile[:], f_tile[:], res_tile[:], ALU.add, eng="gpsimd")


def emit_s5(P, C, nc, st_alloc, x_d, s5par_d, s5b_d, s5c_d, s5d_d, gains_d, yT):
    sb, ps = st_alloc
    V, G = "vector", "gpsimd"
    par = sb("s5par", [128, 3, 32])
    P.dma("sync", par[:], s5par_d)
    pp = sb("s5pp", [128, 24, 32])
    ki = sb("s5ki", [128, 32], I32)
    pw = sb("s5pw", [128, 2, 9, 32])
    cF = sb("s5cF", [128, 2, 8, 32])
    zt = sb("s5zt", [128, 2, 9, 32])
    RR = sb("s5RR", [128, 32])
    dsk = sb("s5dsk", [128, 8])
    P.dma("sync", dsk[:], s5d_d)
    LR, LI, LDT = par[:, 0, :], par[:, 1, :], par[:, 2, :]
    sl = lambda i: pp[:, i, :]
    (DT, LRDT, TH, MAG, Q, KF, THR, SH, CH, SIN, COS, ABR, ABI, NR, DEN, RDEN, T1, T2, COEFR, COEFI, INVR) = [sl(i) for i in range(21)]
    P.act(DT, LDT, AF.Exp)
    P.tt(LRDT, LR, DT, ALU.mult)
    P.tt(TH, LI, DT, ALU.mult)
    P.act(MAG, LRDT, AF.Exp)
    P.ts(Q, TH, 1.0 / (2 * math.pi), ALU.mult)
    P.copy(ki[:], Q)
    P.copy(KF, ki[:])
    P.stt(THR, KF, -2 * math.pi, TH, ALU.mult, ALU.add)
    P.act(SH, THR, AF.Sin, scale=0.5)
    P.act(CH, THR, AF.Sin, scale=-0.5, bias=C.halfpi[:, 0:1])
    P.stt(SIN, SH, 2.0, CH, ALU.mult, ALU.mult)
    P.tt(COS, SH, SH, ALU.mult)
    P.ts(COS, COS, -2.0, ALU.mult, 1.0, ALU.add)
    P.tt(ABR, MAG, COS, ALU.mult)
    P.tt(ABI, MAG, SIN, ALU.mult)
    P.ts(NR, ABR, -1.0, ALU.add)
    P.tt(DEN, LR, LR, ALU.mult)
    P.tt(T1, LI, LI, ALU.mult)
    P.tt(DEN, DEN, T1, ALU.add)
    P.add(V, lambda e: e.reciprocal(out=RDEN, in_=DEN), [pp], [pp])
    P.tt(T1, NR, LR, ALU.mult)
    P.tt(T2, ABI, LI, ALU.mult)
    P.tt(T1, T1, T2, ALU.add)
    P.tt(COEFR, T1, RDEN, ALU.mult)
    P.tt(T1, ABI, LR, ALU.mult)
    P.tt(T2, NR, LI, ALU.mult)
    P.tt(T1, T1, T2, ALU.subtract)
    P.tt(COEFI, T1, RDEN, ALU.mult)

    def cmul(outr, outi, ar, ai, br, bi):
        P.tt(T1, ai, bi, ALU.mult)
        P.tt(T2, ar, br, ALU.mult)
        P.tt(outr, T2, T1, ALU.subtract)
        P.tt(T1, ar, bi, ALU.mult)
        P.tt(T2, ai, br, ALU.mult)
        P.tt(outi, T1, T2, ALU.add)

    P.memset(pw[:, 0, 0, :], 1.0)
    P.memset(pw[:, 1, 0, :], 0.0)
    P.copy(pw[:, 0, 1, :], ABR)
    P.copy(pw[:, 1, 1, :], ABI)
    for k in range(1, 8):
        cmul(pw[:, 0, k + 1, :], pw[:, 1, k + 1, :], pw[:, 0, k, :], pw[:, 1, k, :], ABR, ABI)
    for k in range(8):
        cmul(cF[:, 0, k, :], cF[:, 1, k, :], pw[:, 0, 7 - k, :], pw[:, 1, 7 - k, :], COEFR, COEFI)
    P.act(RR[:], LRDT, AF.Exp, scale=8.0)
    P.act(INVR, LRDT, AF.Exp, scale=-8.0)
    P.tt(zt[:, 0, 0, :], pw[:, 0, 8, :], INVR, ALU.mult)
    P.tt(zt[:, 1, 0, :], pw[:, 1, 8, :], INVR, ALU.mult)
    for j in range(8):
        cmul(zt[:, 0, j + 1, :], zt[:, 1, j + 1, :], zt[:, 0, j, :], zt[:, 1, j, :], zt[:, 0, j, :], zt[:, 1, j, :])

    tabA = sb("s5tabA", [128, 2, 32, 16])
    tabB = sb("s5tabB", [128, 2, 32, 32])
    tT1 = sb("s5A1", [128, 512])
    tT2 = sb("s5A2", [128, 512])
    A1, A2 = tT1, tT2
    for tab, nlev, j0 in ((tabA, 4, 0), (tabB, 5, 4)):
        P.memset(tab[:, 0, :, 0:1], 1.0)
        P.memset(tab[:, 1, :, 0:1], 0.0)
        for lv in range(nlev):
            n = 1 << lv
            zr = zt[:, 0, j0 + lv, :].unsqueeze(2).to_broadcast([128, 32, n])
            zi = zt[:, 1, j0 + lv, :].unsqueeze(2).to_broadcast([128, 32, n])
            ar, ai = tab[:, 0, :, 0:n], tab[:, 1, :, 0:n]
            t1 = tT1[:].rearrange("p (a b) -> p a b", b=16)[:, :, 0:n]
            t2 = tT2[:].rearrange("p (a b) -> p a b", b=16)[:, :, 0:n]
            P.tt(t1, ar, zr, ALU.mult, reads=[tab, zt], writes=[tT1])
            P.tt(t2, ai, zi, ALU.mult, reads=[tab, zt], writes=[tT2])
            P.tt(tab[:, 0, :, n:2 * n], t1, t2, ALU.subtract, reads=[tT1, tT2], writes=[tab])
            P.tt(t1, ar, zi, ALU.mult, reads=[tab, zt], writes=[tT1])
            P.tt(t2, ai, zr, ALU.mult, reads=[tab, zt], writes=[tT2])
            P.tt(tab[:, 1, :, n:2 * n], t1, t2, ALU.add, reads=[tT1, tT2], writes=[tab])

    rstd = sb("s5rstd", [128, NT])
    ssq = sb("s5ssq", [128, NT])
    for tt in range(NT):
        xi = C.xin[tt % 2]
        P.dma("sync", xi[:], x_d[tt * 128:(tt + 1) * 128, :])
        P.act(C.sq[:], xi[:], AF.Square, accum_out=ssq[:, tt:tt + 1], writes=[C.sq, ("ssq", tt)])
    P.act(ssq[:], ssq[:], AF.Sqrt, bias=C.epsb[:, 0:1], scale=1.0 / D,
          reads=[("ssq", t) for t in range(NT)] + [C.epsb], writes=["ssq_all"])
    P.add(V, lambda e: e.reciprocal(out=rstd[:], in_=ssq[:]), ["ssq_all"], [rstd])

    g0 = sb("s5g0", [128, D])
    P.dma("sync", g0[:], gains_d[0:1, :].partition_broadcast(128))
    HB = NT // 4
    xblk = sb("s5xblk", [128, HB, 128])
    hblk = sb("s5hblk", [128, HB, 128], BF16)
    hTd = [sb("s5hT%d" % i, [128, 8, T // 8], BF16) for i in range(1)]
    bc = sb("s5bc", [128, 2, 4, 32])
    cc = sb("s5cc", [128, 2, 4, 32])
    tA = sb("s5tA", [128, 9, 32])
    tB = sb("s5tB", [128, 9, 32])
    Fre = [sb("s5Fre%d" % q, [128, 8, 128], BF16) for q in range(4)]
    Fim = [sb("s5Fim%d" % q, [128, 8, 128], BF16) for q in range(4)]
    Ere = [sb("s5Ere%d" % q, [128, 9, 128], BF16) for q in range(4)]
    Eni = [sb("s5Eni%d" % q, [128, 9, 128], BF16) for q in range(4)]
    for q in range(4):
        for t_ in (Fre[q], Fim[q], Ere[q], Eni[q]):
            P.memset(t_[:], 0.0, eng=G)
    FTre = [sb("s5FTre%d" % i, [128, 8, 128], BF16) for i in range(2)]
    FTim = [sb("s5FTim%d" % i, [128, 8, 128], BF16) for i in range(2)]
    crT = [sb("s5cr%d" % i, [128, 512]) for i in range(2)]
    srT = [sb("s5sr%d" % i, [128, 512]) for i in range(2)]
    pt1 = sb("s5pt1", [128, 512])
    pt2 = sb("s5pt2", [128, 512])
    Vpr = sb("s5Vpr", [128, 512])
    Vpi = sb("s5Vpi", [128, 512])
    Wrs = [sb("s5Wr%d" % i, [128, 512]) for i in range(2)]
    Wis = [sb("s5Wi%d" % i, [128, 512]) for i in range(2)]
    Xs = sb("s5Xs", [128, 4, 2, 520], BF16)
    P.add(G, lambda e: e.memset(Xs[:], 0.0), [], [("Xs", q) for q in range(4)])
    KT = sb("s5KT", [128, 8, 128], BF16)
    ytmp = [sb("s5yt%d" % i, [128, 512]) for i in range(1)] * 2
    pvr, pvi, pk0, pk1, py0, py1 = C.pb[0], C.pb[1], C.pb[2], C.pb[3], C.pb[4], C.pb[5]
    xview = x_d.rearrange("(t p) (dc c) -> p t dc c", p=128, c=128)

    for dc in range(DC):
        hT = hTd[0]
        for hf in range(4):
            P.dma("sync", xblk[:], xview[:, hf * HB:(hf + 1) * HB, dc, :])
            P.tt(xblk[:], xblk[:], rstd[:, hf * HB:(hf + 1) * HB].unsqueeze(2).to_broadcast([128, HB, 128]), ALU.mult)
            P.tt(hblk[:], xblk[:], g0[:, dc * 128:(dc + 1) * 128].unsqueeze(1).to_broadcast([128, HB, 128]), ALU.mult)
            for t8 in range(HB // 8):
                pT = C.ptr[t8 % 2]
                for j in range(8):
                    P.tr(pT[:, j, :], hblk[:, t8 * 8 + j, :], C.identb[:])
                c0_ = (hf * HB + t8 * 8) * 16
                P.copy(hT[:, :, c0_:c0_ + 128].rearrange("p k c -> p c k"),
                       pT[:].rearrange("p a b -> p (a b)").rearrange("p (c k) -> p c k", k=8), eng="scalar")
        P.dma("sync", bc[:], s5b_d[:, :, dc * 4:(dc + 1) * 4, :])
        P.dma("sync", cc[:], s5c_d[:, :, dc * 4:(dc + 1) * 4, :])
        def stage1(q, dc=dc, hT=hT):
            st = dc * 4 + q
            co = 32 * q
            pvr, pvi = C.pb[2 * (q % 2)], C.pb[2 * (q % 2) + 1]
            cFr = cF[:, 0, :, st:st + 1].to_broadcast([128, 8, 32])
            cFi = cF[:, 1, :, st:st + 1].to_broadcast([128, 8, 32])
            Br = bc[:, 0, q:q + 1, :].to_broadcast([128, 8, 32])
            Bi = bc[:, 1, q:q + 1, :].to_broadcast([128, 8, 32])
            a8, b8 = tA[:, 0:8, :], tB[:, 0:8, :]
            P.tt(a8, Br, cFr, ALU.mult, reads=[bc, cF], writes=[tA])
            P.tt(b8, Bi, cFi, ALU.mult, reads=[bc, cF], writes=[tB])
            P.tt(Fre[q][:, :, co:co + 32], a8, b8, ALU.subtract, reads=[tA, tB], writes=[Fre[q]])
            P.tt(a8, Br, cFi, ALU.mult, reads=[bc, cF], writes=[tA])
            P.tt(b8, Bi, cFr, ALU.mult, reads=[bc, cF], writes=[tB])
            P.tt(Fim[q][:, :, co:co + 32], a8, b8, ALU.add, reads=[tA, tB], writes=[Fim[q]])
            pr = pw[:, 0, :, st:st + 1].to_broadcast([128, 9, 32])
            pi = pw[:, 1, :, st:st + 1].to_broadcast([128, 9, 32])
            Cr = cc[:, 0, q:q + 1, :].to_broadcast([128, 9, 32])
            Ci = cc[:, 1, q:q + 1, :].to_broadcast([128, 9, 32])
            P.tt(tA[:], Cr, pr, ALU.mult, reads=[cc, pw], writes=[tA])
            P.tt(tB[:], Ci, pi, ALU.mult, reads=[cc, pw], writes=[tB])
            P.tt(Ere[q][:, :, co:co + 32], tA[:], tB[:], ALU.subtract, reads=[tA, tB], writes=[Ere[q]])
            P.tt(tA[:], Cr, pi, ALU.mult, reads=[cc, pw], writes=[tA])
            P.tt(tB[:], Ci, pr, ALU.mult, reads=[cc, pw], writes=[tB])
            P.stt(Eni[q][:, :, co:co + 32], tA[:], -1.0, tB[:], ALU.mult, ALU.subtract, reads=[tA, tB], writes=[Eni[q]])
            ftr, fti = FTre[q % 2], FTim[q % 2]
            for k in range(8):
                P.tr(C.ptr[0][:, k, :], Fre[q][:, k, :], C.identb[:])
            P.copy(ftr[:], C.ptr[0][:], eng="scalar")
            for k in range(8):
                P.tr(C.ptr[1][:, k, :], Fim[q][:, k, :], C.identb[:])
            P.copy(fti[:], C.ptr[1][:], eng="scalar")
            for k in range(8):
                P.mm(pvr[:], ftr[:, k, :], hT[:, k, :], k == 0, k == 7)
            for k in range(8):
                P.mm(pvi[:], fti[:, k, :], hT[:, k, :], k == 0, k == 7)
            cr, sr = crT[q % 2], srT[q % 2]
            Br = tabB[:, 0, st, :].unsqueeze(2).to_broadcast([128, 32, 16])
            Bi = tabB[:, 1, st, :].unsqueeze(2).to_broadcast([128, 32, 16])
            Ar = tabA[:, 0, st, :].unsqueeze(1).to_broadcast([128, 32, 16])
            Ai = tabA[:, 1, st, :].unsqueeze(1).to_broadcast([128, 32, 16])
            v3 = lambda t_: t_[:].rearrange("p (a b) -> p a b", b=16)
            P.tt(v3(pt1), Br, Ar, ALU.mult, eng=G, reads=[tabA, tabB], writes=[pt1])
            P.tt(v3(pt2), Bi, Ai, ALU.mult, eng=G, reads=[tabA, tabB], writes=[pt2])
            P.tt(cr[:], pt1[:], pt2[:], ALU.subtract, eng=G)
            P.tt(v3(pt1), Br, Ai, ALU.mult, eng=G, reads=[tabA, tabB], writes=[pt1])
            P.tt(v3(pt2), Bi, Ar, ALU.mult, eng=G, reads=[tabA, tabB], writes=[pt2])
            P.tt(sr[:], pt1[:], pt2[:], ALU.add, eng=G)
        def stage2(q, dc=dc):
            st = dc * 4 + q
            pvr, pvi = C.pb[2 * (q % 2)], C.pb[2 * (q % 2) + 1]
            cr, sr = crT[q % 2], srT[q % 2]
            Wr, Wi = Wrs[q % 2], Wis[q % 2]
            P.tt(A1[:], pvr[:], cr[:], ALU.mult)
            P.tt(A2[:], pvi[:], sr[:], ALU.mult)
            P.tt(Vpr[:], A1[:], A2[:], ALU.add)
            P.tt(A1[:], pvi[:], cr[:], ALU.mult)
            P.tt(A2[:], pvr[:], sr[:], ALU.mult)
            P.tt(Vpi[:], A1[:], A2[:], ALU.subtract)
            Rb = RR[:, st:st + 1].to_broadcast([128, 512])
            P.add(V, lambda e, Rb=Rb: e.tensor_tensor_scan(out=Wr[:], data0=Rb, data1=Vpr[:], initial=0.0, op0=ALU.mult, op1=ALU.add),
                  [RR, Vpr], [Wr])
            P.add(V, lambda e, Rb=Rb: e.tensor_tensor_scan(out=Wi[:], data0=Rb, data1=Vpi[:], initial=0.0, op0=ALU.mult, op1=ALU.add),
                  [RR, Vpi], [Wi])
            P.tt(pt1[:], Wr[:], cr[:], ALU.mult, eng=G)
            P.tt(pt2[:], Wi[:], sr[:], ALU.mult, eng=G)
            P.tt(Xs[:, q, 0, 1:513], pt1[:], pt2[:], ALU.subtract, eng=G, writes=[("Xs", q)])
            P.tt(pt1[:], Wi[:], cr[:], ALU.mult, eng=G)
            P.tt(pt2[:], Wr[:], sr[:], ALU.mult, eng=G)
            P.tt(Xs[:, q, 1, 1:513], pt1[:], pt2[:], ALU.add, eng=G, writes=[("Xs", q)])
        stage1(0)
        for q in range(4):
            if q < 3:
                stage1(q + 1)
            stage2(q)
            P.pump(1)
        for tau in range(8):
            pk = (pk0, pk1)[tau // 4]
            o = pk[:, (tau % 4) * 128:(tau % 4 + 1) * 128]
            n = 0
            for q in range(4):
                P.mm(o, Fre[q][:, 7 - tau, :], Ere[q][:, 0, :], n == 0, False)
                n += 1
                P.mm(o, Fim[q][:, 7 - tau, :], Eni[q][:, 0, :], False, n == 7)
                n += 1
        P.copy(KT[:, 0:4, :].rearrange("p a b -> p (a b)"), pk0[:], eng="scalar")
        P.copy(KT[:, 4:8, :].rearrange("p a b -> p (a b)"), pk1[:], eng="scalar")
        for k in range(8):
            py = (py0, py1)[k % 2]
            n_mm = 8 + k + 1
            n = 0
            for q in range(4):
                P.mm(py[:], Ere[q][:, k + 1, :], Xs[:, q, 0, 0:512], n == 0, False, reads=[Ere[q], ("Xs", q)])
                n += 1
                P.mm(py[:], Eni[q][:, k + 1, :], Xs[:, q, 1, 0:512], False, False, reads=[Eni[q], ("Xs", q)])
                n += 1
            for k2 in range(k + 1):
                n += 1
                P.mm(py[:], KT[:, k - k2, :], hT[:, k2, :], False, n == n_mm)
            yt = ytmp[k % 2]
            P.stt(yt[:], hT[:, k, :], dsk[:, dc:dc + 1], py[:], ALU.mult, ALU.add)
            P.act(strided(yT[:, dc, :], k), yt[:], AF.Gelu_apprx_tanh, writes=[("yT", dc)])


def emit_glu(P, C, sb, x_d, wglu_d, gains_d, yT, xmid_d, dbg_m=None):
    wglu = sb("wglu", [128, DC, 2 * D], BF16)
    wv = wglu_d.rearrange("(dc p) n -> p dc n", p=128)
    for dc in range(DC):
        P.dma("gpsimd", wglu[:, dc, :], wv[:, dc, :], writes=[("wglu", dc)])
    g1 = sb("glu_g", [128, D])
    P.dma("sync", g1[:], gains_d[1:2, :].partition_broadcast(128))
    sig = sb("glu_sig", [128, D])
    mt = [sb("glu_m%d" % i, [128, D]) for i in range(2)]
    xo = [sb("glu_xo%d" % i, [128, D]) for i in range(2)]
    banks = [C.pb[0], C.pb[1], C.pb[2], C.pb[3]]
    for tt in range(NT):
        xi = C.xin[tt % 2]
        P.dma("sync", xi[:], x_d[tt * 128:(tt + 1) * 128, :])
        for nch in range(4):
            for dc in range(DC):
                P.mm(banks[nch][:], yT[:, dc, tt * 128:(tt + 1) * 128], wglu[:, dc, nch * 512:(nch + 1) * 512],
                     dc == 0, dc == DC - 1, reads=[("yT", dc), ("wglu", dc)])
        m = mt[tt % 2]
        for hh in range(2):
            P.act(sig[:, hh * 512:(hh + 1) * 512], banks[2 + hh][:], AF.Sigmoid, writes=[("sig", hh)])
            P.tt(m[:, hh * 512:(hh + 1) * 512], banks[hh][:], sig[:, hh * 512:(hh + 1) * 512], ALU.mult,
                 reads=[banks[hh], ("sig", hh)], writes=[m])
        if dbg_m is not None:
            P.dma("sync", dbg_m[tt * 128:(tt + 1) * 128, :], m[:])
        o = xo[tt % 2]
        emit_postnorm_residual(P, C, m, xi, g1[:], g1, o, tt)
        P.dma("sync", xmid_d[tt * 128:(tt + 1) * 128, :], o[:])


def emit_ffn(P, C, sb, layer, src_d, dst_d, win_d, wout_d, conv_d, gains_d, gi_pre, gi_post, wq_eng="sync"):
    L = "f%d_" % layer
    win = sb(L + "win", [128, DC, 2 * FF], BF16)
    wout = sb(L + "wout", [128, FC, D], BF16)
    wiv = win_d.rearrange("(dc p) n -> p dc n", p=128)
    wov = wout_d.rearrange("(fc p) n -> p fc n", p=128)
    HF = FF // 2

    def load_weights():
        n_ = 0
        for cg in range(2):
            for hh in range(2):
                c0 = hh * FF + cg * HF
                P.dma(("sync", "scalar")[n_ % 2], win[:, :, c0:c0 + HF], wiv[:, :, c0:c0 + HF], writes=[(L + "win", hh, cg)])
                n_ += 1
        for hf_ in range(2):
            P.dma(("sync", "scalar")[hf_], wout[:, hf_ * 11:(hf_ + 1) * 11, :], wov[:, hf_ * 11:(hf_ + 1) * 11, :], writes=[(L + "wout", hf_)])
    cv = sb(L + "conv", [128, FC, 4])
    P.dma("sync", cv[:], conv_d)
    gpre = sb(L + "gpre", [128, D])
    gpost = sb(L + "gpost", [128, D])
    P.dma("sync", gpre[:], gains_d[gi_pre:gi_pre + 1, :].partition_broadcast(128))
    P.dma("sync", gpost[:], gains_d[gi_post:gi_post + 1, :].partition_broadcast(128))
    h2Ts = [sb(L + "h2T%d" % i, [128, DC, 512], BF16) for i in range(1)] * 2
    hbx = [C.hb[0], C.hb[1], sb(L + "hb2", [128, D], BF16), sb(L + "hb3", [128, D], BF16)]
    uT = sb(L + "uT", [128, FC, 512], BF16)
    gprev = sb(L + "gprev", [128, FC, 2])
    P.add("vector", lambda e: e.memset(gprev[:], 0.0), [], [(L + "gprev", fc) for fc in range(FC)])
    gbuf = [sb(L + "gbuf%d" % i, [128, 516]) for i in range(1)]
    cva = [sb(L + "cva%d" % i, [128, 512]) for i in range(1)]
    cvb = [sb(L + "cvb%d" % i, [128, 512]) for i in range(1)]
    xr = sb(L + "xr", [128, D])
    ft = [sb(L + "ft%d" % i, [128, D]) for i in range(1)]
    ot = [xr]
    pg = [C.pb[0], C.pb[1]]
    pv = [C.pb[2], C.pb[3]]
    po = [C.pb[4], C.pb[5]]
    NCH = T // 512

    def normA(ch):
        for j in range(4):
            tt = ch * 4 + j
            P.dma("sync", C.xin[j % 2][:], src_d[tt * 128:(tt + 1) * 128, :])
            emit_norm_A(P, C, C.xin[j % 2], gpre[:], gpre, hbx[j], tt)

    def normB(ch):
        for j in range(4):
            emit_norm_B(P, C, hbx[j], h2Ts[ch % 2], j * 128, ch * 4 + j)

    normA(0)
    load_weights()
    normB(0)
    for ch in range(NCH):
        h2T = h2Ts[ch % 2]
        for fc in range(FC):
            g_, v_ = pg[fc % 2], pv[fc % 2]
            for dc in range(DC):
                P.mm(g_[:], win[:, dc, fc * 128:(fc + 1) * 128], h2T[:, dc, :], dc == 0, dc == DC - 1,
                     reads=[(L + "win", 0, fc // 11), h2T])
            for dc in range(DC):
                P.mm(v_[:], win[:, dc, FF + fc * 128:FF + (fc + 1) * 128], h2T[:, dc, :], dc == 0, dc == DC - 1,
                     reads=[(L + "win", 1, fc // 11), h2T])
            gb = gbuf[0]
            P.copy(gb[:, 0:2], gprev[:, fc, :], eng="gpsimd", reads=[(L + "gprev", fc)], writes=[gb])
            P.copy(gb[:, 2:514], g_[:], eng="scalar")
            P.copy(gprev[:, fc, :], gb[:, 512:514], eng="gpsimd", reads=[gb], writes=[(L + "gprev", fc)])
            ca, cb_ = cva[0], cvb[0]
            P.ts(ca[:], gb[:, 0:512], cv[:, fc, 0:1], ALU.mult, cv[:, fc, 3:4], ALU.add)
            P.stt(cb_[:], gb[:, 1:513], cv[:, fc, 1:2], ca[:], ALU.mult, ALU.add)
            P.stt(ca[:], gb[:, 2:514], cv[:, fc, 2:3], cb_[:], ALU.mult, ALU.add)
            P.act(cb_[:], ca[:], AF.Gelu_apprx_tanh)
            P.tt(uT[:, fc, :], cb_[:], v_[:], ALU.mult, writes=[(L + "uT", fc)])
        P.pump(4)
        if ch + 1 < NCH:
            normA(ch + 1)
        for j in range(4):
            tt = ch * 4 + j
            for nch in range(2):
                for fc in range(FC):
                    P.mm(po[nch][:], uT[:, fc, j * 128:(j + 1) * 128], wout[:, fc, nch * 512:(nch + 1) * 512],
                         fc == 0, fc == FC - 1, reads=[(L + "uT", fc), (L + "wout", fc // 11)])
            f = ft[0]
            P.dma("sync", xr[:], src_d[tt * 128:(tt + 1) * 128, :])
            P.copy(f[:, 0:512], po[0][:], eng="scalar", writes=[f])
            P.copy(f[:, 512:1024], po[1][:], eng="scalar", writes=[f])
            o = ot[0]
            emit_postnorm_residual(P, C, f, xr, gpost[:], gpost, o, tt)
            P.dma("sync", dst_d[tt * 128:(tt + 1) * 128, :], o[:])
        if ch + 1 < NCH:
            normB(ch + 1)
NCMP = 255
THETA = 500000.0


def host_consts():
    c = {}
    k = np.arange(T)
    c["ind"] = (np.arange(64)[:, None] == (k[None, :] // 64)).astype(np.float32)
    inv = THETA ** (-np.arange(8, dtype=np.float32) / 8.0)
    ang = k.astype(np.float32)[:, None] * inv[None, :]
    cs = np.stack([np.cos(ang), np.sin(ang)], 0).astype(np.float32)
    c["ropek"] = np.ascontiguousarray(cs.reshape(2, NT, 128, 8).transpose(2, 0, 1, 3))
    pc = (np.arange(256) * 16 + 31).astype(np.float32)
    angc = pc[:, None] * inv[None, :]
    csc = np.stack([np.cos(angc), np.sin(angc)], 0).astype(np.float32)
    c["ropec"] = np.ascontiguousarray(csc.reshape(2, 2, 128, 8).transpose(2, 0, 1, 3))
    n = np.arange(256)
    j = np.arange(64)
    ov = ((n[:, None] * 16 < (j[None, :] + 1) * 64) & (n[:, None] * 16 + 32 > j[None, :] * 64) & (n[:, None] < NCMP))
    c["ovl"] = np.ascontiguousarray(ov.astype(np.float32).reshape(2, 128, 64).transpose(1, 0, 2))
    t = np.arange(T)
    cm = np.where((n[:, None] * 16 + 31 <= t[None, :]) & (n[:, None] < NCMP), 0.0, NEGM).astype(np.float32)
    c["cmask"] = np.ascontiguousarray(cm.reshape(2, 128, NT, 128).transpose(2, 1, 0, 3))
    kp = np.arange(128)
    causal = np.where(kp[:, None] <= kp[None, :], 0.0, NEGM)
    strict = np.where(kp[:, None] > kp[None, :], 0.0, NEGM)
    c["dmask"] = np.ascontiguousarray(np.stack([causal, strict], 1).astype(np.float32))
    cur = t // 64
    valid = j[None, :] <= cur[:, None]
    forced = (j[None, :] == 0) | (j[None, :] == cur[:, None]) | (j[None, :] == cur[:, None] - 1)
    vm = (valid & ~forced).astype(np.float32)
    fb = np.where(forced, 1e4, np.where(valid, 0.0, -1e4)).astype(np.float32)
    sel = np.stack([vm, fb], 1)
    c["selc"] = np.ascontiguousarray(sel.reshape(NT, 128, 2, 64))
    return c


def host_layout_l1(inp, m):
    f = lambda a: np.ascontiguousarray(a, dtype=np.float32)
    m["wkv"] = f(inp["w_kv"])
    m["wq"] = f(inp["b_w_q"][0])
    m["wo"] = f(inp["b_w_o"][0])
    w1 = np.stack([inp["cmp_k_w1"], inp["cmp_v_w1"]], 0).reshape(2, 32, 64, 128).transpose(2, 0, 1, 3)
    m["w1d"] = f(np.concatenate([w1, w1], 0))
    m["w2"] = f(np.stack([inp["cmp_k_w2"], inp["cmp_v_w2"]], 1))
    pe = np.stack([inp["cmp_pe_k"].T, inp["cmp_pe_v"].T], 1)
    m["peT"] = f(pe)
    m.update(host_consts())
    return m


def rope_ops(P, src, dst, cos, sin, ta, tb, nh, skey, dkey):
    cb = cos.unsqueeze(1).to_broadcast([128, nh, 8])
    sb_ = sin.unsqueeze(1).to_broadcast([128, nh, 8])
    x1, x2 = src[:, :, 0:8], src[:, :, 8:16]
    a, b = ta[:, 0:nh, :], tb[:, 0:nh, :]
    P.tt(a, x1, cb, ALU.mult, reads=[skey, "ropetab"], writes=[ta])
    P.tt(b, x2, sb_, ALU.mult, reads=[skey, "ropetab"], writes=[tb])
    P.tt(dst[:, :, 0:8], a, b, ALU.subtract, reads=[ta, tb], writes=[dkey])
    P.tt(a, x2, cb, ALU.mult, reads=[skey, "ropetab"], writes=[ta])
    P.tt(b, x1, sb_, ALU.mult, reads=[skey, "ropetab"], writes=[tb])
    P.tt(dst[:, :, 8:16], a, b, ALU.add, reads=[ta, tb], writes=[dkey])


def emit_kv(P, C, sb, KV, x1_d, D_):
    G = "gpsimd"
    wkv = sb("wkv", [128, DC, 1536], BF16)
    wv = D_["wkv"].rearrange("(dc p) n -> p dc n", p=128)
    for dc in range(DC):
        P.dma(G, wkv[:, dc, :], wv[:, dc, :], writes=[("wkv", dc)])
    gk = sb("kv_g", [128, D])
    P.dma("sync", gk[:], D_["gains"][4:5, :].partition_broadcast(128))
    ropec = sb("ropec", [128, 2, 2, 8])
    P.dma("sync", ropec[:], D_["ropec"], writes=["ropetab"])
    for g in range(4):
        P.dma(G, KV.Kaug[64:128, g, :], D_["ind"], writes=[("Kaug_ind", g)])
    P.dma(G, KV.Ov[:], D_["ovl"])
    P.add("vector", lambda e: e.memset(KV.Vs[:], 1.0), [], ["Vs"])
    P.add("vector", lambda e: e.memset(KV.Vw[:], 1.0), [], ["Vw"])
    P.add("vector", lambda e: e.memset(KV.Vc[:], 1.0), [], ["Vc"])
    P.add("vector", lambda e: e.memset(KV.KcT[:], 0.0), [], ["KcT"])
    P.add("gpsimd", lambda e: e.memset(KV.Kw[64:128, :, :], 0.0), [], ["Kw_pad"])
    zT = sb("kv_zT", [128, 2, 2, 16, T // 16], BF16)
    kfs = [sb("kv_kf%d" % i, [128, 4, 64]) for i in range(2)]
    kbs = [sb("kv_kb%d" % i, [128, 4, 64], BF16) for i in range(2)]
    tas = [sb("kv_ta%d" % i, [128, 4, 8]) for i in range(2)]
    tbs = [sb("kv_tb%d" % i, [128, 4, 8]) for i in range(2)]
    kf, kb, ta, tb = kfs[0], kbs[0], tas[0], tbs[0]
    chunk_scope = contextlib.ExitStack()
    sb_outer = sb
    sb = lambda name, shape, dt=F32: chunk_scope.enter_context(KV.nc.sbuf_tensor("S_" + name, list(shape), dt))
    sT = sb("kv_sT", [128, DC, 512], BF16)
    hbx = [C.hb[0], C.hb[1], sb("kv_hb2", [128, D], BF16), sb("kv_hb3", [128, D], BF16)]
    NCH = T // 512
    pend = []

    def normA(ch):
        for j in range(4):
            tt = ch * 4 + j
            P.dma("sync", C.xin[j % 2][:], x1_d[tt * 128:(tt + 1) * 128, :])
            emit_norm_A(P, C, C.xin[j % 2], gk[:], gk, hbx[j], tt)

    def normB(ch):
        for j in range(4):
            emit_norm_B(P, C, hbx[j], sT, j * 128, ch * 4 + j)

    normA(0)
    normB(0)
    for ch in range(NCH):
        for cc_ in range(4):
            kv, gp = cc_ // 2, cc_ % 2
            pz = C.pb[cc_ % 2]
            for dc in range(DC):
                P.mm(pz[:], wkv[:, dc, cc_ * 128:(cc_ + 1) * 128], sT[:, dc, :], dc == 0, dc == DC - 1, reads=[("wkv", dc), sT])
            P.copy(zT[:, gp, kv, :, ch * 32:(ch + 1) * 32].rearrange("p s n -> p n s"),
                   pz[:].rearrange("p (n s) -> p n s", s=16), eng="scalar", writes=[zT])
        if ch + 1 < NCH:
            normA(ch + 1)
        for j in range(4):
            tt = ch * 4 + j
            for br in range(2):
                pk = C.pb[2 + (2 * j + br) % 2]
                kf, kb, ta, tb = kfs[br], kbs[br], tas[br], tbs[br]
                for dc in range(DC):
                    P.mm(pk[:], sT[:, dc, j * 128:(j + 1) * 128], wkv[:, dc, 512 * (br + 1):512 * (br + 2)], dc == 0, dc == DC - 1,
                         reads=[sT, ("wkv", dc)])
                while pend:
                    pend.pop(0)()
                Vdst = KV.Vs if br == 0 else KV.Vw
                P.copy(Vdst[:, tt, :, 0:64], pk[:, 256:512].rearrange("p (g d) -> p g d", g=4), eng="scalar",
                       writes=["Vs" if br == 0 else "Vw"])
                P.copy(kf[:], pk[:, 0:256].rearrange("p (g d) -> p g d", g=4), eng="scalar")
                P.copy(kb[:], kf[:], eng="gpsimd")
                rope_ops(P, kf, kb, KV.ropek[:, 0, tt, :], KV.ropek[:, 1, tt, :], ta, tb, 4, kf, kb)

                def trans(kb=kb, br=br, tt=tt):
                    pT = C.ptr[0]
                    for g in range(4):
                        P.tr(pT[0:64, g, :], kb[:, g, :], C.identb[:])
                    Kdst = KV.Kaug if br == 0 else KV.Kw
                    P.copy(Kdst[0:64, :, tt * 128:(tt + 1) * 128], pT[0:64, 0:4, :], eng="scalar",
                           writes=["Kaug" if br == 0 else "Kw"])
                pend.append(trans)
        while pend:
            pend.pop(0)()
        if ch + 1 < NCH:
            normB(ch + 1)
    chunk_scope.close()
    P.barrier()
    sb = sb_outer
    w1d = sb("w1d", [128, 2, 32, 128], BF16)
    P.dma(G, w1d[:], D_["w1d"])
    w2 = sb("w2", [128, 2, 64], BF16)
    P.dma(G, w2[:], D_["w2"])
    peT = sb("peT", [64, 2, 32], BF16)
    P.dma(G, peT[:], D_["peT"])
    cbias = sb("kv_cbias", [128, 2])
    hidT = sb("kv_hidT", [128, 256], BF16)
    P.memset(hidT[:], 0.0)
    for kv in range(2):
        for l in range(32):
            P.mm(C.pb[0][:, kv:kv + 1], w1d[0:64, kv, l, :], peT[:, kv, l:l + 1], l == 0, l == 31)
    P.copy(cbias[:], C.pb[0][:, 0:2])
    for kv in range(2):
        for g in range(4):
            kf, kb, ta, tb = kfs[g % 2], kbs[g % 2], tas[g % 2], tbs[g % 2]
            gp, base = g // 2, (g % 2) * 64
            ph = C.pb[(kv * 4 + g) % 2]
            zrow = zT[base:base + 64, gp, kv, :, :]
            for l in range(32):
                rhs = zrow[:, l, 0:NCMP] if l < 16 else zrow[:, l - 16, 1:NCMP + 1]
                P.mm(ph[:, 0:NCMP], w1d[base:base + 64, kv, l, :], rhs, l == 0, l == 31)
            P.act(hidT[:, 0:NCMP], ph[:, 0:NCMP], AF.Gelu_apprx_tanh, bias=cbias[:, kv:kv + 1])
            for nt_ in range(2):
                po = C.pb[2 + nt_]
                P.mm(po[:, 0:64], hidT[:, nt_ * 128:(nt_ + 1) * 128], w2[:, kv, :], True, True)
                if kv == 1:
                    P.copy(KV.Vc[:, nt_, g, 0:64], po[:, 0:64], eng="scalar", writes=["Vc"])
                else:
                    P.copy(kf[:, 0:1, :], po[:, 0:64].unsqueeze(1), eng="scalar", writes=[kf])
                    P.copy(kb[:, 0:1, :], kf[:, 0:1, :], eng="gpsimd", reads=[kf], writes=[kb])
                    rope_ops(P, kf[:, 0:1, :], kb[:, 0:1, :], ropec[:, 0, nt_, :], ropec[:, 1, nt_, :], ta, tb, 1, kf, kb)
                    pT = C.ptr[nt_]
                    P.tr(pT[0:64, 0, :], kb[:, 0, :], C.identb[:])
                    P.copy(KV.KcT[0:64, g, nt_ * 128:(nt_ + 1) * 128], pT[0:64, 0, :], eng="scalar", writes=["KcT"])


def emit_attn(P, C, sb, KV, x1_d, xmid_d, D_, dbg_m2=None, nc=None):
    G = "gpsimd"
    wq = sb("wq", [128, DC, 1072], BF16)
    wo = sb("wo", [128, DC, D], BF16)
    wqv = D_["wq"].rearrange("(dc p) n -> p dc n", p=128)
    wov = D_["wo"].rearrange("(dc p) n -> p dc n", p=128)
    for dc in range(DC):
        P.dma(G, wq[:, dc, :], wqv[:, dc, :])
        P.dma(G, wo[:, dc, :], wov[:, dc, :])
    gpre = sb("at_gpre", [128, D])
    gpost = sb("at_gpost", [128, D])
    P.dma("sync", gpre[:], D_["gains"][5:6, :].partition_broadcast(128))
    P.dma("sync", gpost[:], D_["gains"][6:7, :].partition_broadcast(128))
    dmask = sb("at_dmask", [128, 2, 128], BF16)
    P.dma(G, dmask[:], D_["dmask"])
    cmask = [sb("at_cmask%d" % i, [128, 2, 128], BF16) for i in range(2)]
    cmx = [sb("at_cmx%d" % i, [128, 2, 512], BF16) for i in range(2)]
    dmx = sb("at_dmx", [128, 2, 512], BF16)
    for i_ in range(2):
        P.copy(dmx[:, i_, :].rearrange("p (r t) -> p r t", r=4), dmask[:, i_:i_ + 1, :].to_broadcast([128, 4, 128]),
               reads=[dmask], writes=[dmx])
    selc = [sb("at_selc%d" % i, [128, 2, 64]) for i in range(2)]
    hqT = sb("at_hqT", [128, DC, 128], BF16)
    qb = sb("at_qb", [128, 16, 64], BF16)
    ta = sb("at_ta", [128, 16, 8])
    tb = sb("at_tb", [128, 16, 8])
    gt = [sb("at_gt%d" % i, [128, 16, 3]) for i in range(3)]
    Qaug = [sb("at_Qaug%d" % i, [128, 4, 512], BF16) for i in range(2)]
    for i_ in range(2):
        P.add("vector", lambda e, i_=i_: e.memset(Qaug[i_][:], 0.0), [],
              [("Qq", i_, g_) for g_ in range(4)] + [("Qn", i_, g_) for g_ in range(4)])
    Pexp = [sb("at_Pexp%d" % i, [128, 512], BF16) for i in range(3)]
    rdc = sb("at_rdc", [128, 4])
    rdw = sb("at_rdw", [128, 4])
    rds = sb("at_rds", [128, 4])
    imp = sb("at_imp", [128, 64])
    imt = sb("at_imt", [128, 4, 64])
    score = sb("at_score", [128, 64])
    score2 = sb("at_score2", [128, 64])
    m8a = sb("at_m8a", [128, 8])
    m8b = sb("at_m8b", [128, 8])
    thr = sb("at_thr", [128, 1])
    nmt = [sb("at_nmt%d" % i, [128, 128], BF16) for i in range(4)]
    for t_ in nmt:
        P.memset(t_[:], 0.0)
    ocw = [sb("at_ocw%d" % i, [128, 16, 64], BF16) for i in range(2)]
    tmpw = sb("at_tmpw", [128, 4, 64])
    tmps = sb("at_tmps", [128, 4, 64])
    ob = [sb("at_ob%d" % i, [128, 16, 64], BF16) for i in range(2)]
    obT = sb("at_obT", [128, DC, 128], BF16)
    m2 = sb("at_m2", [128, D])
    qf = m2[:].rearrange("p (h d) -> p h d", h=16)
    xres = sb("at_xres", [128, D])
    xo = xres
    S_b = [C.pb[0], C.pb[1], C.pb[6]]
    pc1, pc2, pw1, ps1 = C.pb[2], C.pb[3], C.pb[4], C.pb[5]
    v65 = lambda bank: bank[:, 0:260].rearrange("p (r d) -> p r d", r=4)
    v64 = lambda bank: bank[:, 0:256].rearrange("p (r d) -> p r d", r=4)
    KVK = "kvcache"
    cnt = [0]
    sbc = [0]

    def sbank():
        S = S_b[sbc[0] % 3]
        sbc[0] += 1
        return S
    defer = []
    late = []

    def run_deferred(upto=None):
        fs = [x for x in defer if upto is None or x[0] <= upto]
        rest = [x for x in defer if not (upto is None or x[0] <= upto)]
        del defer[:]
        defer.extend(rest)
        for _, f in fs:
            f()

    def run_late():
        fs = list(late)
        del late[:]
        for f in fs:
            f()

    def unit(qkey, lhsT, rhs, mask_ap, accv, rhs_v, first, last, extra=None):
        i = cnt[0]
        cnt[0] += 1
        S = sbank()
        P.mm(S[:], lhsT, rhs, True, mask_ap is None, reads=[KVK] + (list(qkey) if isinstance(qkey, list) else [qkey]))
        if mask_ap is not None:
            P.mm(S[:], C.identb[:], mask_ap, False, True)
        pe = Pexp[i % 3]
        P.act(pe[:], S[:], AF.Exp)
        run_deferred(i - 2)

        def pv():
            for r in range(4):
                P.mm(accv[0][:, r, :], pe[:, r * 128:(r + 1) * 128], rhs_v, first and r == 0, last, reads=[pe, KVK], writes=[accv[1]])
                if extra is not None:
                    P.mm(extra[0][:, r, :], pe[:, r * 128:(r + 1) * 128], extra[2], first and r == 0, last, reads=[pe, KVK], writes=[extra[1]])
        defer.append((i, pv))

    def recip_den(dst, bank):
        P.ts(dst[:], v65(bank)[:, :, 64], 1e-30, ALU.max, reads=[bank], writes=[dst])
        P.add("vector", lambda e: e.reciprocal(out=dst[:], in_=dst[:]), [dst], [dst])

    def pA(qt):
        b = qt % 2
        xi = C.xin[b]
        P.dma("sync", xi[:], x1_d[qt * 128:(qt + 1) * 128, :])
        cm, sc, cx = cmask[b], selc[b], cmx[b]
        P.dma(G, cm[:], D_["cmask"][qt])
        P.dma("sync", sc[:], D_["selc"][qt])
        for i_ in range(2):
            P.copy(cx[:, i_, :].rearrange("p (r t) -> p r t", r=4), cm[:, i_:i_ + 1, :].to_broadcast([128, 4, 128]),
                   eng="gpsimd", reads=[cm], writes=[cx])
        emit_norm_A(P, C, xi, gpre[:], gpre, C.hb[b], qt)

    def pB(qt):
        b = qt % 2
        emit_norm_B(P, C, C.hb[b], hqT, 0, qt)
        for nch in range(2):
            S = sbank()
            for dc in range(DC):
                P.mm(S[:], hqT[:, dc, :], wq[:, dc, nch * 512:(nch + 1) * 512], dc == 0, dc == DC - 1)
            P.act(m2[:, nch * 512:(nch + 1) * 512], S[:], AF.Copy, scale=0.125, writes=[m2])

    def pB2(qt):
        S = sbank()
        for dc in range(DC):
            P.mm(S[:, 0:48], hqT[:, dc, :], wq[:, dc, 1024:1072], dc == 0, dc == DC - 1)
        P.act(gt[qt % 3][:].rearrange("p h c -> p (h c)"), S[:, 0:48], AF.Sigmoid)

    def pD(qt):
        P.copy(qb[:], qf, eng="gpsimd", reads=[m2], writes=[qb])
        rope_ops(P, qf, qb, KV.ropek[:, 0, qt, :], KV.ropek[:, 1, qt, :], ta, tb, 16, m2, qb)

    def pE(qt):
        b = qt % 2
        for hh in range(2):
            pT = C.ptr[hh]
            for h8 in range(8):
                P.tr(pT[0:64, h8, :], qb[:, hh * 8 + h8, :], C.identb[:])
            P.copy(Qaug[b][0:64, hh * 2:(hh + 1) * 2, :].rearrange("p g (r t) -> p (g r) t", r=4), pT[0:64, :, :], eng="scalar",
                   writes=[("Qq", b, hh * 2), ("Qq", b, hh * 2 + 1)])

    def prologue(qt):
        pA(qt)
        pB(qt)
        pB2(qt)
        pD(qt)
        pE(qt)

    def a_stage(qt, g):
        b = qt % 2
        gs = slice(g * 4, (g + 1) * 4)
        Q = Qaug[b]
        cx, sc = cmx[b], selc[b]
        for nt_ in range(2):
            unit([("Qq", b, g), ("Qn", b, g)], KV.KcT[:, g, nt_ * 128:(nt_ + 1) * 128], Q[:, g, :], cx[:, nt_, :], (v65(pc1), pc1),
                 KV.Vc[:, nt_, g, 0:65], nt_ == 0, nt_ == 1, extra=(v64(pc2), pc2, KV.Ov[:, nt_, :]))

        def evac_cmp():
            recip_den(rdc, pc1)
            rb = rdc[:, :].unsqueeze(2).to_broadcast([128, 4, 64])
            P.tt(imt[:], v64(pc2), rb, ALU.mult, reads=[pc2, rdc], writes=[imt])
            P.add("vector", lambda e: e.tensor_reduce(out=imp[:], in_=imt[:].rearrange("p r j -> p j r"), axis=AX.X, op=ALU.add), [imt], [imp])
            P.tt(rdc[:], rdc[:], gt[qt % 3][:, gs, 0], ALU.mult)
            P.tt(ocw[b][:, gs, :], v65(pc1)[:, :, 0:64], rb, ALU.mult, reads=[pc1, rdc], writes=[("ocw", b, g)])
            P.tt(score[:], imp[:], sc[:, 0, :], ALU.mult)
            P.tt(score[:], score[:], sc[:, 1, :], ALU.add)
            P.add("vector", lambda e: e.max(out=m8a[:], in_=score[:]), [score], [m8a])
            P.add("vector", lambda e: e.match_replace(out=score2[:], in_to_replace=m8a[:], in_values=score[:], imm_value=-1e9), [m8a, score], [score2])
            P.add("vector", lambda e: e.max(out=m8b[:], in_=score2[:]), [score2], [m8b])
            P.ts(thr[:], m8b[:, 7:8], -5000.0, ALU.max)
            P.ts(score2[:], score[:], thr[:, 0:1], ALU.is_ge)
            nm = nmt[g]
            P.ts(nm[:, 64:128], score2[:], -1.0, ALU.add, -NEGM, ALU.mult)

            def neg_rows():
                pT = C.ptr[g % 2]
                P.tr(pT[:, 0, :], nm[:], C.identb[:])
                P.copy(Q[64:128, g, :].rearrange("p (r t) -> p r t", r=4), pT[64:128, 0:1, :].to_broadcast([64, 4, 128]), eng="scalar",
                       reads=[pT], writes=[("Qn", b, g)])
            late.append(neg_rows)
        defer.append((cnt[0] - 1, evac_cmp))
        kts = [k_ for k_ in range(qt - 4, qt + 1) if k_ >= 0]
        for kt in kts:
            mk = None
            if kt == qt:
                mk = dmx[:, 0, :]
            elif kt == qt - 4:
                mk = dmx[:, 1, :]
            unit([("Qq", b, g), ("Qn", b, g)], KV.Kw[:, g, kt * 128:(kt + 1) * 128], Q[:, g, :], mk,
                 (v65(pw1), pw1), KV.Vw[:, kt, g, :], kt == kts[0], kt == kts[-1])

        def evac_win():
            recip_den(rdw, pw1)
            P.tt(rdw[:], rdw[:], gt[qt % 3][:, gs, 2], ALU.mult)
            P.tt(tmpw[:], v65(pw1)[:, :, 0:64], rdw[:, :].unsqueeze(2).to_broadcast([128, 4, 64]), ALU.mult, reads=[pw1, rdw], writes=[tmpw])
            P.tt(ocw[b][:, gs, :], ocw[b][:, gs, :], tmpw[:], ALU.add, eng="gpsimd", reads=[("ocw", b, g), tmpw], writes=[("ocw", b, g)])
        defer.append((cnt[0] - 1, evac_win))

    def b_stage(qt, g):
        b = qt % 2
        gs = slice(g * 4, (g + 1) * 4)
        Q = Qaug[b]
        for kt in range(qt + 1):
            unit([("Qn", b, g), ("Qq", b, g)], KV.Kaug[:, g, kt * 128:(kt + 1) * 128], Q[:, g, :], dmx[:, 0, :] if kt == qt else None,
                 (v65(ps1), ps1), KV.Vs[:, kt, g, :], kt == 0, kt == qt)

        def evac_sel():
            recip_den(rds, ps1)
            P.tt(rds[:], rds[:], gt[qt % 3][:, gs, 1], ALU.mult)
            P.tt(tmps[:], v65(ps1)[:, :, 0:64], rds[:, :].unsqueeze(2).to_broadcast([128, 4, 64]), ALU.mult, reads=[ps1, rds], writes=[tmps])
            P.tt(ob[b][:, gs, :], tmps[:], ocw[b][:, gs, :], ALU.add, eng="gpsimd", reads=[tmps, ("ocw", b, g)], writes=[("ob", b, g)])
        defer.append((cnt[0] - 1, evac_sel))

    def epilogue(qt):
        b = qt % 2
        P.dma("sync", xres[:], x1_d[qt * 128:(qt + 1) * 128, :])
        pT = C.ptr[0]
        for dc in range(DC):
            P.tr(pT[:, dc, :], ob[b][:].rearrange("p h d -> p (h d)")[:, dc * 128:(dc + 1) * 128], C.identb[:],
                 reads=[("ob", b, dc // 2), C.identb])
        P.copy(obT[:], pT[:], eng="scalar")
        for nch in range(2):
            for dc in range(DC):
                P.mm(C.pb[2 + nch][:], obT[:, dc, :], wo[:, dc, nch * 512:(nch + 1) * 512], dc == 0, dc == DC - 1)
            P.copy(m2[:, nch * 512:(nch + 1) * 512], C.pb[2 + nch][:], eng="scalar", writes=[m2])
        if dbg_m2 is not None:
            P.dma("sync", dbg_m2[qt * 128:(qt + 1) * 128, :], m2[:])
        emit_postnorm_residual(P, C, m2, xres, gpost[:], gpost, xo, qt)
        P.dma("sync", xmid_d[qt * 128:(qt + 1) * 128, :], xo[:])

    prologue(0)
    for g in range(4):
        a_stage(0, g)
    run_deferred()
    run_late()
    if NT > 1:
        prologue(1)
    for qt in range(NT):
        nxt = qt + 2
        for g in range(4):
            if qt + 1 < NT:
                a_stage(qt + 1, g)
            b_stage(qt, g)
            run_late()
            if g == 0 and qt > 0:
                epilogue(qt - 1)
            if nxt < NT:
                if g == 0:
                    pA(nxt)
                elif g == 1:
                    pB(nxt)
                elif g == 2:
                    pB2(nxt)
                    pD(nxt)
        if nxt < NT:
            pE(nxt)
    run_deferred()
    run_late()
    epilogue(NT - 1)
def build(stage="full", debug=False):
    nc = bass.Bass("TRN2", target_bir_lowering=False)
    P = Prog(nc)

    def din(name, shape, dt=F32):
        return nc.dram_tensor(name, list(shape), dt, kind="ExternalInput").ap()

    def dout(name, shape, dt=F32):
        return nc.dram_tensor(name, list(shape), dt, kind="ExternalOutput").ap()

    def dscr(name, shape, dt=F32):
        return nc.dram_tensor(name, list(shape), dt, kind="Internal").ap()

    x_d = din("x", [T, D])
    s5par_d = din("s5par", [128, 3, 32])
    s5b_d = din("s5b", [128, 2, 32, 32])
    s5c_d = din("s5c", [128, 2, 32, 32])
    s5d_d = din("s5d", [128, 8])
    wglu_d = din("wglu", [D, 2 * D])
    gains_d = din("gains", [9, D])
    win_d = [din("win%d" % l, [D, 2 * FF]) for l in range(2)]
    wout_d = [din("wout%d" % l, [FF, D]) for l in range(2)]
    conv_d = [din("conv%d" % l, [128, FC, 4]) for l in range(2)]
    ident_d = din("ident", [128, 128])
    D_ = {"gains": gains_d}
    for nm, shp in (("wkv", [D, 1536]), ("wq", [D, 1072]), ("wo", [D, D]), ("w1d", [128, 2, 32, 128]), ("w2", [128, 2, 64]),
                    ("peT", [64, 2, 32]), ("ind", [64, T]), ("ropek", [128, 2, NT, 8]), ("ropec", [128, 2, 2, 8]),
                    ("ovl", [128, 2, 64]), ("cmask", [NT, 128, 2, 128]), ("dmask", [128, 2, 128]), ("selc", [NT, 128, 2, 64])):
        D_[nm] = din(nm, shp)

    out_d = dout("out", [T, D])
    xmid_d = dscr("xmid", [T, D])
    x1_d = out_d if stage == "l0" else dscr("x1", [T, D])
    dbg_m = dout("dbg_m", [T, D]) if debug else None
    dbg_xmid = dout("dbg_xmid", [T, D]) if debug else None
    dbg_m2 = dout("dbg_m2", [T, D]) if (debug and stage != "l0") else None
    dbg_x1 = dout("dbg_x1", [T, D]) if (debug and stage != "l0") else None

    winb_d = [dscr("winb%d" % l, [D, 2 * FF], BF16) for l in range(2)]
    woutb_d = [dscr("woutb%d" % l, [FF, D], BF16) for l in range(2)]
    for l in range(2):
        P.bg.add("winb%d" % l)
        P.bg.add("woutb%d" % l)
    def precast(l):
        for r0 in range(0, D, 128):
            P.bgq.append(lambda r0=r0: P.dma("gpsimd", winb_d[l][r0:r0 + 128, :], win_d[l][r0:r0 + 128, :]))
        for r0 in range(0, FF, 128):
            P.bgq.append(lambda r0=r0: P.dma("gpsimd", woutb_d[l][r0:r0 + 128, :], wout_d[l][r0:r0 + 128, :]))
    precast(0)
    with contextlib.ExitStack() as glob:
        def gsb(name, shape, dt=F32):
            return glob.enter_context(nc.sbuf_tensor("S_" + name, list(shape), dt))

        def gps(name, shape, dt=F32):
            return glob.enter_context(nc.psum_tensor("P_" + name, list(shape), dt))

        C = Ctx()
        identf = gsb("identf", [128, 128])
        C.identb = gsb("identb", [128, 128], BF16)
        P.dma("sync", identf[:], ident_d)
        P.copy(C.identb[:], identf[:])
        C.epsb = gsb("epsb", [128, 1])
        P.memset(C.epsb[:], EPS)
        C.halfpi = gsb("halfpi", [128, 1])
        P.memset(C.halfpi[:], math.pi / 2)
        C.pb = [gps("pb%d" % i, [128, 512]) for i in range(7)]
        _ptr = gps("ptr0", [128, 8, 128], BF16)
        C.ptr = [_ptr, _ptr]
        C.xin = [gsb("xin%d" % i, [128, D]) for i in range(2)]
        C.sq = gsb("sqjunk", [128, D])
        C.stat = [gsb("stat%d" % i, [128, 4]) for i in range(2)]
        C.hb = [gsb("hb%d" % i, [128, D], BF16) for i in range(2)]

        with contextlib.ExitStack() as ph1:
            sb1 = lambda name, shape, dt=F32: ph1.enter_context(nc.sbuf_tensor("S_" + name, list(shape), dt))
            yT = sb1("yT", [128, DC, T], BF16)
            with contextlib.ExitStack() as ph1a:
                sba = lambda name, shape, dt=F32: ph1a.enter_context(nc.sbuf_tensor("S_" + name, list(shape), dt))
                emit_s5(P, C, nc, (sba, None), x_d, s5par_d, s5b_d, s5c_d, s5d_d, gains_d, yT)
            P.barrier()
            with contextlib.ExitStack() as ph1b:
                sbb = lambda name, shape, dt=F32: ph1b.enter_context(nc.sbuf_tensor("S_" + name, list(shape), dt))
                emit_glu(P, C, sbb, x_d, wglu_d, gains_d, yT, xmid_d, dbg_m)
            P.barrier()
        if debug:
            with contextlib.ExitStack() as phd:
                t_ = phd.enter_context(nc.sbuf_tensor("dbgt", [128, D], F32))
                for tt in range(NT):
                    P.dma("sync", t_[:], xmid_d[tt * 128:(tt + 1) * 128, :], reads=["xmid_all"])
                    P.dma("sync", dbg_xmid[tt * 128:(tt + 1) * 128, :], t_[:])
            P.barrier()
        with contextlib.ExitStack() as ph2:
            sb2 = lambda name, shape, dt=F32: ph2.enter_context(nc.sbuf_tensor("S_" + name, list(shape), dt))
            P.pump()
            precast(1)
            emit_ffn(P, C, sb2, 0, xmid_d, x1_d, winb_d[0], woutb_d[0], conv_d[0], gains_d, 2, 3, wq_eng="scalar")
        P.barrier()
        if stage != "l0":
            with contextlib.ExitStack() as phk:
                sbk = lambda name, shape, dt=F32: phk.enter_context(nc.sbuf_tensor("S_" + name, list(shape), dt))
                KV = Ctx()
                KV.nc = nc
                KV.Kaug = sbk("Kaug", [128, 4, T], BF16)
                KV.Kw = sbk("Kw", [128, 4, T], BF16)
                KV.Vs = sbk("Vs", [128, NT, 4, 65], BF16)
                KV.Vw = sbk("Vw", [128, NT, 4, 65], BF16)
                KV.KcT = sbk("KcT", [128, 4, 256], BF16)
                KV.Vc = sbk("Vc", [128, 2, 4, 65], BF16)
                KV.Ov = sbk("Ov", [128, 2, 64], BF16)
                KV.ropek = sbk("ropek", [128, 2, NT, 8])
                P.dma("sync", KV.ropek[:], D_["ropek"], writes=["ropetab"])
                with contextlib.ExitStack() as phk1:
                    sbk1 = lambda name, shape, dt=F32: phk1.enter_context(nc.sbuf_tensor("S_" + name, list(shape), dt))
                    emit_kv(P, C, sbk1, KV, x1_d, D_)
                P.barrier()
                if debug:
                    for nm, t_ in (("Kaug", KV.Kaug), ("Kw", KV.Kw), ("Vs", KV.Vs), ("Vw", KV.Vw), ("KcT", KV.KcT), ("Vc", KV.Vc)):
                        dd = nc.dram_tensor("dbg_" + nm, list(t_.shape), BF16, kind="ExternalOutput").ap()
                        P.dma("sync", dd, t_[:])
                    P.barrier()
                with contextlib.ExitStack() as phk2:
                    sbk2 = lambda name, shape, dt=F32: phk2.enter_context(nc.sbuf_tensor("S_" + name, list(shape), dt))
                    emit_attn(P, C, sbk2, KV, x1_d, xmid_d, D_, dbg_m2, nc=nc)
                P.barrier()
            if debug:
                with contextlib.ExitStack() as phd:
                    t_ = phd.enter_context(nc.sbuf_tensor("S_dbgt2", [128, D], F32))
                    for tt in range(NT):
                        P.dma("sync", t_[:], x1_d[tt * 128:(tt + 1) * 128, :])
                        P.dma("sync", dbg_x1[tt * 128:(tt + 1) * 128, :], t_[:])
                P.barrier()
            P.pump()
            with contextlib.ExitStack() as ph3:
                sb3 = lambda name, shape, dt=F32: ph3.enter_context(nc.sbuf_tensor("S_" + name, list(shape), dt))
                emit_ffn(P, C, sb3, 1, xmid_d, out_d, winb_d[1], woutb_d[1], conv_d[1], gains_d, 7, 8, wq_eng="scalar")
            P.barrier()
        P.emit()
    return nc, P


_NC_CACHE = {}


def kernel(**inputs):
    if "nc" not in _NC_CACHE:
        _NC_CACHE["nc"] = build("full")[0]
    nc = _NC_CACHE["nc"]
    in_maps = [host_layout_l1(inputs, host_layout(inputs, c // 2)) for c in range(8)]
    res = run_bass_kernel_spmd(nc, in_maps, core_ids=list(range(8)))
    out = np.stack([res.results[2 * b]["out"] for b in range(4)], axis=0)
    return out.astype(np.float32)
```

```python
import contextlib
import math
import numpy as np
import concourse.bass as bass
import concourse.mybir as mybir
from concourse.bass_utils import run_bass_kernel_spmd

F32 = mybir.dt.float32
BF16 = mybir.dt.bfloat16
I32 = mybir.dt.int32
AF = mybir.ActivationFunctionType
ALU = mybir.AluOpType
AX = mybir.AxisListType

T = 4096
D = 1024
NT = T // 128
DC = 8
FF = 2816
FC = FF // 128
EPS = 1e-6
NEGM = -30000.0

ENGS = ("tensor", "vector", "scalar", "gpsimd", "sync")


class Op:
    __slots__ = ("eng", "fn", "reads", "writes", "dma", "waits", "sig", "idx")

    def __init__(self, eng, fn, reads, writes, dma):
        self.eng, self.fn, self.reads, self.writes, self.dma = eng, fn, reads, writes, dma
        self.waits = {}
        self.sig = None
        self.idx = None


def _key(t):
    if isinstance(t, (str, tuple)):
        return t
    return t.name


class Prog:
    def __init__(self, nc):
        self.nc = nc
        self.ops = []
        self.bg = set()
        self.bgq = []

    def pump(self, n=None):
        k = len(self.bgq) if n is None else min(n, len(self.bgq))
        for _ in range(k):
            self.bgq.pop(0)()

    def add(self, eng, fn, reads=(), writes=(), dma=False):
        op = Op(eng, fn, [_key(r) for r in reads if r is not None],
                [_key(w) for w in writes if w is not None], dma)
        op.idx = len(self.ops)
        self.ops.append(op)
        return op

    def barrier(self):
        op = Op("barrier", None, [], [], False)
        op.idx = len(self.ops)
        self.ops.append(op)
        return op

    def dma(self, eng, out, in_, reads=None, writes=None):
        r = [in_] if reads is None else reads
        w = [out] if writes is None else writes
        return self.add(eng, lambda e: e.dma_start(out=out, in_=in_), r, w, dma=True)

    def act(self, out, in_, func, bias=None, scale=None, accum_out=None, eng="scalar", reads=None, writes=None):
        kw = {}
        rd = [in_]
        if bias is not None:
            kw["bias"] = bias
            if not isinstance(bias, (int, float)):
                rd.append(bias)
        if scale is not None:
            kw["scale"] = scale
            if not isinstance(scale, (int, float)):
                rd.append(scale)
        wr = [out]
        if accum_out is not None:
            kw["accum_out"] = accum_out
            wr.append(accum_out)
        return self.add(eng, lambda e: e.activation(out=out, in_=in_, func=func, **kw),
                        rd if reads is None else reads, wr if writes is None else writes)

    def tt(self, out, in0, in1, op, eng="vector", reads=None, writes=None):
        return self.add(eng, lambda e: e.tensor_tensor(out=out, in0=in0, in1=in1, op=op),
                        [in0, in1] if reads is None else reads, [out] if writes is None else writes)

    def ts(self, out, in0, s1, op0, s2=None, op1=None, eng="vector", reads=None, writes=None):
        rd = [in0]
        for s in (s1, s2):
            if s is not None and not isinstance(s, (int, float)):
                rd.append(s)
        if op1 is None:
            f = lambda e: e.tensor_scalar(out=out, in0=in0, scalar1=s1, scalar2=None, op0=op0)
        else:
            f = lambda e: e.tensor_scalar(out=out, in0=in0, scalar1=s1, scalar2=s2, op0=op0, op1=op1)
        return self.add(eng, f, rd if reads is None else reads, [out] if writes is None else writes)

    def stt(self, out, in0, scalar, in1, op0, op1, reads=None, writes=None):
        rd = [in0, in1]
        if not isinstance(scalar, (int, float)):
            rd.append(scalar)
        return self.add("vector", lambda e: e.scalar_tensor_tensor(out=out, in0=in0, scalar=scalar, in1=in1, op0=op0, op1=op1),
                        rd if reads is None else reads, [out] if writes is None else writes)

    def copy(self, out, in_, eng="vector", reads=None, writes=None):
        if eng == "scalar":
            return self.add(eng, lambda e: e.activation(out=out, in_=in_, func=AF.Copy), [in_] if reads is None else reads,
                            [out] if writes is None else writes)
        return self.add(eng, lambda e: e.tensor_copy(out=out, in_=in_), [in_] if reads is None else reads,
                        [out] if writes is None else writes)

    def memset(self, ap, val, eng="vector"):
        return self.add(eng, lambda e: e.memset(ap, val), [], [ap])

    def mm(self, out, lhsT, rhs, start, stop, reads=None, writes=None):
        return self.add("tensor", lambda e: e.matmul(out, lhsT=lhsT, rhs=rhs, start=start, stop=stop, skip_group_check=True),
                        [lhsT, rhs] if reads is None else reads, [out] if writes is None else writes)

    def tr(self, out, in_, ident, reads=None, writes=None):
        return self.add("tensor", lambda e: e.transpose(out, in_, ident),
                        [in_, ident] if reads is None else reads, [out] if writes is None else writes)

    def emit(self):
        nc, ops = self.nc, self.ops
        last_w, readers = {}, {}
        deps = [set() for _ in ops]
        for op in ops:
            ds = deps[op.idx]
            if op.eng == "barrier":
                last_w = {k_: v_ for k_, v_ in last_w.items() if k_ in self.bg}
                readers = {}
                continue
            for b in op.reads:
                if b in last_w:
                    ds.add(last_w[b])
            for b in op.writes:
                if b in last_w:
                    ds.add(last_w[b])
                for r in readers.get(b, ()):
                    ds.add(r)
            for b in op.reads:
                readers.setdefault(b, []).append(op.idx)
            for b in op.writes:
                last_w[b] = op.idx
                readers[b] = []
            ds.discard(op.idx)

        def needs(p, c):
            if p.dma:
                return True
            if p.eng != c.eng:
                return True
            return p.eng != "tensor"

        need_sig = set()
        for op in ops:
            for d in deps[op.idx]:
                if needs(ops[d], op):
                    need_sig.add(d)
        last_on = {}
        for op in ops:
            if op.eng == "barrier":
                need_sig.update(last_on.values())
                continue
            if op.dma:
                need_sig.add(op.idx)
            else:
                last_on[op.eng] = op.idx
        eng_cnt = {e: 0 for e in ENGS}
        dma_cnt, dma_keys, sigval = {}, [], {}
        bar_snap = {}
        for op in ops:
            if op.eng == "barrier":
                snap = {("eng", e): v for e, v in eng_cnt.items() if v > 0}
                snap.update({k_: v_ for k_, v_ in dma_cnt.items() if k_[1] not in self.bg})
                bar_snap[op.idx] = snap
                continue
            if op.idx not in need_sig:
                continue
            if op.dma:
                k = ("dma", op.writes[0] if op.writes else op.reads[0])
                if k not in dma_cnt:
                    dma_cnt[k] = 0
                    dma_keys.append(k)
                dma_cnt[k] += 16
                op.sig = (k, 16)
                sigval[op.idx] = (k, dma_cnt[k])
            else:
                k = ("eng", op.eng)
                eng_cnt[op.eng] += 1
                op.sig = (k, 1)
                sigval[op.idx] = (k, eng_cnt[op.eng])
        seen = {e: {} for e in ENGS}
        pend = {e: {} for e in ENGS}
        for op in ops:
            if op.eng == "barrier":
                for e in ENGS:
                    for k, v in bar_snap[op.idx].items():
                        if pend[e].get(k, 0) < v:
                            pend[e][k] = v
                continue
            if pend[op.eng]:
                for k, v in pend[op.eng].items():
                    if seen[op.eng].get(k, 0) < v and op.waits.get(k, 0) < v:
                        op.waits[k] = v
                pend[op.eng] = {}
            for d in deps[op.idx]:
                if d not in sigval or not needs(ops[d], op):
                    continue
                k, v = sigval[d]
                if seen[op.eng].get(k, 0) >= v:
                    continue
                if op.waits.get(k, 0) < v:
                    op.waits[k] = v
            for k, v in op.waits.items():
                seen[op.eng][k] = v
        self.stats = dict(n_ops=len(ops), n_dma_sems=len(dma_keys), eng_cnt=eng_cnt)
        with contextlib.ExitStack() as st:
            sems = {}
            for e in ENGS:
                sems[("eng", e)] = st.enter_context(nc.semaphore("s_" + e))
            for i, k in enumerate(dma_keys):
                sems[k] = st.enter_context(nc.semaphore("d%d" % i))
            block = st.enter_context(nc.Block())
            by_eng = {e: [o for o in ops if o.eng == e] for e in ENGS}
            self.stats["per_eng"] = {e: len(v) for e, v in by_eng.items()}

            def make(e):
                def body(eng):
                    for op in by_eng[e]:
                        for k, v in op.waits.items():
                            eng.wait_ge(sems[k], v)
                        ins = op.fn(eng)
                        if op.sig is not None:
                            ins.then_inc(sems[op.sig[0]], op.sig[1])
                    if e == "sync":
                        for k, v in dma_cnt.items():
                            eng.wait_ge(sems[k], v)
                        for e2 in ENGS:
                            if e2 != "sync" and eng_cnt[e2] > 0:
                                eng.wait_ge(sems[("eng", e2)], eng_cnt[e2])
                return body

            for e in ENGS:
                getattr(block, e)(make(e))
        return self


def _st_layout(a):
    return np.ascontiguousarray(a.reshape(32, 2, 64).transpose(1, 2, 0).reshape(128, 32))


def _blk_layout(a):
    out = np.zeros((128, 32, 32), np.float32)
    a = a.reshape(32, 2, 64, 16)
    for gl in range(2):
        out[gl * 64:(gl + 1) * 64, :, gl * 16:(gl + 1) * 16] = a[:, gl].transpose(1, 0, 2)
    return out


def host_layout(inp, b):
    f = lambda a: np.ascontiguousarray(a, dtype=np.float32)
    m = {}
    m["x"] = f(inp["x"][b])
    lam = np.stack([_st_layout(f(inp["a_lam_re"][0])), _st_layout(f(inp["a_lam_im"][0])),
                    _st_layout(np.repeat(f(inp["a_log_dt"][0])[:, None], 64, axis=1))], axis=1)
    m["s5par"] = f(lam)
    m["s5b"] = f(np.stack([_blk_layout(f(inp["a_b_re"][0])), _blk_layout(f(inp["a_b_im"][0]))], axis=1))
    m["s5c"] = f(np.stack([_blk_layout(f(inp["a_c_re"][0]).transpose(0, 2, 1)),
                           _blk_layout(f(inp["a_c_im"][0]).transpose(0, 2, 1))], axis=1))
    m["s5d"] = f(inp["a_d"][0].reshape(8, 128).T)
    m["wglu"] = f(inp["a_w_glu"][0])
    gains = np.stack([inp["mix_pre_g"][0], inp["mix_post_g"][0], inp["ffn_pre_g"][0], inp["ffn_post_g"][0],
                      inp["kv_norm_g"], inp["mix_pre_g"][1], inp["mix_post_g"][1], inp["ffn_pre_g"][1],
                      inp["ffn_post_g"][1]], axis=0)
    m["gains"] = f(gains)
    for l in range(2):
        m["win%d" % l] = f(inp["ffn_w_in"][l])
        m["wout%d" % l] = f(inp["ffn_w_out"][l])
        cw = inp["ffn_conv_w"][l].reshape(3, FC, 128).transpose(2, 1, 0)
        cb = inp["ffn_conv_b"][l].reshape(FC, 128).T[:, :, None]
        m["conv%d" % l] = f(np.concatenate([cw, cb], axis=2))
    m["ident"] = np.eye(128, dtype=np.float32)
    return m


class Ctx:
    pass


def strided(ap2d, k, step=8):
    return ap2d.rearrange("p (c k) -> p k c", k=step)[:, k, :]


def emit_norm_A(P, C, src_tile, gain_ap, gain_key, h, idx):
    s = C.stat[idx % 2]
    P.act(C.sq[:], src_tile[:], AF.Square, accum_out=s[:, 0:1])
    P.act(s[:, 1:2], s[:, 0:1], AF.Sqrt, bias=C.epsb[:, 0:1], scale=1.0 / D)
    P.add("vector", lambda e: e.reciprocal(out=s[:, 2:3], in_=s[:, 1:2]), [s], [s])
    P.stt(h[:], src_tile[:], s[:, 2:3], gain_ap, ALU.mult, ALU.mult, reads=[src_tile, s, gain_key])


def emit_norm_B(P, C, h, dstT, col0, idx, ceng="scalar"):
    pT = C.ptr[idx % 2]
    for dc in range(DC):
        P.tr(pT[:, dc, :], h[:, dc * 128:(dc + 1) * 128], C.identb[:])
    P.copy(dstT[:, :, col0:col0 + 128], pT[:], eng=ceng)


def emit_norm_to_T(P, C, src_tile, gain_ap, gain_key, dstT, col0, idx, ceng="scalar"):
    s = C.stat[idx % 2]
    P.act(C.sq[:], src_tile[:], AF.Square, accum_out=s[:, 0:1])
    P.act(s[:, 1:2], s[:, 0:1], AF.Sqrt, bias=C.epsb[:, 0:1], scale=1.0 / D)
    P.add("vector", lambda e: e.reciprocal(out=s[:, 2:3], in_=s[:, 1:2]), [s], [s])
    h = C.hb[idx % 2]
    P.stt(h[:], src_tile[:], s[:, 2:3], gain_ap, ALU.mult, ALU.mult, reads=[src_tile, s, gain_key])
    pT = C.ptr[idx % 2]
    for dc in range(DC):
        P.tr(pT[:, dc, :], h[:, dc * 128:(dc + 1) * 128], C.identb[:])
    P.copy(dstT[:, :, col0:col0 + 128], pT[:], eng=ceng)


def emit_postnorm_residual(P, C, f_tile, res_tile, gain_ap, gain_key, out_tile, idx):
    s = C.stat[idx % 2]
    P.act(C.sq[:], f_tile[:], AF.Square, accum_out=s[:, 0:1])
    P.act(s[:, 1:2], s[:, 0:1], AF.Sqrt, bias=C.epsb[:, 0:1], scale=1.0 / D)
    P.add("vector", lambda e: e.reciprocal(out=s[:, 2:3], in_=s[:, 1:2]), [s], [s])
    P.stt(f_tile[:], f_tile[:], s[:, 2:3], gain_ap, ALU.mult, ALU.mult, reads=[f_tile, s, gain_key])
    P.tt(out_tile[:], f_tile[:], res_tile[:], ALU.add, eng="gpsimd")


def emit_s5(P, C, nc, st_alloc, x_d, s5par_d, s5b_d, s5c_d, s5d_d, gains_d, yT):
    sb, ps = st_alloc
    V, G = "vector", "gpsimd"
    par = sb("s5par", [128, 3, 32])
    P.dma("sync", par[:], s5par_d)
    pp = sb("s5pp", [128, 24, 32])
    ki = sb("s5ki", [128, 32], I32)
    pw = sb("s5pw", [128, 2, 9, 32])
    cF = sb("s5cF", [128, 2, 8, 32])
    zt = sb("s5zt", [128, 2, 9, 32])
    RR = sb("s5RR", [128, 32])
    dsk = sb("s5dsk", [128, 8])
    P.dma("sync", dsk[:], s5d_d)
    LR, LI, LDT = par[:, 0, :], par[:, 1, :], par[:, 2, :]
    sl = lambda i: pp[:, i, :]
    (DT, LRDT, TH, MAG, Q, KF, THR, SH, CH, SIN, COS, ABR, ABI, NR, DEN, RDEN, T1, T2, COEFR, COEFI, INVR) = [sl(i) for i in range(21)]
    P.act(DT, LDT, AF.Exp)
    P.tt(LRDT, LR, DT, ALU.mult)
    P.tt(TH, LI, DT, ALU.mult)
    P.act(MAG, LRDT, AF.Exp)
    P.ts(Q, TH, 1.0 / (2 * math.pi), ALU.mult)
    P.copy(ki[:], Q)
    P.copy(KF, ki[:])
    P.stt(THR, KF, -2 * math.pi, TH, ALU.mult, ALU.add)
    P.act(SH, THR, AF.Sin, scale=0.5)
    P.act(CH, THR, AF.Sin, scale=-0.5, bias=C.halfpi[:, 0:1])
    P.stt(SIN, SH, 2.0, CH, ALU.mult, ALU.mult)
    P.tt(COS, SH, SH, ALU.mult)
    P.ts(COS, COS, -2.0, ALU.mult, 1.0, ALU.add)
    P.tt(ABR, MAG, COS, ALU.mult)
    P.tt(ABI, MAG, SIN, ALU.mult)
    P.ts(NR, ABR, -1.0, ALU.add)
    P.tt(DEN, LR, LR, ALU.mult)
    P.tt(T1, LI, LI, ALU.mult)
    P.tt(DEN, DEN, T1, ALU.add)
    P.add(V, lambda e: e.reciprocal(out=RDEN, in_=DEN), [pp], [pp])
    P.tt(T1, NR, LR, ALU.mult)
    P.tt(T2, ABI, LI, ALU.mult)
    P.tt(T1, T1, T2, ALU.add)
    P.tt(COEFR, T1, RDEN, ALU.mult)
    P.tt(T1, ABI, LR, ALU.mult)
    P.tt(T2, NR, LI, ALU.mult)
    P.tt(T1, T1, T2, ALU.subtract)
    P.tt(COEFI, T1, RDEN, ALU.mult)

    def cmul(outr, outi, ar, ai, br, bi):
        P.tt(T1, ai, bi, ALU.mult)
        P.tt(T2, ar, br, ALU.mult)
        P.tt(outr, T2, T1, ALU.subtract)
        P.tt(T1, ar, bi, ALU.mult)
        P.tt(T2, ai, br, ALU.mult)
        P.tt(outi, T1, T2, ALU.add)

    P.memset(pw[:, 0, 0, :], 1.0)
    P.memset(pw[:, 1, 0, :], 0.0)
    P.copy(pw[:, 0, 1, :], ABR)
    P.copy(pw[:, 1, 1, :], ABI)
    for k in range(1, 8):
        cmul(pw[:, 0, k + 1, :], pw[:, 1, k + 1, :], pw[:, 0, k, :], pw[:, 1, k, :], ABR, ABI)
    for k in range(8):
        cmul(cF[:, 0, k, :], cF[:, 1, k, :], pw[:, 0, 7 - k, :], pw[:, 1, 7 - k, :], COEFR, COEFI)
    P.act(RR[:], LRDT, AF.Exp, scale=8.0)
    P.act(INVR, LRDT, AF.Exp, scale=-8.0)
    P.tt(zt[:, 0, 0, :], pw[:, 0, 8, :], INVR, ALU.mult)
    P.tt(zt[:, 1, 0, :], pw[:, 1, 8, :], INVR, ALU.mult)
    for j in range(8):
        cmul(zt[:, 0, j + 1, :], zt[:, 1, j + 1, :], zt[:, 0, j, :], zt[:, 1, j, :], zt[:, 0, j, :], zt[:, 1, j, :])

    tabA = sb("s5tabA", [128, 2, 32, 16])
    tabB = sb("s5tabB", [128, 2, 32, 32])
    tT1 = sb("s5A1", [128, 512])
    tT2 = sb("s5A2", [128, 512])
    A1, A2 = tT1, tT2
    for tab, nlev, j0 in ((tabA, 4, 0), (tabB, 5, 4)):
        P.memset(tab[:, 0, :, 0:1], 1.0)
        P.memset(tab[:, 1, :, 0:1], 0.0)
        for lv in range(nlev):
            n = 1 << lv
            zr = zt[:, 0, j0 + lv, :].unsqueeze(2).to_broadcast([128, 32, n])
            zi = zt[:, 1, j0 + lv, :].unsqueeze(2).to_broadcast([128, 32, n])
            ar, ai = tab[:, 0, :, 0:n], tab[:, 1, :, 0:n]
            t1 = tT1[:].rearrange("p (a b) -> p a b", b=16)[:, :, 0:n]
            t2 = tT2[:].rearrange("p (a b) -> p a b", b=16)[:, :, 0:n]
            P.tt(t1, ar, zr, ALU.mult, reads=[tab, zt], writes=[tT1])
            P.tt(t2, ai, zi, ALU.mult, reads=[tab, zt], writes=[tT2])
            P.tt(tab[:, 0, :, n:2 * n], t1, t2, ALU.subtract, reads=[tT1, tT2], writes=[tab])
            P.tt(t1, ar, zi, ALU.mult, reads=[tab, zt], writes=[tT1])
            P.tt(t2, ai, zr, ALU.mult, reads=[tab, zt], writes=[tT2])
            P.tt(tab[:, 1, :, n:2 * n], t1, t2, ALU.add, reads=[tT1, tT2], writes=[tab])

    rstd = sb("s5rstd", [128, NT])
    ssq = sb("s5ssq", [128, NT])
    for tt in range(NT):
        xi = C.xin[tt % 2]
        P.dma("sync", xi[:], x_d[tt * 128:(tt + 1) * 128, :])
        P.act(C.sq[:], xi[:], AF.Square, accum_out=ssq[:, tt:tt + 1], writes=[C.sq, ("ssq", tt)])
    P.act(ssq[:], ssq[:], AF.Sqrt, bias=C.epsb[:, 0:1], scale=1.0 / D,
          reads=[("ssq", t) for t in range(NT)] + [C.epsb], writes=["ssq_all"])
    P.add(V, lambda e: e.reciprocal(out=rstd[:], in_=ssq[:]), ["ssq_all"], [rstd])

    g0 = sb("s5g0", [128, D])
    P.dma("sync", g0[:], gains_d[0:1, :].partition_broadcast(128))
    HB = NT // 4
    xblk = sb("s5xblk", [128, HB, 128])
    hblk = sb("s5hblk", [128, HB, 128], BF16)
    hTd = [sb("s5hT%d" % i, [128, 8, T // 8], BF16) for i in range(1)]
    bc = sb("s5bc", [128, 2, 4, 32])
    cc = sb("s5cc", [128, 2, 4, 32])
    tA = sb("s5tA", [128, 9, 32])
    tB = sb("s5tB", [128, 9, 32])
    Fre = [sb("s5Fre%d" % q, [128, 8, 128], BF16) for q in range(4)]
    Fim = [sb("s5Fim%d" % q, [128, 8, 128], BF16) for q in range(4)]
    Ere = [sb("s5Ere%d" % q, [128, 9, 128], BF16) for q in range(4)]
    Eni = [sb("s5Eni%d" % q, [128, 9, 128], BF16) for q in range(4)]
    for q in range(4):
        for t_ in (Fre[q], Fim[q], Ere[q], Eni[q]):
            P.memset(t_[:], 0.0, eng=G)
    FTre = [sb("s5FTre%d" % i, [128, 8, 128], BF16) for i in range(2)]
    FTim = [sb("s5FTim%d" % i, [128, 8, 128], BF16) for i in range(2)]
    crT = [sb("s5cr%d" % i, [128, 512]) for i in range(2)]
    srT = [sb("s5sr%d" % i, [128, 512]) for i in range(2)]
    pt1 = sb("s5pt1", [128, 512])
    pt2 = sb("s5pt2", [128, 512])
    Vpr = sb("s5Vpr", [128, 512])
    Vpi = sb("s5Vpi", [128, 512])
    Wrs = [sb("s5Wr%d" % i, [128, 512]) for i in range(2)]
    Wis = [sb("s5Wi%d" % i, [128, 512]) for i in range(2)]
    Xs = sb("s5Xs", [128, 4, 2, 520], BF16)
    P.add(G, lambda e: e.memset(Xs[:], 0.0), [], [("Xs", q) for q in range(4)])
    KT = sb("s5KT", [128, 8, 128], BF16)
    ytmp = [sb("s5yt%d" % i, [128, 512]) for i in range(1)] * 2
    pvr, pvi, pk0, pk1, py0, py1 = C.pb[0], C.pb[1], C.pb[2], C.pb[3], C.pb[4], C.pb[5]
    xview = x_d.rearrange("(t p) (dc c) -> p t dc c", p=128, c=128)

    for dc in range(DC):
        hT = hTd[0]
        for hf in range(4):
            P.dma("sync", xblk[:], xview[:, hf * HB:(hf + 1) * HB, dc, :])
            P.tt(xblk[:], xblk[:], rstd[:, hf * HB:(hf + 1) * HB].unsqueeze(2).to_broadcast([128, HB, 128]), ALU.mult)
            P.tt(hblk[:], xblk[:], g0[:, dc * 128:(dc + 1) * 128].unsqueeze(1).to_broadcast([128, HB, 128]), ALU.mult)
            for t8 in range(HB // 8):
                pT = C.ptr[t8 % 2]
                for j in range(8):
                    P.tr(pT[:, j, :], hblk[:, t8 * 8 + j, :], C.identb[:])
                c0_ = (hf * HB + t8 * 8) * 16
                P.copy(hT[:, :, c0_:c0_ + 128].rearrange("p k c -> p c k"),
                       pT[:].rearrange("p a b -> p (a b)").rearrange("p (c k) -> p c k", k=8), eng="scalar")
        P.dma("sync", bc[:], s5b_d[:, :, dc * 4:(dc + 1) * 4, :])
        P.dma("sync", cc[:], s5c_d[:, :, dc * 4:(dc + 1) * 4, :])
        def stage1(q, dc=dc, hT=hT):
            st = dc * 4 + q
            co = 32 * q
            pvr, pvi = C.pb[2 * (q % 2)], C.pb[2 * (q % 2) + 1]
            cFr = cF[:, 0, :, st:st + 1].to_broadcast([128, 8, 32])
            cFi = cF[:, 1, :, st:st + 1].to_broadcast([128, 8, 32])
            Br = bc[:, 0, q:q + 1, :].to_broadcast([128, 8, 32])
            Bi = bc[:, 1, q:q + 1, :].to_broadcast([128, 8, 32])
            a8, b8 = tA[:, 0:8, :], tB[:, 0:8, :]
            P.tt(a8, Br, cFr, ALU.mult, reads=[bc, cF], writes=[tA])
            P.tt(b8, Bi, cFi, ALU.mult, reads=[bc, cF], writes=[tB])
            P.tt(Fre[q][:, :, co:co + 32], a8, b8, ALU.subtract, reads=[tA, tB], writes=[Fre[q]])
            P.tt(a8, Br, cFi, ALU.mult, reads=[bc, cF], writes=[tA])
            P.tt(b8, Bi, cFr, ALU.mult, reads=[bc, cF], writes=[tB])
            P.tt(Fim[q][:, :, co:co + 32], a8, b8, ALU.add, reads=[tA, tB], writes=[Fim[q]])
            pr = pw[:, 0, :, st:st + 1].to_broadcast([128, 9, 32])
            pi = pw[:, 1, :, st:st + 1].to_broadcast([128, 9, 32])
            Cr = cc[:, 0, q:q + 1, :].to_broadcast([128, 9, 32])
            Ci = cc[:, 1, q:q + 1, :].to_broadcast([128, 9, 32])
            P.tt(tA[:], Cr, pr, ALU.mult, reads=[cc, pw], writes=[tA])
            P.tt(tB[:], Ci, pi, ALU.mult, reads=[cc, pw], writes=[tB])
            P.tt(Ere[q][:, :, co:co + 32], tA[:], tB[:], ALU.subtract, reads=[tA, tB], writes=[Ere[q]])
            P.tt(tA[:], Cr, pi, ALU.mult, reads=[cc, pw], writes=[tA])
            P.tt(tB[:], Ci, pr, ALU.mult, reads=[cc, pw], writes=[tB])
            P.stt(Eni[q][:, :, co:co + 32], tA[:], -1.0, tB[:], ALU.mult, ALU.subtract, reads=[tA, tB], writes=[Eni[q]])
            ftr, fti = FTre[q % 2], FTim[q % 2]
            for k in range(8):
                P.tr(C.ptr[0][:, k, :], Fre[q][:, k, :], C.identb[:])
            P.copy(ftr[:], C.ptr[0][:], eng="scalar")
            for k in range(8):
                P.tr(C.ptr[1][:, k, :], Fim[q][:, k, :], C.identb[:])
            P.copy(fti[:], C.ptr[1][:], eng="scalar")
            for k in range(8):
                P.mm(pvr[:], ftr[:, k, :], hT[:, k, :], k == 0, k == 7)
            for k in range(8):
                P.mm(pvi[:], fti[:, k, :], hT[:, k, :], k == 0, k == 7)
            cr, sr = crT[q % 2], srT[q % 2]
            Br = tabB[:, 0, st, :].unsqueeze(2).to_broadcast([128, 32, 16])
            Bi = tabB[:, 1, st, :].unsqueeze(2).to_broadcast([128, 32, 16])
            Ar = tabA[:, 0, st, :].unsqueeze(1).to_broadcast([128, 32, 16])
            Ai = tabA[:, 1, st, :].unsqueeze(1).to_broadcast([128, 32, 16])
            v3 = lambda t_: t_[:].rearrange("p (a b) -> p a b", b=16)
            P.tt(v3(pt1), Br, Ar, ALU.mult, eng=G, reads=[tabA, tabB], writes=[pt1])
            P.tt(v3(pt2), Bi, Ai, ALU.mult, eng=G, reads=[tabA, tabB], writes=[pt2])
            P.tt(cr[:], pt1[:], pt2[:], ALU.subtract, eng=G)
            P.tt(v3(pt1), Br, Ai, ALU.mult, eng=G, reads=[tabA, tabB], writes=[pt1])
            P.tt(v3(pt2), Bi, Ar, ALU.mult, eng=G, reads=[tabA, tabB], writes=[pt2])
            P.tt(sr[:], pt1[:], pt2[:], ALU.add, eng=G)
        def stage2(q, dc=dc):
            st = dc * 4 + q
            pvr, pvi = C.pb[2 * (q % 2)], C.pb[2 * (q % 2) + 1]
            cr, sr = crT[q % 2], srT[q % 2]
            Wr, Wi = Wrs[q % 2], Wis[q % 2]
            P.tt(A1[:], pvr[:], cr[:], ALU.mult)
            P.tt(A2[:], pvi[:], sr[:], ALU.mult)
            P.tt(Vpr[:], A1[:], A2[:], ALU.add)
            P.tt(A1[:], pvi[:], cr[:], ALU.mult)
            P.tt(A2[:], pvr[:], sr[:], ALU.mult)
            P.tt(Vpi[:], A1[:], A2[:], ALU.subtract)
            Rb = RR[:, st:st + 1].to_broadcast([128, 512])
            P.add(V, lambda e, Rb=Rb: e.tensor_tensor_scan(out=Wr[:], data0=Rb, data1=Vpr[:], initial=0.0, op0=ALU.mult, op1=ALU.add),
                  [RR, Vpr], [Wr])
            P.add(V, lambda e, Rb=Rb: e.tensor_tensor_scan(out=Wi[:], data0=Rb, data1=Vpi[:], initial=0.0, op0=ALU.mult, op1=ALU.add),
                  [RR, Vpi], [Wi])
            P.tt(pt1[:], Wr[:], cr[:], ALU.mult, eng=G)
            P.tt(pt2[:], Wi[:], sr[:], ALU.mult, eng=G)
            P.tt(Xs[:, q, 0, 1:513], pt1[:], pt2[:], ALU.subtract, eng=G, writes=[("Xs", q)])
            P.tt(pt1[:], Wi[:], cr[:], ALU.mult, eng=G)
            P.tt(pt2[:], Wr[:], sr[:], ALU.mult, eng=G)
            P.tt(Xs[:, q, 1, 1:513], pt1[:], pt2[:], ALU.add, eng=G, writes=[("Xs", q)])
        stage1(0)
        for q in range(4):
            if q < 3:
                stage1(q + 1)
            stage2(q)
            P.pump(1)
        for tau in range(8):
            pk = (pk0, pk1)[tau // 4]
            o = pk[:, (tau % 4) * 128:(tau % 4 + 1) * 128]
            n = 0
            for q in range(4):
                P.mm(o, Fre[q][:, 7 - tau, :], Ere[q][:, 0, :], n == 0, False)
                n += 1
                P.mm(o, Fim[q][:, 7 - tau, :], Eni[q][:, 0, :], False, n == 7)
                n += 1
        P.copy(KT[:, 0:4, :].rearrange("p a b -> p (a b)"), pk0[:], eng="scalar")
        P.copy(KT[:, 4:8, :].rearrange("p a b -> p (a b)"), pk1[:], eng="scalar")
        for k in range(8):
            py = (py0, py1)[k % 2]
            n_mm = 8 + k + 1
            n = 0
            for q in range(4):
                P.mm(py[:], Ere[q][:, k + 1, :], Xs[:, q, 0, 0:512], n == 0, False, reads=[Ere[q], ("Xs", q)])
                n += 1
                P.mm(py[:], Eni[q][:, k + 1, :], Xs[:, q, 1, 0:512], False, False, reads=[Eni[q], ("Xs", q)])
                n += 1
            for k2 in range(k + 1):
                n += 1
                P.mm(py[:], KT[:, k - k2, :], hT[:, k2, :], False, n == n_mm)
            yt = ytmp[k % 2]
            P.stt(yt[:], hT[:, k, :], dsk[:, dc:dc + 1], py[:], ALU.mult, ALU.add)
            P.act(strided(yT[:, dc, :], k), yt[:], AF.Gelu_apprx_tanh, writes=[("yT", dc)])


def emit_glu(P, C, sb, x_d, wglu_d, gains_d, yT, xmid_d, dbg_m=None):
    wglu = sb("wglu", [128, DC, 2 * D], BF16)
    wv = wglu_d.rearrange("(dc p) n -> p dc n", p=128)
    for dc in range(DC):
        P.dma("gpsimd", wglu[:, dc, :], wv[:, dc, :], writes=[("wglu", dc)])
    g1 = sb("glu_g", [128, D])
    P.dma("sync", g1[:], gains_d[1:2, :].partition_broadcast(128))
    sig = sb("glu_sig", [128, D])
    mt = [sb("glu_m%d" % i, [128, D]) for i in range(2)]
    xo = [sb("glu_xo%d" % i, [128, D]) for i in range(2)]
    banks = [C.pb[0], C.pb[1], C.pb[2], C.pb[3]]
    for tt in range(NT):
        xi = C.xin[tt % 2]
        P.dma("sync", xi[:], x_d[tt * 128:(tt + 1) * 128, :])
        for nch in range(4):
            for dc in range(DC):
                P.mm(banks[nch][:], yT[:, dc, tt * 128:(tt + 1) * 128], wglu[:, dc, nch * 512:(nch + 1) * 512],
                     dc == 0, dc == DC - 1, reads=[("yT", dc), ("wglu", dc)])
        m = mt[tt % 2]
        for hh in range(2):
            P.act(sig[:, hh * 512:(hh + 1) * 512], banks[2 + hh][:], AF.Sigmoid, writes=[("sig", hh)])
            P.tt(m[:, hh * 512:(hh + 1) * 512], banks[hh][:], sig[:, hh * 512:(hh + 1) * 512], ALU.mult,
                 reads=[banks[hh], ("sig", hh)], writes=[m])
        if dbg_m is not None:
            P.dma("sync", dbg_m[tt * 128:(tt + 1) * 128, :], m[:])
        o = xo[tt % 2]
        emit_postnorm_residual(P, C, m, xi, g1[:], g1, o, tt)
        P.dma("sync", xmid_d[tt * 128:(tt + 1) * 128, :], o[:])


def emit_ffn(P, C, sb, layer, src_d, dst_d, win_d, wout_d, conv_d, gains_d, gi_pre, gi_post, wq_eng="sync"):
    L = "f%d_" % layer
    win = sb(L + "win", [128, DC, 2 * FF], BF16)
    wout = sb(L + "wout", [128, FC, D], BF16)
    wiv = win_d.rearrange("(dc p) n -> p dc n", p=128)
    wov = wout_d.rearrange("(fc p) n -> p fc n", p=128)
    HF = FF // 2

    def load_weights():
        n_ = 0
        for cg in range(2):
            for hh in range(2):
                c0 = hh * FF + cg * HF
                P.dma(("sync", "scalar")[n_ % 2], win[:, :, c0:c0 + HF], wiv[:, :, c0:c0 + HF], writes=[(L + "win", hh, cg)])
                n_ += 1
        for hf_ in range(2):
            P.dma(("sync", "scalar")[hf_], wout[:, hf_ * 11:(hf_ + 1) * 11, :], wov[:, hf_ * 11:(hf_ + 1) * 11, :], writes=[(L + "wout", hf_)])
    cv = sb(L + "conv", [128, FC, 4])
    P.dma("sync", cv[:], conv_d)
    gpre = sb(L + "gpre", [128, D])
    gpost = sb(L + "gpost", [128, D])
    P.dma("sync", gpre[:], gains_d[gi_pre:gi_pre + 1, :].partition_broadcast(128))
    P.dma("sync", gpost[:], gains_d[gi_post:gi_post + 1, :].partition_broadcast(128))
    h2Ts = [sb(L + "h2T%d" % i, [128, DC, 512], BF16) for i in range(1)] * 2
    hbx = [C.hb[0], C.hb[1], sb(L + "hb2", [128, D], BF16), sb(L + "hb3", [128, D], BF16)]
    uT = sb(L + "uT", [128, FC, 512], BF16)
    gprev = sb(L + "gprev", [128, FC, 2])
    P.add("vector", lambda e: e.memset(gprev[:], 0.0), [], [(L + "gprev", fc) for fc in range(FC)])
    gbuf = [sb(L + "gbuf%d" % i, [128, 516]) for i in range(1)]
    cva = [sb(L + "cva%d" % i, [128, 512]) for i in range(1)]
    cvb = [sb(L + "cvb%d" % i, [128, 512]) for i in range(1)]
    xr = sb(L + "xr", [128, D])
    ft = [sb(L + "ft%d" % i, [128, D]) for i in range(1)]
    ot = [xr]
    pg = [C.pb[0], C.pb[1]]
    pv = [C.pb[2], C.pb[3]]
    po = [C.pb[4], C.pb[5]]
    NCH = T // 512

    def normA(ch):
        for j in range(4):
            tt = ch * 4 + j
            P.dma("sync", C.xin[j % 2][:], src_d[tt * 128:(tt + 1) * 128, :])
            emit_norm_A(P, C, C.xin[j % 2], gpre[:], gpre, hbx[j], tt)

    def normB(ch):
        for j in range(4):
            emit_norm_B(P, C, hbx[j], h2Ts[ch % 2], j * 128, ch * 4 + j)

    normA(0)
    load_weights()
    normB(0)
    for ch in range(NCH):
        h2T = h2Ts[ch % 2]
        for fc in range(FC):
            g_, v_ = pg[fc % 2], pv[fc % 2]
            for dc in range(DC):
                P.mm(g_[:], win[:, dc, fc * 128:(fc + 1) * 128], h2T[:, dc, :], dc == 0, dc == DC - 1,
                     reads=[(L + "win", 0, fc // 11), h2T])
            for dc in range(DC):
                P.mm(v_[:], win[:, dc, FF + fc * 128:FF + (fc + 1) * 128], h2T[:, dc, :], dc == 0, dc == DC - 1,
                     reads=[(L + "win", 1, fc // 11), h2T])
            gb = gbuf[0]
            P.copy(gb[:, 0:2], gprev[:, fc, :], eng="gpsimd", reads=[(L + "gprev", fc)], writes=[gb])
            P.copy(gb[:, 2:514], g_[:], eng="scalar")
            P.copy(gprev[:, fc, :], gb[:, 512:514], eng="gpsimd", reads=[gb], writes=[(L + "gprev", fc)])
            ca, cb_ = cva[0], cvb[0]
            P.ts(ca[:], gb[:, 0:512], cv[:, fc, 0:1], ALU.mult, cv[:, fc, 3:4], ALU.add)
            P.stt(cb_[:], gb[:, 1:513], cv[:, fc, 1:2], ca[:], ALU.mult, ALU.add)
            P.stt(ca[:], gb[:, 2:514], cv[:, fc, 2:3], cb_[:], ALU.mult, ALU.add)
            P.act(cb_[:], ca[:], AF.Gelu_apprx_tanh)
            P.tt(uT[:, fc, :], cb_[:], v_[:], ALU.mult, writes=[(L + "uT", fc)])
        P.pump(4)
        if ch + 1 < NCH:
            normA(ch + 1)
        for j in range(4):
            tt = ch * 4 + j
            for nch in range(2):
                for fc in range(FC):
                    P.mm(po[nch][:], uT[:, fc, j * 128:(j + 1) * 128], wout[:, fc, nch * 512:(nch + 1) * 512],
                         fc == 0, fc == FC - 1, reads=[(L + "uT", fc), (L + "wout", fc // 11)])
            f = ft[0]
            P.dma("sync", xr[:], src_d[tt * 128:(tt + 1) * 128, :])
            P.copy(f[:, 0:512], po[0][:], eng="scalar", writes=[f])
            P.copy(f[:, 512:1024], po[1][:], eng="scalar", writes=[f])
            o = ot[0]
            emit_postnorm_residual(P, C, f, xr, gpost[:], gpost, o, tt)
            P.dma("sync", dst_d[tt * 128:(tt + 1) * 128, :], o[:])
        if ch + 1 < NCH:
            normB(ch + 1)
NCMP = 255
THETA = 500000.0


def host_consts():
    c = {}
    k = np.arange(T)
    c["ind"] = (np.arange(64)[:, None] == (k[None, :] // 64)).astype(np.float32)
    inv = THETA ** (-np.arange(8, dtype=np.float32) / 8.0)
    ang = k.astype(np.float32)[:, None] * inv[None, :]
    cs = np.stack([np.cos(ang), np.sin(ang)], 0).astype(np.float32)
    c["ropek"] = np.ascontiguousarray(cs.reshape(2, NT, 128, 8).transpose(2, 0, 1, 3))
    pc = (np.arange(256) * 16 + 31).astype(np.float32)
    angc = pc[:, None] * inv[None, :]
    csc = np.stack([np.cos(angc), np.sin(angc)], 0).astype(np.float32)
    c["ropec"] = np.ascontiguousarray(csc.reshape(2, 2, 128, 8).transpose(2, 0, 1, 3))
    n = np.arange(256)
    j = np.arange(64)
    ov = ((n[:, None] * 16 < (j[None, :] + 1) * 64) & (n[:, None] * 16 + 32 > j[None, :] * 64) & (n[:, None] < NCMP))
    c["ovl"] = np.ascontiguousarray(ov.astype(np.float32).reshape(2, 128, 64).transpose(1, 0, 2))
    t = np.arange(T)
    cm = np.where((n[:, None] * 16 + 31 <= t[None, :]) & (n[:, None] < NCMP), 0.0, NEGM).astype(np.float32)
    c["cmask"] = np.ascontiguousarray(cm.reshape(2, 128, NT, 128).transpose(2, 1, 0, 3))
    kp = np.arange(128)
    causal = np.where(kp[:, None] <= kp[None, :], 0.0, NEGM)
    strict = np.where(kp[:, None] > kp[None, :], 0.0, NEGM)
    c["dmask"] = np.ascontiguousarray(np.stack([causal, strict], 1).astype(np.float32))
    cur = t // 64
    valid = j[None, :] <= cur[:, None]
    forced = (j[None, :] == 0) | (j[None, :] == cur[:, None]) | (j[None, :] == cur[:, None] - 1)
    vm = (valid & ~forced).astype(np.float32)
    fb = np.where(forced, 1e4, np.where(valid, 0.0, -1e4)).astype(np.float32)
    sel = np.stack([vm, fb], 1)
    c["selc"] = np.ascontiguousarray(sel.reshape(NT, 128, 2, 64))
    return c


def host_layout_l1(inp, m):
    f = lambda a: np.ascontiguousarray(a, dtype=np.float32)
    m["wkv"] = f(inp["w_kv"])
    m["wq"] = f(inp["b_w_q"][0])
    m["wo"] = f(inp["b_w_o"][0])
    w1 = np.stack([inp["cmp_k_w1"], inp["cmp_v_w1"]], 0).reshape(2, 32, 64, 128).transpose(2, 0, 1, 3)
    m["w1d"] = f(np.concatenate([w1, w1], 0))
    m["w2"] = f(np.stack([inp["cmp_k_w2"], inp["cmp_v_w2"]], 1))
    pe = np.stack([inp["cmp_pe_k"].T, inp["cmp_pe_v"].T], 1)
    m["peT"] = f(pe)
    m.update(host_consts())
    return m


def rope_ops(P, src, dst, cos, sin, ta, tb, nh, skey, dkey):
    cb = cos.unsqueeze(1).to_broadcast([128, nh, 8])
    sb_ = sin.unsqueeze(1).to_broadcast([128, nh, 8])
    x1, x2 = src[:, :, 0:8], src[:, :, 8:16]
    a, b = ta[:, 0:nh, :], tb[:, 0:nh, :]
    P.tt(a, x1, cb, ALU.mult, reads=[skey, "ropetab"], writes=[ta])
    P.tt(b, x2, sb_, ALU.mult, reads=[skey, "ropetab"], writes=[tb])
    P.tt(dst[:, :, 0:8], a, b, ALU.subtract, reads=[ta, tb], writes=[dkey])
    P.tt(a, x2, cb, ALU.mult, reads=[skey, "ropetab"], writes=[ta])
    P.tt(b, x1, sb_, ALU.mult, reads=[skey, "ropetab"], writes=[tb])
    P.tt(dst[:, :, 8:16], a, b, ALU.add, reads=[ta, tb], writes=[dkey])


def emit_kv(P, C, sb, KV, x1_d, D_):
    G = "gpsimd"
    wkv = sb("wkv", [128, DC, 1536], BF16)
    wv = D_["wkv"].rearrange("(dc p) n -> p dc n", p=128)
    for dc in range(DC):
        P.dma(G, wkv[:, dc, :], wv[:, dc, :], writes=[("wkv", dc)])
    gk = sb("kv_g", [128, D])
    P.dma("sync", gk[:], D_["gains"][4:5, :].partition_broadcast(128))
    ropec = sb("ropec", [128, 2, 2, 8])
    P.dma("sync", ropec[:], D_["ropec"], writes=["ropetab"])
    for g in range(4):
        P.dma(G, KV.Kaug[64:128, g, :], D_["ind"], writes=[("Kaug_ind", g)])
    P.dma(G, KV.Ov[:], D_["ovl"])
    P.add("vector", lambda e: e.memset(KV.Vs[:], 1.0), [], ["Vs"])
    P.add("vector", lambda e: e.memset(KV.Vw[:], 1.0), [], ["Vw"])
    P.add("vector", lambda e: e.memset(KV.Vc[:], 1.0), [], ["Vc"])
    P.add("vector", lambda e: e.memset(KV.KcT[:], 0.0), [], ["KcT"])
    P.add("gpsimd", lambda e: e.memset(KV.Kw[64:128, :, :], 0.0), [], ["Kw_pad"])
    zT = sb("kv_zT", [128, 2, 2, 16, T // 16], BF16)
    kfs = [sb("kv_kf%d" % i, [128, 4, 64]) for i in range(2)]
    kbs = [sb("kv_kb%d" % i, [128, 4, 64], BF16) for i in range(2)]
    tas = [sb("kv_ta%d" % i, [128, 4, 8]) for i in range(2)]
    tbs = [sb("kv_tb%d" % i, [128, 4, 8]) for i in range(2)]
    kf, kb, ta, tb = kfs[0], kbs[0], tas[0], tbs[0]
    chunk_scope = contextlib.ExitStack()
    sb_outer = sb
    sb = lambda name, shape, dt=F32: chunk_scope.enter_context(KV.nc.sbuf_tensor("S_" + name, list(shape), dt))
    sT = sb("kv_sT", [128, DC, 512], BF16)
    hbx = [C.hb[0], C.hb[1], sb("kv_hb2", [128, D], BF16), sb("kv_hb3", [128, D], BF16)]
    NCH = T // 512
    pend = []

    def normA(ch):
        for j in range(4):
            tt = ch * 4 + j
            P.dma("sync", C.xin[j % 2][:], x1_d[tt * 128:(tt + 1) * 128, :])
            emit_norm_A(P, C, C.xin[j % 2], gk[:], gk, hbx[j], tt)

    def normB(ch):
        for j in range(4):
            emit_norm_B(P, C, hbx[j], sT, j * 128, ch * 4 + j)

    normA(0)
    normB(0)
    for ch in range(NCH):
        for cc_ in range(4):
            kv, gp = cc_ // 2, cc_ % 2
            pz = C.pb[cc_ % 2]
            for dc in range(DC):
                P.mm(pz[:], wkv[:, dc, cc_ * 128:(cc_ + 1) * 128], sT[:, dc, :], dc == 0, dc == DC - 1, reads=[("wkv", dc), sT])
            P.copy(zT[:, gp, kv, :, ch * 32:(ch + 1) * 32].rearrange("p s n -> p n s"),
                   pz[:].rearrange("p (n s) -> p n s", s=16), eng="scalar", writes=[zT])
        if ch + 1 < NCH:
            normA(ch + 1)
        for j in range(4):
            tt = ch * 4 + j
            for br in range(2):
                pk = C.pb[2 + (2 * j + br) % 2]
                kf, kb, ta, tb = kfs[br], kbs[br], tas[br], tbs[br]
                for dc in range(DC):
                    P.mm(pk[:], sT[:, dc, j * 128:(j + 1) * 128], wkv[:, dc, 512 * (br + 1):512 * (br + 2)], dc == 0, dc == DC - 1,
                         reads=[sT, ("wkv", dc)])
                while pend:
                    pend.pop(0)()
                Vdst = KV.Vs if br == 0 else KV.Vw
                P.copy(Vdst[:, tt, :, 0:64], pk[:, 256:512].rearrange("p (g d) -> p g d", g=4), eng="scalar",
                       writes=["Vs" if br == 0 else "Vw"])
                P.copy(kf[:], pk[:, 0:256].rearrange("p (g d) -> p g d", g=4), eng="scalar")
                P.copy(kb[:], kf[:], eng="gpsimd")
                rope_ops(P, kf, kb, KV.ropek[:, 0, tt, :], KV.ropek[:, 1, tt, :], ta, tb, 4, kf, kb)

                def trans(kb=kb, br=br, tt=tt):
                    pT = C.ptr[0]
                    for g in range(4):
                        P.tr(pT[0:64, g, :], kb[:, g, :], C.identb[:])
                    Kdst = KV.Kaug if br == 0 else KV.Kw
                    P.copy(Kdst[0:64, :, tt * 128:(tt + 1) * 128], pT[0:64, 0:4, :], eng="scalar",
                           writes=["Kaug" if br == 0 else "Kw"])
                pend.append(trans)
        while pend:
            pend.pop(0)()
        if ch + 1 < NCH:
            normB(ch + 1)
    chunk_scope.close()
    P.barrier()
    sb = sb_outer
    w1d = sb("w1d", [128, 2, 32, 128], BF16)
    P.dma(G, w1d[:], D_["w1d"])
    w2 = sb("w2", [128, 2, 64], BF16)
    P.dma(G, w2[:], D_["w2"])
    peT = sb("peT", [64, 2, 32], BF16)
    P.dma(G, peT[:], D_["peT"])
    cbias = sb("kv_cbias", [128, 2])
    hidT = sb("kv_hidT", [128, 256], BF16)
    P.memset(hidT[:], 0.0)
    for kv in range(2):
        for l in range(32):
            P.mm(C.pb[0][:, kv:kv + 1], w1d[0:64, kv, l, :], peT[:, kv, l:l + 1], l == 0, l == 31)
    P.copy(cbias[:], C.pb[0][:, 0:2])
    for kv in range(2):
        for g in range(4):
            kf, kb, ta, tb = kfs[g % 2], kbs[g % 2], tas[g % 2], tbs[g % 2]
            gp, base = g // 2, (g % 2) * 64
            ph = C.pb[(kv * 4 + g) % 2]
            zrow = zT[base:base + 64, gp, kv, :, :]
            for l in range(32):
                rhs = zrow[:, l, 0:NCMP] if l < 16 else zrow[:, l - 16, 1:NCMP + 1]
                P.mm(ph[:, 0:NCMP], w1d[base:base + 64, kv, l, :], rhs, l == 0, l == 31)
            P.act(hidT[:, 0:NCMP], ph[:, 0:NCMP], AF.Gelu_apprx_tanh, bias=cbias[:, kv:kv + 1])
            for nt_ in range(2):
                po = C.pb[2 + nt_]
                P.mm(po[:, 0:64], hidT[:, nt_ * 128:(nt_ + 1) * 128], w2[:, kv, :], True, True)
                if kv == 1:
                    P.copy(KV.Vc[:, nt_, g, 0:64], po[:, 0:64], eng="scalar", writes=["Vc"])
                else:
                    P.copy(kf[:, 0:1, :], po[:, 0:64].unsqueeze(1), eng="scalar", writes=[kf])
                    P.copy(kb[:, 0:1, :], kf[:, 0:1, :], eng="gpsimd", reads=[kf], writes=[kb])
                    rope_ops(P, kf[:, 0:1, :], kb[:, 0:1, :], ropec[:, 0, nt_, :], ropec[:, 1, nt_, :], ta, tb, 1, kf, kb)
                    pT = C.ptr[nt_]
                    P.tr(pT[0:64, 0, :], kb[:, 0, :], C.identb[:])
                    P.copy(KV.KcT[0:64, g, nt_ * 128:(nt_ + 1) * 128], pT[0:64, 0, :], eng="scalar", writes=["KcT"])


def emit_attn(P, C, sb, KV, x1_d, xmid_d, D_, dbg_m2=None, nc=None):
    G = "gpsimd"
    wq = sb("wq", [128, DC, 1072], BF16)
    wo = sb("wo", [128, DC, D], BF16)
    wqv = D_["wq"].rearrange("(dc p) n -> p dc n", p=128)
    wov = D_["wo"].rearrange("(dc p) n -> p dc n", p=128)
    for dc in range(DC):
        P.dma(G, wq[:, dc, :], wqv[:, dc, :])
        P.dma(G, wo[:, dc, :], wov[:, dc, :])
    gpre = sb("at_gpre", [128, D])
    gpost = sb("at_gpost", [128, D])
    P.dma("sync", gpre[:], D_["gains"][5:6, :].partition_broadcast(128))
    P.dma("sync", gpost[:], D_["gains"][6:7, :].partition_broadcast(128))
    dmask = sb("at_dmask", [128, 2, 128], BF16)
    P.dma(G, dmask[:], D_["dmask"])
    cmask = [sb("at_cmask%d" % i, [128, 2, 128], BF16) for i in range(2)]
    cmx = [sb("at_cmx%d" % i, [128, 2, 512], BF16) for i in range(2)]
    dmx = sb("at_dmx", [128, 2, 512], BF16)
    for i_ in range(2):
        P.copy(dmx[:, i_, :].rearrange("p (r t) -> p r t", r=4), dmask[:, i_:i_ + 1, :].to_broadcast([128, 4, 128]),
               reads=[dmask], writes=[dmx])
    selc = [sb("at_selc%d" % i, [128, 2, 64]) for i in range(2)]
    hqT = sb("at_hqT", [128, DC, 128], BF16)
    qb = sb("at_qb", [128, 16, 64], BF16)
    ta = sb("at_ta", [128, 16, 8])
    tb = sb("at_tb", [128, 16, 8])
    gt = [sb("at_gt%d" % i, [128, 16, 3]) for i in range(3)]
    Qaug = [sb("at_Qaug%d" % i, [128, 4, 512], BF16) for i in range(2)]
    for i_ in range(2):
        P.add("vector", lambda e, i_=i_: e.memset(Qaug[i_][:], 0.0), [],
              [("Qq", i_, g_) for g_ in range(4)] + [("Qn", i_, g_) for g_ in range(4)])
    Pexp = [sb("at_Pexp%d" % i, [128, 512], BF16) for i in range(3)]
    rdc = sb("at_rdc", [128, 4])
    rdw = sb("at_rdw", [128, 4])
    rds = sb("at_rds", [128, 4])
    imp = sb("at_imp", [128, 64])
    imt = sb("at_imt", [128, 4, 64])
    score = sb("at_score", [128, 64])
    score2 = sb("at_score2", [128, 64])
    m8a = sb("at_m8a", [128, 8])
    m8b = sb("at_m8b", [128, 8])
    thr = sb("at_thr", [128, 1])
    nmt = [sb("at_nmt%d" % i, [128, 128], BF16) for i in range(4)]
    for t_ in nmt:
        P.memset(t_[:], 0.0)
    ocw = [sb("at_ocw%d" % i, [128, 16, 64], BF16) for i in range(2)]
    tmpw = sb("at_tmpw", [128, 4, 64])
    tmps = sb("at_tmps", [128, 4, 64])
    ob = [sb("at_ob%d" % i, [128, 16, 64], BF16) for i in range(2)]
    obT = sb("at_obT", [128, DC, 128], BF16)
    m2 = sb("at_m2", [128, D])
    qf = m2[:].rearrange("p (h d) -> p h d", h=16)
    xres = sb("at_xres", [128, D])
    xo = xres
    S_b = [C.pb[0], C.pb[1], C.pb[6]]
    pc1, pc2, pw1, ps1 = C.pb[2], C.pb[3], C.pb[4], C.pb[5]
    v65 = lambda bank: bank[:, 0:260].rearrange("p (r d) -> p r d", r=4)
    v64 = lambda bank: bank[:, 0:256].rearrange("p (r d) -> p r d", r=4)
    KVK = "kvcache"
    cnt = [0]
    sbc = [0]

    def sbank():
        S = S_b[sbc[0] % 3]
        sbc[0] += 1
        return S
    defer = []
    late = []

    def run_deferred(upto=None):
        fs = [x for x in defer if upto is None or x[0] <= upto]
        rest = [x for x in defer if not (upto is None or x[0] <= upto)]
        del defer[:]
        defer.extend(rest)
        for _, f in fs:
            f()

    def run_late():
        fs = list(late)
        del late[:]
        for f in fs:
            f()

    def unit(qkey, lhsT, rhs, mask_ap, accv, rhs_v, first, last, extra=None):
        i = cnt[0]
        cnt[0] += 1
        S = sbank()
        P.mm(S[:], lhsT, rhs, True, mask_ap is None, reads=[KVK] + (list(qkey) if isinstance(qkey, list) else [qkey]))
        if mask_ap is not None:
            P.mm(S[:], C.identb[:], mask_ap, False, True)
        pe = Pexp[i % 3]
        P.act(pe[:], S[:], AF.Exp)
        run_deferred(i - 2)

        def pv():
            for r in range(4):
                P.mm(accv[0][:, r, :], pe[:, r * 128:(r + 1) * 128], rhs_v, first and r == 0, last, reads=[pe, KVK], writes=[accv[1]])
                if extra is not None:
                    P.mm(extra[0][:, r, :], pe[:, r * 128:(r + 1) * 128], extra[2], first and r == 0, last, reads=[pe, KVK], writes=[extra[1]])
        defer.append((i, pv))

    def recip_den(dst, bank):
        P.ts(dst[:], v65(bank)[:, :, 64], 1e-30, ALU.max, reads=[bank], writes=[dst])
        P.add("vector", lambda e: e.reciprocal(out=dst[:], in_=dst[:]), [dst], [dst])

    def pA(qt):
        b = qt % 2
        xi = C.xin[b]
        P.dma("sync", xi[:], x1_d[qt * 128:(qt + 1) * 128, :])
        cm, sc, cx = cmask[b], selc[b], cmx[b]
        P.dma(G, cm[:], D_["cmask"][qt])
        P.dma("sync", sc[:], D_["selc"][qt])
        for i_ in range(2):
            P.copy(cx[:, i_, :].rearrange("p (r t) -> p r t", r=4), cm[:, i_:i_ + 1, :].to_broadcast([128, 4, 128]),
                   eng="gpsimd", reads=[cm], writes=[cx])
        emit_norm_A(P, C, xi, gpre[:], gpre, C.hb[b], qt)

    def pB(qt):
        b = qt % 2
        emit_norm_B(P, C, C.hb[b], hqT, 0, qt)
        for nch in range(2):
            S = sbank()
            for dc in range(DC):
                P.mm(S[:], hqT[:, dc, :], wq[:, dc, nch * 512:(nch + 1) * 512], dc == 0, dc == DC - 1)
            P.act(m2[:, nch * 512:(nch + 1) * 512], S[:], AF.Copy, scale=0.125, writes=[m2])

    def pB2(qt):
        S = sbank()
        for dc in range(DC):
            P.mm(S[:, 0:48], hqT[:, dc, :], wq[:, dc, 1024:1072], dc == 0, dc == DC - 1)
        P.act(gt[qt % 3][:].rearrange("p h c -> p (h c)"), S[:, 0:48], AF.Sigmoid)

    def pD(qt):
        P.copy(qb[:], qf, eng="gpsimd", reads=[m2], writes=[qb])
        rope_ops(P, qf, qb, KV.ropek[:, 0, qt, :], KV.ropek[:, 1, qt, :], ta, tb, 16, m2, qb)

    def pE(qt):
        b = qt % 2
        for hh in range(2):
            pT = C.ptr[hh]
            for h8 in range(8):
                P.tr(pT[0:64, h8, :], qb[:, hh * 8 + h8, :], C.identb[:])
            P.copy(Qaug[b][0:64, hh * 2:(hh + 1) * 2, :].rearrange("p g (r t) -> p (g r) t", r=4), pT[0:64, :, :], eng="scalar",
                   writes=[("Qq", b, hh * 2), ("Qq", b, hh * 2 + 1)])

    def prologue(qt):
        pA(qt)
        pB(qt)
        pB2(qt)
        pD(qt)
        pE(qt)

    def a_stage(qt, g):
        b = qt % 2
        gs = slice(g * 4, (g + 1) * 4)
        Q = Qaug[b]
        cx, sc = cmx[b], selc[b]
        nts = [0] if (qt * 128 + 127) < 16 * 128 + 31 else [0, 1]
        for nt_ in nts:
            need_mask = not (nt_ == 0 and qt * 128 >= 16 * 127 + 31)
            unit([("Qq", b, g), ("Qn", b, g)], KV.KcT[:, g, nt_ * 128:(nt_ + 1) * 128], Q[:, g, :], cx[:, nt_, :] if need_mask else None,
                 (v65(pc1), pc1), KV.Vc[:, nt_, g, 0:65], nt_ == nts[0], nt_ == nts[-1], extra=(v64(pc2), pc2, KV.Ov[:, nt_, :]))

        def evac_cmp():
            recip_den(rdc, pc1)
            rb = rdc[:, :].unsqueeze(2).to_broadcast([128, 4, 64])
            P.tt(imt[:], v64(pc2), rb, ALU.mult, reads=[pc2, rdc], writes=[imt])
            P.add("vector", lambda e: e.tensor_reduce(out=imp[:], in_=imt[:].rearrange("p r j -> p j r"), axis=AX.X, op=ALU.add), [imt], [imp])
            P.tt(rdc[:], rdc[:], gt[qt % 3][:, gs, 0], ALU.mult)
            P.tt(ocw[b][:, gs, :], v65(pc1)[:, :, 0:64], rb, ALU.mult, reads=[pc1, rdc], writes=[("ocw", b, g)])
            P.tt(score[:], imp[:], sc[:, 0, :], ALU.mult)
            P.tt(score[:], score[:], sc[:, 1, :], ALU.add)
            P.add("vector", lambda e: e.max(out=m8a[:], in_=score[:]), [score], [m8a])
            P.add("vector", lambda e: e.match_replace(out=score2[:], in_to_replace=m8a[:], in_values=score[:], imm_value=-1e9), [m8a, score], [score2])
            P.add("vector", lambda e: e.max(out=m8b[:], in_=score2[:]), [score2], [m8b])
            P.ts(thr[:], m8b[:, 7:8], -5000.0, ALU.max)
            P.ts(score2[:], score[:], thr[:, 0:1], ALU.is_ge)
            nm = nmt[g]
            P.ts(nm[:, 64:128], score2[:], -1.0, ALU.add, -NEGM, ALU.mult)

            def neg_rows():
                pT = C.ptr[g % 2]
                P.tr(pT[:, 0, :], nm[:], C.identb[:])
                P.copy(Q[64:128, g, :].rearrange("p (r t) -> p r t", r=4), pT[64:128, 0:1, :].to_broadcast([64, 4, 128]), eng="scalar",
                       reads=[pT], writes=[("Qn", b, g)])
            late.append(neg_rows)
        defer.append((cnt[0] - 1, evac_cmp))
        kts = [k_ for k_ in range(qt - 4, qt + 1) if k_ >= 0]
        for kt in kts:
            mk = None
            if kt == qt:
                mk = dmx[:, 0, :]
            elif kt == qt - 4:
                mk = dmx[:, 1, :]
            unit([("Qq", b, g), ("Qn", b, g)], KV.Kw[:, g, kt * 128:(kt + 1) * 128], Q[:, g, :], mk,
                 (v65(pw1), pw1), KV.Vw[:, kt, g, :], kt == kts[0], kt == kts[-1])

        def evac_win():
            recip_den(rdw, pw1)
            P.tt(rdw[:], rdw[:], gt[qt % 3][:, gs, 2], ALU.mult)
            P.tt(tmpw[:], v65(pw1)[:, :, 0:64], rdw[:, :].unsqueeze(2).to_broadcast([128, 4, 64]), ALU.mult, reads=[pw1, rdw], writes=[tmpw])
            P.tt(ocw[b][:, gs, :], ocw[b][:, gs, :], tmpw[:], ALU.add, eng="gpsimd", reads=[("ocw", b, g), tmpw], writes=[("ocw", b, g)])
        defer.append((cnt[0] - 1, evac_win))

    def b_stage(qt, g):
        b = qt % 2
        gs = slice(g * 4, (g + 1) * 4)
        Q = Qaug[b]
        for kt in range(qt + 1):
            unit([("Qn", b, g), ("Qq", b, g)], KV.Kaug[:, g, kt * 128:(kt + 1) * 128], Q[:, g, :], dmx[:, 0, :] if kt == qt else None,
                 (v65(ps1), ps1), KV.Vs[:, kt, g, :], kt == 0, kt == qt)

        def evac_sel():
            recip_den(rds, ps1)
            P.tt(rds[:], rds[:], gt[qt % 3][:, gs, 1], ALU.mult)
            P.tt(tmps[:], v65(ps1)[:, :, 0:64], rds[:, :].unsqueeze(2).to_broadcast([128, 4, 64]), ALU.mult, reads=[ps1, rds], writes=[tmps])
            P.tt(ob[b][:, gs, :], tmps[:], ocw[b][:, gs, :], ALU.add, eng="gpsimd", reads=[tmps, ("ocw", b, g)], writes=[("ob", b, g)])
        defer.append((cnt[0] - 1, evac_sel))

    def epilogue(qt):
        b = qt % 2
        P.dma("sync", xres[:], x1_d[qt * 128:(qt + 1) * 128, :])
        pT = C.ptr[0]
        for dc in range(DC):
            P.tr(pT[:, dc, :], ob[b][:].rearrange("p h d -> p (h d)")[:, dc * 128:(dc + 1) * 128], C.identb[:],
                 reads=[("ob", b, dc // 2), C.identb])
        P.copy(obT[:], pT[:], eng="scalar")
        for nch in range(2):
            for dc in range(DC):
                P.mm(C.pb[2 + nch][:], obT[:, dc, :], wo[:, dc, nch * 512:(nch + 1) * 512], dc == 0, dc == DC - 1)
            P.copy(m2[:, nch * 512:(nch + 1) * 512], C.pb[2 + nch][:], eng="scalar", writes=[m2])
        if dbg_m2 is not None:
            P.dma("sync", dbg_m2[qt * 128:(qt + 1) * 128, :], m2[:])
        emit_postnorm_residual(P, C, m2, xres, gpost[:], gpost, xo, qt)
        P.dma("sync", xmid_d[qt * 128:(qt + 1) * 128, :], xo[:])

    prologue(0)
    for g in range(4):
        a_stage(0, g)
    run_deferred()
    run_late()
    if NT > 1:
        prologue(1)
    for qt in range(NT):
        nxt = qt + 2
        for g in range(4):
            if qt + 1 < NT:
                a_stage(qt + 1, g)
            b_stage(qt, g)
            run_late()
            if g == 0 and qt > 0:
                epilogue(qt - 1)
            if nxt < NT:
                if g == 0:
                    pA(nxt)
                elif g == 1:
                    pB(nxt)
                elif g == 2:
                    pB2(nxt)
                    pD(nxt)
        if nxt < NT:
            pE(nxt)
    run_deferred()
    run_late()
    epilogue(NT - 1)
def build(stage="full", debug=False):
    nc = bass.Bass("TRN2", target_bir_lowering=False)
    P = Prog(nc)

    def din(name, shape, dt=F32):
        return nc.dram_tensor(name, list(shape), dt, kind="ExternalInput").ap()

    def dout(name, shape, dt=F32):
        return nc.dram_tensor(name, list(shape), dt, kind="ExternalOutput").ap()

    def dscr(name, shape, dt=F32):
        return nc.dram_tensor(name, list(shape), dt, kind="Internal").ap()

    x_d = din("x", [T, D])
    s5par_d = din("s5par", [128, 3, 32])
    s5b_d = din("s5b", [128, 2, 32, 32])
    s5c_d = din("s5c", [128, 2, 32, 32])
    s5d_d = din("s5d", [128, 8])
    wglu_d = din("wglu", [D, 2 * D])
    gains_d = din("gains", [9, D])
    win_d = [din("win%d" % l, [D, 2 * FF]) for l in range(2)]
    wout_d = [din("wout%d" % l, [FF, D]) for l in range(2)]
    conv_d = [din("conv%d" % l, [128, FC, 4]) for l in range(2)]
    ident_d = din("ident", [128, 128])
    D_ = {"gains": gains_d}
    for nm, shp in (("wkv", [D, 1536]), ("wq", [D, 1072]), ("wo", [D, D]), ("w1d", [128, 2, 32, 128]), ("w2", [128, 2, 64]),
                    ("peT", [64, 2, 32]), ("ind", [64, T]), ("ropek", [128, 2, NT, 8]), ("ropec", [128, 2, 2, 8]),
                    ("ovl", [128, 2, 64]), ("cmask", [NT, 128, 2, 128]), ("dmask", [128, 2, 128]), ("selc", [NT, 128, 2, 64])):
        D_[nm] = din(nm, shp)

    out_d = dout("out", [T, D])
    xmid_d = dscr("xmid", [T, D])
    x1_d = out_d if stage == "l0" else dscr("x1", [T, D])
    dbg_m = dout("dbg_m", [T, D]) if debug else None
    dbg_xmid = dout("dbg_xmid", [T, D]) if debug else None
    dbg_m2 = dout("dbg_m2", [T, D]) if (debug and stage != "l0") else None
    dbg_x1 = dout("dbg_x1", [T, D]) if (debug and stage != "l0") else None

    winb_d = [dscr("winb%d" % l, [D, 2 * FF], BF16) for l in range(2)]
    woutb_d = [dscr("woutb%d" % l, [FF, D], BF16) for l in range(2)]
    for l in range(2):
        P.bg.add("winb%d" % l)
        P.bg.add("woutb%d" % l)
    def precast(l):
        for r0 in range(0, D, 128):
            P.bgq.append(lambda r0=r0: P.dma("gpsimd", winb_d[l][r0:r0 + 128, :], win_d[l][r0:r0 + 128, :]))
        for r0 in range(0, FF, 128):
            P.bgq.append(lambda r0=r0: P.dma("gpsimd", woutb_d[l][r0:r0 + 128, :], wout_d[l][r0:r0 + 128, :]))
    precast(0)
    with contextlib.ExitStack() as glob:
        def gsb(name, shape, dt=F32):
            return glob.enter_context(nc.sbuf_tensor("S_" + name, list(shape), dt))

        def gps(name, shape, dt=F32):
            return glob.enter_context(nc.psum_tensor("P_" + name, list(shape), dt))

        C = Ctx()
        identf = gsb("identf", [128, 128])
        C.identb = gsb("identb", [128, 128], BF16)
        P.dma("sync", identf[:], ident_d)
        P.copy(C.identb[:], identf[:])
        C.epsb = gsb("epsb", [128, 1])
        P.memset(C.epsb[:], EPS)
        C.halfpi = gsb("halfpi", [128, 1])
        P.memset(C.halfpi[:], math.pi / 2)
        C.pb = [gps("pb%d" % i, [128, 512]) for i in range(7)]
        _ptr = gps("ptr0", [128, 8, 128], BF16)
        C.ptr = [_ptr, _ptr]
        C.xin = [gsb("xin%d" % i, [128, D]) for i in range(2)]
        C.sq = gsb("sqjunk", [128, D])
        C.stat = [gsb("stat%d" % i, [128, 4]) for i in range(2)]
        C.hb = [gsb("hb%d" % i, [128, D], BF16) for i in range(2)]

        with contextlib.ExitStack() as ph1:
            sb1 = lambda name, shape, dt=F32: ph1.enter_context(nc.sbuf_tensor("S_" + name, list(shape), dt))
            yT = sb1("yT", [128, DC, T], BF16)
            with contextlib.ExitStack() as ph1a:
                sba = lambda name, shape, dt=F32: ph1a.enter_context(nc.sbuf_tensor("S_" + name, list(shape), dt))
                emit_s5(P, C, nc, (sba, None), x_d, s5par_d, s5b_d, s5c_d, s5d_d, gains_d, yT)
            P.barrier()
            with contextlib.ExitStack() as ph1b:
                sbb = lambda name, shape, dt=F32: ph1b.enter_context(nc.sbuf_tensor("S_" + name, list(shape), dt))
                emit_glu(P, C, sbb, x_d, wglu_d, gains_d, yT, xmid_d, dbg_m)
            P.barrier()
        if debug:
            with contextlib.ExitStack() as phd:
                t_ = phd.enter_context(nc.sbuf_tensor("dbgt", [128, D], F32))
                for tt in range(NT):
                    P.dma("sync", t_[:], xmid_d[tt * 128:(tt + 1) * 128, :], reads=["xmid_all"])
                    P.dma("sync", dbg_xmid[tt * 128:(tt + 1) * 128, :], t_[:])
            P.barrier()
        with contextlib.ExitStack() as ph2:
            sb2 = lambda name, shape, dt=F32: ph2.enter_context(nc.sbuf_tensor("S_" + name, list(shape), dt))
            P.pump()
            precast(1)
            emit_ffn(P, C, sb2, 0, xmid_d, x1_d, winb_d[0], woutb_d[0], conv_d[0], gains_d, 2, 3, wq_eng="scalar")
        P.barrier()
        if stage != "l0":
            with contextlib.ExitStack() as phk:
                sbk = lambda name, shape, dt=F32: phk.enter_context(nc.sbuf_tensor("S_" + name, list(shape), dt))
                KV = Ctx()
                KV.nc = nc
                KV.Kaug = sbk("Kaug", [128, 4, T], BF16)
                KV.Kw = sbk("Kw", [128, 4, T], BF16)
                KV.Vs = sbk("Vs", [128, NT, 4, 65], BF16)
                KV.Vw = sbk("Vw", [128, NT, 4, 65], BF16)
                KV.KcT = sbk("KcT", [128, 4, 256], BF16)
                KV.Vc = sbk("Vc", [128, 2, 4, 65], BF16)
                KV.Ov = sbk("Ov", [128, 2, 64], BF16)
                KV.ropek = sbk("ropek", [128, 2, NT, 8])
                P.dma("sync", KV.ropek[:], D_["ropek"], writes=["ropetab"])
                with contextlib.ExitStack() as phk1:
                    sbk1 = lambda name, shape, dt=F32: phk1.enter_context(nc.sbuf_tensor("S_" + name, list(shape), dt))
                    emit_kv(P, C, sbk1, KV, x1_d, D_)
                P.barrier()
                if debug:
                    for nm, t_ in (("Kaug", KV.Kaug), ("Kw", KV.Kw), ("Vs", KV.Vs), ("Vw", KV.Vw), ("KcT", KV.KcT), ("Vc", KV.Vc)):
                        dd = nc.dram_tensor("dbg_" + nm, list(t_.shape), BF16, kind="ExternalOutput").ap()
                        P.dma("sync", dd, t_[:])
                    P.barrier()
                with contextlib.ExitStack() as phk2:
                    sbk2 = lambda name, shape, dt=F32: phk2.enter_context(nc.sbuf_tensor("S_" + name, list(shape), dt))
                    emit_attn(P, C, sbk2, KV, x1_d, xmid_d, D_, dbg_m2, nc=nc)
                P.barrier()
            if debug:
                with contextlib.ExitStack() as phd:
                    t_ = phd.enter_context(nc.sbuf_tensor("S_dbgt2", [128, D], F32))
                    for tt in range(NT):
                        P.dma("sync", t_[:], x1_d[tt * 128:(tt + 1) * 128, :])
                        P.dma("sync", dbg_x1[tt * 128:(tt + 1) * 128, :], t_[:])
                P.barrier()
            P.pump()
            with contextlib.ExitStack() as ph3:
                sb3 = lambda name, shape, dt=F32: ph3.enter_context(nc.sbuf_tensor("S_" + name, list(shape), dt))
                emit_ffn(P, C, sb3, 1, xmid_d, out_d, winb_d[1], woutb_d[1], conv_d[1], gains_d, 7, 8, wq_eng="scalar")
            P.barrier()
        P.emit()
    return nc, P


_NC_CACHE = {}


def kernel(**inputs):
    if "nc" not in _NC_CACHE:
        _NC_CACHE["nc"] = build("full")[0]
    nc = _NC_CACHE["nc"]
    in_maps = [host_layout_l1(inputs, host_layout(inputs, c // 2)) for c in range(8)]
    res = run_bass_kernel_spmd(nc, in_maps, core_ids=list(range(8)))
    out = np.stack([res.results[2 * b]["out"] for b in range(4)], axis=0)
    return out.astype(np.float32)
```
